# Optimizing a Trainium2 kernel written in Bass

```python
import math
import jax, jax.numpy as jnp
from jax import lax
import numpy as np

D_MODEL = 1024
BATCH = 8
SEQ = 2048
DEPTH = 4

EXPAND = 2
E_WIDTH = EXPAND * D_MODEL
HEAD_DIM = 128
SB_HEADS = E_WIDTH // HEAD_DIM
DF_HEADS = E_WIDTH // (2 * HEAD_DIM)
Q_BLOCK = 128
ROPE_THETA = 10000.0
EPS = 1e-6
N_MIXERS = 2
N_SB = (DEPTH + 1) // 2
N_DF = DEPTH // 2

kernel_name = "hybrid_stickbreak_diffattn_gated"


def rms_norm(x, g):
    xf = x.astype(jnp.float32)
    y = xf * lax.rsqrt(jnp.mean(xf * xf, axis=-1, keepdims=True) + EPS)
    return (y * g.astype(jnp.float32)).astype(x.dtype)


def rope_tables(seq, dim):
    inv = 1.0 / (ROPE_THETA ** (jnp.arange(0, dim, 2, dtype=jnp.float32) / dim))
    ang = jnp.arange(seq, dtype=jnp.float32)[:, None] * inv[None, :]
    return jnp.cos(ang), jnp.sin(ang)


def apply_rope(x, cos, sin):
    xf = x.astype(jnp.float32)
    half = xf.shape[-1] // 2
    x1, x2 = xf[..., :half], xf[..., half:]
    out = jnp.concatenate([x1 * cos - x2 * sin, x2 * cos + x1 * sin], axis=-1)
    return out.astype(x.dtype)


def stick_breaking_attention(q, k, v):
    S = q.shape[2]
    scale = 1.0 / math.sqrt(q.shape[-1])
    outs = []
    for t0 in range(0, S, Q_BLOCK):
        t1 = t0 + Q_BLOCK
        z = jnp.einsum('bhqd,bhkd->bhqk', q[:, :, t0:t1], k[:, :, :t1]).astype(jnp.float32) * scale
        t_idx = t0 + jnp.arange(Q_BLOCK)[:, None]
        s_idx = jnp.arange(t1)[None, :]
        mask = s_idx < t_idx
        log_beta = jax.nn.log_sigmoid(z)
        log_1m = jnp.where(mask, jax.nn.log_sigmoid(-z), 0.0)
        suffix = lax.cumsum(log_1m, axis=3, reverse=True) - log_1m
        attn = jnp.where(mask, jnp.exp(log_beta + suffix), 0.0)
        outs.append(jnp.einsum('bhqk,bhkd->bhqd', attn.astype(v.dtype), v[:, :, :t1]))
    return jnp.concatenate(outs, axis=2)


def differential_attention(q, k, v, lam):
    S = q.shape[3]
    scale = 1.0 / math.sqrt(q.shape[-1])
    outs = []
    for t0 in range(0, S, Q_BLOCK):
        t1 = t0 + Q_BLOCK
        z = jnp.einsum('bhmqd,bhmkd->bhmqk', q[:, :, :, t0:t1], k[:, :, :, :t1]).astype(jnp.float32) * scale
        t_idx = t0 + jnp.arange(Q_BLOCK)[:, None]
        s_idx = jnp.arange(t1)[None, :]
        z = jnp.where(s_idx <= t_idx, z, -jnp.inf)
        p = jax.nn.softmax(z, axis=-1)
        a = p[:, :, 0] - lam * p[:, :, 1]
        outs.append(jnp.einsum('bhqk,bhkd->bhqd', a.astype(v.dtype), v[:, :, :t1]))
    return jnp.concatenate(outs, axis=2)


def setup_inputs(seed: int = 0) -> dict:
    key = jax.random.key(seed)
    ks = jax.random.split(key, 16)
    f32 = jnp.float32
    x = jax.random.normal(ks[0], (BATCH, SEQ, D_MODEL), f32)
    sb_norm = 1.0 + 0.02 * jax.random.normal(ks[1], (N_SB, D_MODEL), f32)
    sb_w_in = jax.random.normal(ks[2], (N_SB, D_MODEL, 4 * E_WIDTH), f32) * D_MODEL ** -0.5
    sb_w_out = jax.random.normal(ks[3], (N_SB, E_WIDTH, D_MODEL), f32) * E_WIDTH ** -0.5
    df_norm = 1.0 + 0.02 * jax.random.normal(ks[4], (N_DF, D_MODEL), f32)
    df_w_in = jax.random.normal(ks[5], (N_DF, D_MODEL, 4 * E_WIDTH), f32) * D_MODEL ** -0.5
    df_w_out = jax.random.normal(ks[6], (N_DF, E_WIDTH, D_MODEL), f32) * E_WIDTH ** -0.5
    df_q_norm = 1.0 + 0.02 * jax.random.normal(ks[7], (N_DF, HEAD_DIM), f32)
    df_k_norm = 1.0 + 0.02 * jax.random.normal(ks[8], (N_DF, HEAD_DIM), f32)
    df_lam_q1 = 0.1 * jax.random.normal(ks[9], (N_DF, HEAD_DIM), f32)
    df_lam_k1 = 0.1 * jax.random.normal(ks[10], (N_DF, HEAD_DIM), f32)
    df_lam_q2 = 0.1 * jax.random.normal(ks[11], (N_DF, HEAD_DIM), f32)
    df_lam_k2 = 0.1 * jax.random.normal(ks[12], (N_DF, HEAD_DIM), f32)
    df_sub_norm = 1.0 + 0.02 * jax.random.normal(ks[13], (N_DF, 2 * HEAD_DIM), f32)
    return {"x": x, "sb_norm": sb_norm, "sb_w_in": sb_w_in, "sb_w_out": sb_w_out,
            "df_norm": df_norm, "df_w_in": df_w_in, "df_w_out": df_w_out,
            "df_q_norm": df_q_norm, "df_k_norm": df_k_norm,
            "df_lam_q1": df_lam_q1, "df_lam_k1": df_lam_k1,
            "df_lam_q2": df_lam_q2, "df_lam_k2": df_lam_k2,
            "df_sub_norm": df_sub_norm}


def reference(x, sb_norm, sb_w_in, sb_w_out, df_norm, df_w_in, df_w_out,
              df_q_norm, df_k_norm, df_lam_q1, df_lam_k1, df_lam_q2, df_lam_k2,
              df_sub_norm):
    B, S, _ = x.shape
    cos, sin = rope_tables(S, HEAD_DIM)
    h = x
    for i in range(DEPTH):
        j = i // N_MIXERS
        if i % N_MIXERS == 0:
            u = rms_norm(h, sb_norm[j])
            proj = jnp.einsum('bsd,de->bse', u, sb_w_in[j])
            q, k, v, z = jnp.split(proj, 4, axis=-1)
            to_heads = lambda t: t.reshape(B, S, SB_HEADS, HEAD_DIM).transpose(0, 2, 1, 3)
            o = stick_breaking_attention(to_heads(q), to_heads(k), to_heads(v))
            o = o.transpose(0, 2, 1, 3).reshape(B, S, E_WIDTH)
            y = o * jax.nn.silu(z)
            h = h + jnp.einsum('bse,ed->bsd', y, sb_w_out[j])
        else:
            lam_init = 0.8 - 0.6 * math.exp(-0.3 * i)
            u = rms_norm(h, df_norm[j])
            proj = jnp.einsum('bsd,de->bse', u, df_w_in[j])
            q, k, v, z = jnp.split(proj, 4, axis=-1)
            q = q.reshape(B, S, DF_HEADS, 2, HEAD_DIM).transpose(0, 2, 3, 1, 4)
            k = k.reshape(B, S, DF_HEADS, 2, HEAD_DIM).transpose(0, 2, 3, 1, 4)
            v = v.reshape(B, S, DF_HEADS, 2 * HEAD_DIM).transpose(0, 2, 1, 3)
            q = apply_rope(rms_norm(q, df_q_norm[j]), cos, sin)
            k = apply_rope(rms_norm(k, df_k_norm[j]), cos, sin)
            lam = (jnp.exp(jnp.sum(df_lam_q1[j].astype(jnp.float32) * df_lam_k1[j].astype(jnp.float32)))
                   - jnp.exp(jnp.sum(df_lam_q2[j].astype(jnp.float32) * df_lam_k2[j].astype(jnp.float32)))
                   + lam_init)
            o = differential_attention(q, k, v, lam)
            o = rms_norm(o, df_sub_norm[j]) * (1.0 - lam_init)
            o = o.transpose(0, 2, 1, 3).reshape(B, S, E_WIDTH)
            y = o * jax.nn.silu(z)
            h = h + jnp.einsum('bse,ed->bsd', y, df_w_out[j])
    return h
```

```python
import math
from contextlib import ExitStack

import numpy as np
import concourse.bass as bass
import concourse.mybir as mybir
from concourse.bass_utils import run_bass_kernel_spmd

F32 = mybir.dt.float32
BF16 = mybir.dt.bfloat16
AF = mybir.ActivationFunctionType
ALU = mybir.AluOpType

HEAD_DIM = 128
EPS = 1e-6
ROPE_THETA = 10000.0
MASKV = 30000.0


class Cfg:
    def __init__(self, S=2048, D=1024, depth=4):
        self.S, self.D, self.depth = S, D, depth
        self.E = 2 * D
        self.NC = D // 128
        self.NB = S // 512
        self.NT = S // 128
        self.H_SB = self.E // 128
        self.H_DF = self.E // 256
        self.PL = self.NC + 8
        self.UW = self.NC * 4 * 128


class Buf:
    __slots__ = ("name", "w", "r")

    def __init__(self, name):
        self.name = name
        self.w = None
        self.r = {}


class KB:
    def __init__(self, nc, es):
        self.nc = nc
        self.es = es
        self.eng = {"pe": nc.tensor, "act": nc.scalar, "dve": nc.vector,
                    "pool": nc.gpsimd, "sp": nc.sync}
        self.sems = {}
        self.cnt = {}
        self.seen = {e: {} for e in self.eng}
        for e in self.eng:
            self._sem(e)
        self.n_wait = 0
        self.n_ins = 0

    def _sem(self, key):
        if key not in self.sems:
            self.sems[key] = self.es.enter_context(self.nc.semaphore("s_" + key))
            self.cnt[key] = 0
        return self.sems[key]

    def _waits(self, eng, reads, writes, pe_acc):
        deps = {}

        def add(ev):
            if ev is None:
                return
            k, v = ev
            if deps.get(k, 0) < v:
                deps[k] = v

        for b in reads:
            add(b.w)
        for b in writes:
            if not (pe_acc and eng == "pe" and b.w is not None and b.w[0] == "pe"):
                add(b.w)
            for k, v in b.r.items():
                add((k, v))
        seen = self.seen[eng]
        for k, v in deps.items():
            if k.startswith("d_"):
                v = self.cnt[k]
            if seen.get(k, 0) < v:
                self.eng[eng].wait_ge(self.sems[k], v)
                seen[k] = v
                self.n_wait += 1

    def _record(self, ev, reads, writes):
        k, v = ev
        for b in reads:
            if b.r.get(k, 0) < v:
                b.r[k] = v
        for b in writes:
            b.w = ev
            b.r = {}

    def op(self, eng, fn, reads=(), writes=(), pe_acc=False, inc=True):
        self._waits(eng, reads, writes, pe_acc)
        ins = fn(self.eng[eng])
        self.n_ins += 1
        if inc:
            self.cnt[eng] += 1
            ins.then_inc(self.sems[eng], 1)
            ev = (eng, self.cnt[eng])
        else:
            ev = (eng, self.cnt[eng] + 1)
        self._record(ev, reads, writes)
        return ev

    def dma(self, q, out, in_, semkey, reads=(), writes=(), **kw):
        self._sem(semkey)
        self._waits(q, reads, writes, False)
        ins = self.eng[q].dma_start(out=out, in_=in_, **kw)
        self.n_ins += 1
        self.cnt[semkey] += 16
        ins.then_inc(self.sems[semkey], 16)
        ev = (semkey, self.cnt[semkey])
        self._record(ev, reads, writes)
        return ev

    def wait_all(self, eng, keys):
        for k in keys:
            v = self.cnt.get(k, 0)
            if v > 0 and self.seen[eng].get(k, 0) < v:
                self.eng[eng].wait_ge(self.sems[k], v)
                self.seen[eng][k] = v


C_IDENT, C_UINCL, C_MNEG_S, C_MPOS_S, C_MNEG_I, C_ONES, C_PSWAP, C_EBIG = (
    0, 128, 256, 384, 512, 640, 768, 896)
C_BSEL = 896 + 144
C_TOTAL = C_BSEL + 128


def make_consts():
    c = np.zeros((128, C_TOTAL), np.float32)
    p = np.arange(128)[:, None]
    f = np.arange(128)[None, :]
    c[:, C_IDENT:C_IDENT + 128] = (p == f)
    c[:, C_UINCL:C_UINCL + 128] = (p >= f)
    c[:, C_MNEG_S:C_MNEG_S + 128] = np.where(p >= f, -MASKV, 0.0)
    c[:, C_MPOS_S:C_MPOS_S + 128] = np.where(p >= f, MASKV, 0.0)
    c[:, C_MNEG_I:C_MNEG_I + 128] = np.where(p > f, -MASKV, 0.0)
    c[:, C_ONES:C_ONES + 128] = 1.0
    c[:, C_PSWAP:C_PSWAP + 128] = (p == (f + 64) % 128)
    c[:, C_EBIG + 15] = 1.0
    c[0, C_BSEL:C_BSEL + 128] = 1.0
    return c


def make_rope(S):
    inv = (1.0 / (ROPE_THETA ** (np.arange(0, HEAD_DIM, 2, dtype=np.float32) / np.float32(HEAD_DIM)))).astype(np.float32)
    ang = (np.arange(S, dtype=np.float32)[:, None] * inv[None, :]).astype(np.float32)
    cos = np.cos(ang).astype(np.float32).T
    sin = np.sin(ang).astype(np.float32).T
    out = np.zeros((128, 2, S), np.float32)
    out[:64, 0] = cos
    out[64:, 0] = cos
    out[:64, 1] = -sin
    out[64:, 1] = sin
    return out.reshape(128, 2 * S)


def build_program(cfg):
    S, D, E, NC, NB, NT = cfg.S, cfg.D, cfg.E, cfg.NC, cfg.NB, cfg.NT
    depth, PL, UW = cfg.depth, cfg.PL, cfg.UW
    scale = 1.0 / math.sqrt(HEAD_DIM)

    nc = bass.Bass("TRN2", target_bir_lowering=False)
    es = ExitStack()
    with es:
        x_d = nc.dram_tensor("x", [S, D], F32, kind="ExternalInput").ap()
        out_d = nc.dram_tensor("out", [S, D], F32, kind="ExternalOutput").ap()
        consts_d = nc.dram_tensor("consts", [128, C_TOTAL], F32, kind="ExternalInput").ap()
        rope_d = nc.dram_tensor("rope", [128, 2 * S], F32, kind="ExternalInput").ap()
        par_d = nc.dram_tensor("params", [128, depth * PL], F32, kind="ExternalInput").ap()
        win_d, wout_d = [], []
        for l in range(depth):
            nu = cfg.H_SB if l % 2 == 0 else 2 * cfg.H_DF
            win_d.append(nc.dram_tensor(f"win{l}", [nu, 128, UW], F32, kind="ExternalInput").ap())
            wout_d.append(nc.dram_tensor(f"wout{l}", [E, D], F32, kind="ExternalInput").ap())

        def sb(name, shape, dt):
            return es.enter_context(nc.sbuf_tensor(name, shape, dt))

        hT = sb("hT", [128, NC, S], F32)
        uT = sb("uT", [128, NC, S], BF16)
        vbuf = [sb(f"vbuf{i}", [128, NT, 256], BF16) for i in range(2)]
        qk = [sb(f"qk{i}", [128, S], BF16) for i in range(4)]
        gate = sb("gate", [128, 2, S], BF16)
        scr = sb("scr", [128, 4096], F32)
        etmp = sb("etmp", [128, 2, 512], F32)
        NSP = 4
        spr = sb("spr", [128, NSP, 512], BF16)
        wtmp = sb("wtmp", [128, 2, 512], F32)
        NATT = 5
        attn = sb("attn", [128, NATT, 512], BF16)
        cs_sb = sb("cs_sb", [128, 2, 512], BF16)
        NSLOT = 2
        wslot = [sb(f"wslot{i}", [128, UW], BF16) for i in range(NSLOT)]
        NOSLOT = 3
        woslot = [sb(f"woslot{i}", [128, D], BF16) for i in range(NOSLOT)]
        cbf = sb("cbf", [128, C_TOTAL], BF16)
        cf32 = sb("cf32", [128, 256], F32)
        par = sb("par", [128, depth * PL], F32)
        dpar = sb("dpar", [128, depth * 4 + 4], F32)
        o_sb = sb("o_sb", [128, 2, 512], F32)
        sacc = sb("sacc", [128, 2, 512], F32)

        NE = 7
        ering = scr[:, 0:NE * 512].rearrange("p (i n) -> p i n", n=512)
        scr_f = scr[:, :]
        print("[build] SBUF bytes/partition remaining:", nc.sbuf_bytes_remaining)

        banks = [es.enter_context(nc.psum_tensor(f"bank{i}", [128, 512], F32)) for i in range(8)]
        bankB = [Buf(f"bank{i}") for i in range(8)]

        kb = KB(nc, es)

        B_hT = [[Buf(f"hT{c}_{b}") for b in range(NB)] for c in range(NC)]
        B_uT = [[Buf(f"uT{c}_{b}") for b in range(NB)] for c in range(NC)]
        B_v = [[Buf(f"v{p}_{t}") for t in range(NT)] for p in range(2)]
        B_qk = [[Buf(f"qk{i}_{b}") for b in range(NB)] for i in range(4)]
        B_gate = [[Buf(f"gate{c}_{b}") for b in range(NB)] for c in range(2)]
        B_E = [Buf(f"E{i}") for i in range(NE)]
        B_scr1 = Buf("scr_rest")
        SCR_ALL = B_E + [B_scr1]
        B_sp = [Buf(f"sp{i}") for i in range(NSP)]
        B_etmp = [Buf("etmp0"), Buf("etmp1")]
        B_wt = [Buf("wt0"), Buf("wt1")]
        B_attn = [Buf(f"attn{i}") for i in range(NATT)]
        B_cs = [Buf("cs_sb0"), Buf("cs_sb1")]
        B_w = [Buf(f"w{i}") for i in range(NSLOT)]
        B_wo = [Buf(f"wo{i}") for i in range(NOSLOT)]
        B_c = Buf("consts")
        B_par = Buf("par")
        B_dpar = Buf("dpar")
        B_o = [Buf("o_sb0"), Buf("o_sb1")]
        B_sacc = [[Buf(f"sacc{p}_{h}") for h in range(2)] for p in range(2)]

        def cc(off, n=128):
            return cbf[:, off:off + n]

        ident = cc(C_IDENT)
        uincl = cc(C_UINCL)
        mneg_s = cc(C_MNEG_S)
        mneg_i = cc(C_MNEG_I)
        ones = cc(C_ONES)
        pswap = cc(C_PSWAP)
        ident_f = cf32[:, 0:128]
        ones_f = cf32[:, 128:256]

        cst = scr_f[:, 0:C_TOTAL]
        kb.dma("sp", cst, consts_d[:, :], "d_misc", writes=SCR_ALL)
        kb.dma("sp", par[:, :], par_d[:, :], "d_misc", writes=[B_par])
        kb.op("dve", lambda e: e.tensor_copy(out=cbf[:, :], in_=cst), reads=SCR_ALL, writes=[B_c])
        kb.op("dve", lambda e: e.tensor_copy(out=cf32[:, 0:128], in_=cst[:, C_IDENT:C_IDENT + 128]),
              reads=SCR_ALL, writes=[B_c])
        kb.op("dve", lambda e: e.tensor_copy(out=cf32[:, 128:256], in_=cst[:, C_ONES:C_ONES + 128]),
              reads=SCR_ALL, writes=[B_c])
        kb.op("dve", lambda e: e.memset(dpar[:, depth * 4:depth * 4 + 1], EPS), writes=[B_dpar])
        kb.op("dve", lambda e: e.memset(dpar[:, depth * 4 + 1:depth * 4 + 2], 1.0), writes=[B_dpar])
        EPS_AP = dpar[:, depth * 4:depth * 4 + 1]
        ONE_AP = dpar[:, depth * 4 + 1:depth * 4 + 2]

        for l in range(depth):
            if l % 2 == 0:
                continue
            pb = l * PL
            db = l * 4
            lam_init = 0.8 - 0.6 * math.exp(-0.3 * l)
            kb.op("dve", lambda e: e.tensor_scalar(out=dpar[:, db:db + 1], in0=par[:, pb + NC:pb + NC + 1],
                                                   scalar1=scale, scalar2=None, op0=ALU.mult),
                  reads=[B_par], writes=[B_dpar])
            kb.op("dve", lambda e: e.tensor_scalar(out=dpar[:, db + 1:db + 3], in0=par[:, pb + NC + 6:pb + NC + 8],
                                                   scalar1=(1.0 - lam_init), scalar2=None, op0=ALU.mult),
                  reads=[B_par], writes=[B_dpar])
            kb.op("dve", lambda e: e.tensor_tensor(out=etmp[:, 0, 0:1], in0=par[:, pb + NC + 2:pb + NC + 3],
                                                   in1=par[:, pb + NC + 3:pb + NC + 4], op=ALU.mult),
                  reads=[B_par], writes=[B_etmp[0]])
            kb.op("dve", lambda e: e.tensor_tensor(out=etmp[:, 0, 1:2], in0=par[:, pb + NC + 4:pb + NC + 5],
                                                   in1=par[:, pb + NC + 5:pb + NC + 6], op=ALU.mult),
                  reads=[B_par], writes=[B_etmp[0]])
            kb.op("pe", lambda e: e.matmul(banks[0][:, 0:2], lhsT=ones_f, rhs=etmp[:, 0, 0:2], start=True, stop=True),
                  reads=[B_c, B_etmp[0]], writes=[bankB[0]])
            kb.op("act", lambda e: e.activation(out=etmp[:, 1, 0:2], in_=banks[0][:, 0:2], func=AF.Exp),
                  reads=[bankB[0]], writes=[B_etmp[1]])
            kb.op("dve", lambda e: e.tensor_tensor(out=etmp[:, 1, 2:3], in0=etmp[:, 1, 1:2], in1=etmp[:, 1, 0:1],
                                                   op=ALU.subtract),
                  reads=[B_etmp[1]], writes=[B_etmp[1]])
            kb.op("dve", lambda e: e.tensor_scalar(out=dpar[:, db + 3:db + 4], in0=etmp[:, 1, 2:3],
                                                   scalar1=-lam_init, scalar2=None, op0=ALU.add),
                  reads=[B_etmp[1]], writes=[B_dpar])

        unit_list = []
        for l in range(depth):
            nu = cfg.H_SB if l % 2 == 0 else 2 * cfg.H_DF
            for u in range(nu):
                unit_list.append((l, u))
        unit_pos = {lu: i for i, lu in enumerate(unit_list)}
        wo_list = list(unit_list)
        wo_pos = {lu: i for i, lu in enumerate(wo_list)}
        state = {"w_next": 0, "wo_next": 0}

        def issue_w(upto):
            while state["w_next"] <= min(upto, len(unit_list) - 1):
                i = state["w_next"]
                l, u = unit_list[i]
                s = i % NSLOT
                nparts = 1
                pw = UW // nparts
                for k in range(nparts):
                    kb.dma("pool", wslot[s][:, k * pw:(k + 1) * pw], win_d[l][u, :, k * pw:(k + 1) * pw],
                           f"d_w{s}", writes=[B_w[s]])
                state["w_next"] += 1

        def issue_wo(upto):
            while state["wo_next"] <= min(upto, len(wo_list) - 1):
                i = state["wo_next"]
                l, u = wo_list[i]
                s = i % NOSLOT
                kb.dma("pool", woslot[s][:, :], wout_d[l][u * 128:(u + 1) * 128, :], f"d_wo{s}", writes=[B_wo[s]])
                state["wo_next"] += 1

        uT_f = uT[:, :, :].rearrange("p c s -> p (c s)").bitcast(F32)
        ALL_UT = [B_uT[c][b_] for c in range(NC) for b_ in range(NB)]
        half_ut = (NC * S // 2) // 2
        stage_bufs = [(scr_f, 4096, SCR_ALL)]
        if half_ut >= D:
            stage_bufs.append((uT_f[:, 0:half_ut], half_ut, ALL_UT))
            stage_bufs.append((uT_f[:, half_ut:2 * half_ut], half_ut, ALL_UT))
        tiles_per_stage = max(1, min(4096, half_ut if half_ut >= D else 4096) // D)
        tiles_per_stage = max(1, 4096 // D) if tiles_per_stage * D > 4096 else tiles_per_stage
        mmb = 0
        for si, t0 in enumerate(range(0, NT, tiles_per_stage)):
            nt = min(tiles_per_stage, NT - t0)
            sbuf_ap, scap, sbufs = stage_bufs[si % len(stage_bufs)]
            xv = sbuf_ap[:, 0:nt * D].rearrange("p (a d) -> p a d", a=nt)
            for a_ in range(nt):
                kb.dma("sp", xv[:, a_, :], x_d[(t0 + a_) * 128:(t0 + a_ + 1) * 128, :],
                       f"d_x{si % len(stage_bufs)}", writes=sbufs)
            for g0 in range(0, nt, 4):
                gn = min(4, nt - g0)
                tt0 = t0 + g0
                for c in range(NC):
                    bk = mmb % 4
                    mmb += 1
                    for k_ in range(gn):
                        kb.op("pe", lambda e: e.transpose(out=banks[bk][:, k_ * 128:(k_ + 1) * 128],
                                                          in_=xv[:, g0 + k_, c * 128:(c + 1) * 128], identity=ident_f),
                              reads=sbufs + [B_c], writes=[bankB[bk]], pe_acc=(k_ > 0))
                    kb.op("dve", lambda e: e.tensor_copy(out=hT[:, c, tt0 * 128:(tt0 + gn) * 128], in_=banks[bk][:, 0:gn * 128]),
                          reads=[bankB[bk]], writes=[B_hT[c][tt0 // 4]])

        def rmsnorm_to_uT(l):
            pb = l * PL
            for b in range(NB):
                cols = slice(b * 512, (b + 1) * 512)
                ssb = 2 + b % 2
                for c in range(NC):
                    ai = c % 3
                    kb.op("act", lambda e: e.activation(out=attn[:, ai, :], in_=hT[:, c, cols], func=AF.Square),
                          reads=[B_hT[c][b]], writes=[B_attn[ai]])
                    kb.op("pe", lambda e: e.matmul(banks[ssb][:, :], lhsT=ones, rhs=attn[:, ai, :],
                                                   start=(c == 0), stop=(c == NC - 1)),
                          reads=[B_attn[ai], B_c], writes=[bankB[ssb]], pe_acc=(c > 0))
                eb = b % 2
                kb.op("act", lambda e: e.activation(out=etmp[:, eb, :], in_=banks[ssb][:, :], func=AF.Ln,
                                                    scale=1.0 / D, bias=EPS_AP),
                      reads=[bankB[ssb], B_dpar], writes=[B_etmp[eb]])
                kb.op("act", lambda e: e.activation(out=etmp[:, eb, :], in_=etmp[:, eb, :], func=AF.Exp, scale=-0.5),
                      reads=[B_etmp[eb]], writes=[B_etmp[eb]])
                for c in range(NC):
                    kb.op("dve", lambda e: e.scalar_tensor_tensor(out=uT[:, c, cols], in0=hT[:, c, cols],
                                                                  scalar=par[:, pb + c:pb + c + 1], in1=etmp[:, eb, :],
                                                                  op0=ALU.mult, op1=ALU.mult),
                          reads=[B_hT[c][b], B_par, B_etmp[eb]], writes=[B_uT[c][b]])

        def proj_fm(slot, chunk, b, bank):
            for c in range(NC):
                woff = (c * 4 + chunk) * 128
                kb.op("pe", lambda e: e.matmul(banks[bank][:, :], lhsT=wslot[slot][:, woff:woff + 128],
                                               rhs=uT[:, c, b * 512:(b + 1) * 512],
                                               start=(c == 0), stop=(c == NC - 1)),
                      reads=[B_w[slot], B_uT[c][b]], writes=[bankB[bank]], pe_acc=(c > 0), inc=(c == NC - 1))

        def proj_v(slot, chunk0, width, vp, t0, ntile, bank, defer_fn=None, evac_eng="dve"):
            for k in range(ntile):
                tt = t0 + k
                for c in range(NC):
                    woff = (c * 4 + chunk0) * 128
                    kb.op("pe", lambda e: e.matmul(banks[bank][:, k * width:(k + 1) * width],
                                                   lhsT=uT[:, c, tt * 128:(tt + 1) * 128],
                                                   rhs=wslot[slot][:, woff:woff + width],
                                                   start=(c == 0), stop=(c == NC - 1)),
                          reads=[B_w[slot], B_uT[c][tt // 4]], writes=[bankB[bank]], pe_acc=(c > 0 or k > 0),
                          inc=(c == NC - 1 and k == ntile - 1))
            def ev():
                src_ = banks[bank][:, 0:ntile * width].rearrange("p (k w) -> p k w", w=width)
                dst_ = vbuf[vp][:, t0:t0 + ntile, 0:width]
                if evac_eng == "act":
                    kb.op("act", lambda e: e.copy(out=dst_, in_=src_),
                          reads=[bankB[bank]], writes=[B_v[vp][t0 + k] for k in range(ntile)])
                else:
                    kb.op("dve", lambda e: e.tensor_copy(out=dst_, in_=src_),
                          reads=[bankB[bank]], writes=[B_v[vp][t0 + k] for k in range(ntile)])
            if defer_fn is None:
                ev()
            else:
                defer_fn(bank, ev)

        def gate_from_bank(bank, gc, b, use_act=False, part="all"):
            cols = slice(b * 512, (b + 1) * 512)
            if part == "dve":
                kb.op("dve", lambda e: e.tensor_tensor(out=gate[:, gc, cols], in0=banks[bank][:, :], in1=etmp[:, 1, :],
                                                       op=ALU.mult),
                      reads=[bankB[bank], B_etmp[1]], writes=[B_gate[gc][b]])
                return
            kb.op("act", lambda e: e.activation(out=etmp[:, 1, :], in_=banks[bank][:, :], func=AF.Exp, scale=-1.0),
                  reads=[bankB[bank]], writes=[B_etmp[1]])
            if use_act:
                kb.op("act", lambda e: e.activation(out=etmp[:, 1, :], in_=etmp[:, 1, :], func=AF.Ln, bias=ONE_AP),
                      reads=[B_etmp[1], B_dpar], writes=[B_etmp[1]])
                kb.op("act", lambda e: e.activation(out=etmp[:, 1, :], in_=etmp[:, 1, :], func=AF.Exp, scale=-1.0),
                      reads=[B_etmp[1]], writes=[B_etmp[1]])
            else:
                kb.op("dve", lambda e: e.tensor_scalar(out=etmp[:, 1, :], in0=etmp[:, 1, :], scalar1=1.0, scalar2=None,
                                                       op0=ALU.add),
                      reads=[B_etmp[1]], writes=[B_etmp[1]])
                kb.op("dve", lambda e: e.reciprocal(out=etmp[:, 1, :], in_=etmp[:, 1, :]),
                      reads=[B_etmp[1]], writes=[B_etmp[1]])
            if part == "act":
                return
            kb.op("dve", lambda e: e.tensor_tensor(out=gate[:, gc, cols], in0=banks[bank][:, :], in1=etmp[:, 1, :],
                                                   op=ALU.mult),
                  reads=[bankB[bank], B_etmp[1]], writes=[B_gate[gc][b]])

        def out_proj_unit(l, wo_units, gcs, m, b, bank, defer_fn=None):
            n = len(wo_units)
            for j in range(n):
                s = wo_pos[(l, wo_units[j])] % NOSLOT
                kb.op("pe", lambda e: e.matmul(banks[bank][:, :], lhsT=woslot[s][:, m * 128:(m + 1) * 128],
                                               rhs=gate[:, gcs[j], b * 512:(b + 1) * 512],
                                               start=(j == 0), stop=(j == n - 1)),
                      reads=[B_wo[s], B_gate[gcs[j]][b]], writes=[bankB[bank]], pe_acc=(j > 0), inc=(j == n - 1))
            def ev():
                kb.op("dve", lambda e: e.tensor_tensor(out=hT[:, m, b * 512:(b + 1) * 512],
                                                       in0=hT[:, m, b * 512:(b + 1) * 512],
                                                       in1=banks[bank][:, :], op=ALU.add),
                      reads=[bankB[bank], B_hT[m][b]], writes=[B_hT[m][b]])
            if defer_fn is None:
                ev()
            else:
                defer_fn(bank, ev)

        def tile_geom(b, i):
            if i >= 4 * b:
                return 128 * (i - 4 * b), True
            return 0, False

        def sb_layer(l):
            rmsnorm_to_uT(l)
            BZ = [0, 1]
            B_ser = [Buf("ser0"), Buf("ser1"), Buf("ser2")]
            BC = [2, 3]
            BO = 5
            BMM = [4, 6, 7]
            H = cfg.H_SB
            mmrr = {"i": 0}

            bank_evac = {bk: None for bk in BMM}
            cur_step = {"s": -1}

            def flush_bank(bk):
                while bank_evac[bk] is not None:
                    f = bank_evac[bk][1]
                    bank_evac[bk] = None
                    f()

            def flush_old_evacs():
                for bk in BMM:
                    if bank_evac[bk] is not None and bank_evac[bk][0] < cur_step["s"]:
                        flush_bank(bk)

            def next_mm():
                mmrr["i"] += 1
                bk = BMM[mmrr["i"] % len(BMM)]
                flush_bank(bk)
                return bk

            def mm_available():
                bk = BMM[(mmrr["i"] + 1) % len(BMM)]
                return bank_evac[bk] is None

            def defer(bk, f, delay=1):
                bank_evac[bk] = (cur_step["s"] + delay - 1, f)

            issue_w(unit_pos[(l, 0)])
            issue_wo(wo_pos[(l, 0)])

            def proj_units(h):
                u = unit_pos[(l, h)]
                slot = u % NSLOT
                par_ = h % 2
                qs, ks = [], []
                for b in range(NB):
                    def fq(b=b):
                        cols = slice(b * 512, (b + 1) * 512)
                        bank = next_mm()
                        proj_fm(slot, 0, b, bank)
                        defer(bank, lambda: kb.op("dve", lambda e: e.tensor_scalar(
                            out=qk[par_][:, cols], in0=banks[bank][:, :], scalar1=scale, scalar2=None, op0=ALU.mult),
                            reads=[bankB[bank]], writes=[B_qk[par_][b]]), delay=2)
                    qs.append(fq)

                    def fk(b=b):
                        cols = slice(b * 512, (b + 1) * 512)
                        bank = next_mm()
                        proj_fm(slot, 1, b, bank)
                        defer(bank, lambda: kb.op("dve", lambda e: e.tensor_copy(out=qk[2 + par_][:, cols], in_=banks[bank][:, :]),
                                                  reads=[bankB[bank]], writes=[B_qk[2 + par_][b]]), delay=2)
                    ks.append(fk)
                gs, vs = [], []
                for b in range(NB):
                    def fg(b=b):
                        bank = next_mm()
                        proj_fm(slot, 2, b, bank)
                        def gA(bank=bank, b=b):
                            for obk in BMM:
                                if obk != bank and bank_evac[obk] is not None and getattr(bank_evac[obk][1], "is_gate_dve", False):
                                    flush_bank(obk)
                            gate_from_bank(bank, par_, b, use_act=True, part="act")
                            def gB():
                                gate_from_bank(bank, par_, b, use_act=True, part="dve")
                            gB.is_gate_dve = True
                            defer(bank, gB, delay=1)
                        defer(bank, gA, delay=2)
                    gs.append(fg)
                for t0 in range(0, NT, 4):
                    def fv(t0=t0):
                        bank = next_mm()
                        proj_v(slot, 3, 128, par_, t0, 4, bank, defer_fn=defer)
                    vs.append(fv)
                return qs, ks, gs, vs

            def outproj_units(h):
                us = []
                for b in range(NB):
                    for m in range(NC):
                        def fo(b=b, m=m):
                            bank = next_mm()
                            out_proj_unit(l, [h], [h % 2], m, b, bank, defer_fn=defer)
                        us.append(fo)
                return us

            qs, ks, gs, vs = proj_units(0)
            for f in qs + ks + gs + vs:
                f()
            for bk in BMM:
                flush_bank(bk)
            issue_w(unit_pos[(l, 0)] + 1)

            items = []
            for h in range(H):
                for b in range(NB):
                    n = 4 * b + 4
                    for i in range(n - 1, -1, -1):
                        off, diag = tile_geom(b, i)
                        items.append(dict(h=h, b=b, i=i, n=n, off=off, diag=diag))
            for j, it in enumerate(items):
                it["j"] = j
            nitems_head = len(items) // H
            flags = {"p4_done_head": -1, "y_done_head": -1}

            def P1(it):
                j, h, b, i, off, diag = it["j"], it["h"], it["b"], it["i"], it["off"], it["diag"]
                zb = BZ[j % 2]
                qb, kbuf = h % 2, 2 + h % 2
                qc = b * 512
                kb.op("pe", lambda e: e.matmul(banks[zb][:, off:512], lhsT=qk[kbuf][:, i * 128:(i + 1) * 128],
                                               rhs=qk[qb][:, qc + off:qc + 512], start=True, stop=(not diag)),
                      reads=[B_qk[kbuf][i // 4], B_qk[qb][b]], writes=[bankB[zb]], inc=(not diag))
                if diag:
                    kb.op("pe", lambda e: e.matmul(banks[zb][:, off:off + 128], lhsT=ident, rhs=mneg_s,
                                                   start=False, stop=True),
                          reads=[B_c], writes=[bankB[zb]], pe_acc=True)

            def A1(it):
                j, off = it["j"], it["off"]
                zb = BZ[j % 2]
                kb.op("act", lambda e: e.activation(out=ering[:, j % NE, off:512], in_=banks[zb][:, off:512], func=AF.Exp),
                      reads=[bankB[zb]], writes=[B_E[j % NE]])

            def A2(it):
                j, off = it["j"], it["off"]
                kb.op("act", lambda e: e.activation(out=spr[:, j % NSP, off:512], in_=ering[:, j % NE, off:512],
                                                    func=AF.Ln, bias=ONE_AP),
                      reads=[B_E[j % NE], B_dpar], writes=[B_sp[j % NSP]])

            def D1(it):
                j, i, n, off = it["j"], it["i"], it["n"], it["off"]
                if i == 0:
                    return
                cb = BC[j % 2]
                kb.op("dve", lambda e: e.tensor_copy(out=cs_sb[:, j % 2, off:512], in_=banks[cb][:, off:512]),
                      reads=[bankB[cb]], writes=[B_cs[j % 2], B_ser[j % 3]])

            def P3(it):
                j, i, n, off = it["j"], it["i"], it["n"], it["off"]
                cb = BC[j % 2]
                first = (i == n - 1)
                kb.op("pe", lambda e: e.matmul(banks[cb][:, off:512], lhsT=uincl, rhs=spr[:, j % NSP, off:512],
                                               start=True, stop=first, skip_group_check=True),
                      reads=[B_sp[j % NSP], B_c], writes=[bankB[cb]], inc=first)
                if not first:
                    poff = tile_geom(it["b"], i + 1)[0]
                    kb.op("pe", lambda e: e.matmul(banks[cb][:, poff:512], lhsT=cbf[:, C_BSEL:C_BSEL + 128],
                                                   rhs=cs_sb[:, (j - 1) % 2, poff:512], start=False, stop=True,
                                                   skip_group_check=True),
                          reads=[B_cs[(j - 1) % 2], B_c], writes=[bankB[cb]], pe_acc=True)

            def A3(it):
                j, off = it["j"], it["off"]
                cb = BC[j % 2]
                kb.op("act", lambda e: e.activation(out=wtmp[:, j % 2, off:512], in_=banks[cb][:, off:512],
                                                    func=AF.Exp, scale=-1.0),
                      reads=[bankB[cb], B_ser[j % 3]], writes=[B_wt[j % 2]])

            def D2(it):
                j, off = it["j"], it["off"]
                kb.op("pool", lambda e: e.tensor_tensor(out=attn[:, j % 3, off:512], in0=ering[:, j % NE, off:512],
                                                        in1=wtmp[:, j % 2, off:512], op=ALU.mult),
                      reads=[B_E[j % NE], B_wt[j % 2]], writes=[B_attn[j % 3]])

            def P4(it):
                j, h, b, i, n, off = it["j"], it["h"], it["b"], it["i"], it["n"], it["off"]
                vp = h % 2
                kb.op("pe", lambda e: e.matmul(banks[BO][:, off:512], lhsT=vbuf[vp][:, i, 0:128],
                                               rhs=attn[:, j % 3, off:512], start=(i == n - 1), stop=(i == 0),
                                               skip_group_check=True),
                      reads=[B_v[vp][i], B_attn[j % 3]], writes=[bankB[BO]], pe_acc=(i < n - 1))
                if i == 0:
                    cols = slice(b * 512, (b + 1) * 512)
                    gc = h % 2
                    kb.op("dve", lambda e: e.tensor_tensor(out=gate[:, gc, cols], in0=banks[BO][:, :], in1=gate[:, gc, cols],
                                                           op=ALU.mult),
                          reads=[bankB[BO], B_gate[gc][b]], writes=[B_gate[gc][b]])
                    if b == NB - 1:
                        flags["p4_done_head"] = h
                        flags["y_done_head"] = h

            stages = [(4, D1), (5, D2), (1, A1), (2, A2), (4, A3), (0, P1), (7, P4), (3, P3)]
            maxoff = 7
            nsteps = len(items) + maxoff

            bgq = []
            bg_head = {"h": -1}

            def enqueue_bg(h):
                if h + 1 < H:
                    qs, ks, gs, vs = proj_units(h + 1)
                else:
                    qs, ks, gs, vs = [], [], [], []
                ops = outproj_units(h - 1) if h - 1 >= 0 else []
                pre_y = (lambda hh=h - 1: flags["y_done_head"] >= hh)
                pre_v = (lambda hh=h - 1: flags["p4_done_head"] >= hh)
                lst = [(f, None, 8.0) for f in qs]
                heavy = [(f, None, 8.0) for f in ks] + [(f, pre_v, 8.0) for f in vs]
                opl = [(f, pre_y, 3.0) for f in ops]
                per = 4
                while heavy or opl:
                    if heavy:
                        lst.append(heavy.pop(0))
                    for _ in range(per):
                        if opl:
                            lst.append(opl.pop(0))
                lst += [(f, None, 8.0) for f in gs]
                return lst

            def run_bg(budget, force=False):
                spent = 0.0
                while bgq and (force or spent + 0.5 * bgq[0][2] <= budget):
                    if not force and not mm_available():
                        break
                    f, pre, cost = bgq[0]
                    if pre is not None and not pre():
                        if force:
                            raise RuntimeError("background precondition not met at forced drain")
                        break
                    bgq.pop(0)
                    f()
                    spent += cost
                return spent

            credit = 0.0
            rate = 0.0
            pend_w = {"step": -1, "idx": 0}
            for s in range(nsteps):
                if s < len(items):
                    h = items[s]["h"]
                    if h != bg_head["h"]:
                        run_bg(0, force=True)
                        for bk in BMM:
                            flush_bank(bk)
                        bg_head["h"] = h
                        bgq.extend(enqueue_bg(h))
                        pend_w["step"] = s + 8
                        pend_w["idx"] = unit_pos[(l, h)] + 2
                        issue_wo(wo_pos[(l, h)] + 1)
                        rate = sum(c for _, _, c in bgq) / float(max(3, nitems_head - 14))
                        credit = 0.0
                cur_step["s"] = s
                if pend_w["step"] == s:
                    issue_w(pend_w["idx"])
                for off_, fn in stages:
                    j = s - off_
                    if 0 <= j < len(items):
                        fn(items[j])
                    if fn is D1:
                        flush_old_evacs()
                credit += rate
                credit -= run_bg(credit)
                credit = min(credit, 3.0 * rate)
            run_bg(0, force=True)
            for bk in BMM:
                flush_bank(bk)
            issue_w(pend_w["idx"])
            for f in outproj_units(H - 1):
                f()
            for bk in BMM:
                flush_bank(bk)


        def df_layer(l):
            rmsnorm_to_uT(l)
            pb = l * PL
            db = l * 4
            BZ = [0, 1]
            SETS = [dict(O=[2, 3], SS=4), dict(O=[5, 6], SS=7)]
            H = cfg.H_DF
            kb.dma("sp", scr_f[:, 0:2 * S], rope_d[:, :], "d_misc", writes=SCR_ALL)
            rope3 = scr_f[:, 0:2 * S].rearrange("p (a s) -> p a s", a=2)
            issue_w(unit_pos[(l, 0)] + 1)
            gq_s = dpar[:, db:db + 1]
            gk = par[:, pb + NC + 1:pb + NC + 2]
            nlam = dpar[:, db + 3:db + 4]
            zc = {"i": 0}
            grp = {"i": 0}

            op_pending = []
            op_evac = {0: None, 1: None}
            op_rr = {"i": 0}

            def op_flush(bk):
                if op_evac[bk] is not None:
                    f = op_evac[bk]
                    op_evac[bk] = None
                    f()

            def make_op_units(hd):
                us = []
                for b_ in range(NB):
                    for m_ in range(NC):
                        def fo(b_=b_, m_=m_):
                            op_rr["i"] += 1
                            bk = op_rr["i"] % 2
                            op_flush(bk)
                            out_proj_unit(l, [2 * hd, 2 * hd + 1], [0, 1], m_, b_, bk,
                                          defer_fn=lambda bank, ev: op_evac.__setitem__(bank, ev))
                        us.append(fo)
                return us

            def run_ops(k, flush=True):
                if flush:
                    for bk in (0, 1):
                        op_flush(bk)
                for _ in range(min(k, 2)):
                    if op_pending:
                        op_pending.pop(0)()

            epi_carry = []
            for hd in range(H):
                vp = hd % 2
                uA = unit_pos[(l, 2 * hd)]
                uB = uA + 1
                sA, sB = uA % NSLOT, uB % NSLOT
                def make_v_units():
                    us = []
                    for t0_ in range(0, NT, 2):
                        def fv(t0_=t0_):
                            op_rr["i"] += 1
                            bk = op_rr["i"] % 2
                            op_flush(bk)
                            proj_v(sB, 2, 256, vp, t0_, 2, bk,
                                   defer_fn=lambda bank, ev: op_evac.__setitem__(bank, ev), evac_eng="act")
                        us.append(fv)
                    return us

                vus = make_v_units()
                merged = []
                while op_pending or vus:
                    for _ in range(4):
                        if op_pending:
                            merged.append(op_pending.pop(0))
                    if vus:
                        merged.append(vus.pop(0))
                op_pending.extend(merged)
                qitems = [(qi, b) for qi in range(4) for b in range(NB)]
                MMR = [4, 5, 6, 7]
                BSQ = 2
                BRB = 3

                def q_proj(j, qi, b):
                    proj_fm(sA, qi, b, MMR[j % 4])

                def q_sq(j, qi, b):
                    kb.op("act", lambda e: e.activation(out=attn[:, j % 2, :], in_=banks[MMR[j % 4]][:, :], func=AF.Square),
                          reads=[bankB[MMR[j % 4]]], writes=[B_attn[j % 2]])

                def q_ss(j, qi, b):
                    kb.op("pe", lambda e: e.matmul(banks[BSQ][:, :], lhsT=ones, rhs=attn[:, j % 2, :], start=True, stop=True),
                          reads=[B_attn[j % 2], B_c], writes=[bankB[BSQ]])

                def q_ln(j, qi, b):
                    kb.op("act", lambda e: e.activation(out=etmp[:, j % 2, :], in_=banks[BSQ][:, :], func=AF.Ln,
                                                        scale=1.0 / HEAD_DIM, bias=EPS_AP),
                          reads=[bankB[BSQ], B_dpar], writes=[B_etmp[j % 2]])
                    kb.op("act", lambda e: e.activation(out=etmp[:, j % 2, :], in_=etmp[:, j % 2, :], func=AF.Exp, scale=-0.5),
                          reads=[B_etmp[j % 2]], writes=[B_etmp[j % 2]])

                def q_qg(j, qi, b):
                    gcol = gq_s if qi < 2 else gk
                    kb.op("dve", lambda e: e.scalar_tensor_tensor(out=spr[:, j % NSP, :], in0=banks[MMR[j % 4]][:, :], scalar=gcol,
                                                                  in1=etmp[:, j % 2, :], op0=ALU.mult, op1=ALU.mult),
                          reads=[bankB[MMR[j % 4]], B_etmp[j % 2], B_dpar, B_par], writes=[B_sp[j % NSP]])

                def q_rot(j, qi, b):
                    kb.op("pe", lambda e: e.matmul(banks[BRB][:, :], lhsT=pswap, rhs=spr[:, j % NSP, :], start=True, stop=True),
                          reads=[B_sp[j % NSP], B_c], writes=[bankB[BRB]])

                def q_t(j, qi, b):
                    cols = slice(b * 512, (b + 1) * 512)
                    kb.op("pool", lambda e: e.tensor_tensor(out=wtmp[:, j % 2, :], in0=spr[:, j % NSP, :], in1=rope3[:, 0, cols],
                                                            op=ALU.mult),
                          reads=[B_sp[j % NSP]] + SCR_ALL, writes=[B_wt[j % 2]])
                    kb.op("dve", lambda e: e.tensor_tensor(out=o_sb[:, j % 2, :], in0=banks[BRB][:, :], in1=rope3[:, 1, cols],
                                                           op=ALU.mult),
                          reads=[bankB[BRB]] + SCR_ALL, writes=[B_o[j % 2]])

                def q_out(j, qi, b):
                    cols = slice(b * 512, (b + 1) * 512)
                    kb.op("pool", lambda e: e.tensor_tensor(out=qk[qi][:, cols], in0=o_sb[:, j % 2, :], in1=wtmp[:, j % 2, :],
                                                            op=ALU.add),
                          reads=[B_o[j % 2], B_wt[j % 2]], writes=[B_qk[qi][b]])

                qst = [(5, q_out), (4, q_rot), (4, q_t), (3, q_qg), (2, q_ss), (2, q_ln), (1, q_sq), (0, q_proj)]
                nq = len(qitems) + 5
                for s_ in range(nq):
                    if s_ >= 1 and epi_carry:
                        epi_carry.pop(0)()
                    for bk in (0, 1):
                        op_flush(bk)
                    for off_, fn in qst:
                        j = s_ - off_
                        if 0 <= j < len(qitems):
                            fn(j, *qitems[j])
                    run_ops(2 if len(op_pending) > (nq - s_) else 1, flush=False)
                issue_w(uA + 2)
                while op_pending:
                    run_ops(2)
                for bk in (0, 1):
                    op_flush(bk)
                issue_wo(wo_pos[(l, 2 * hd + 1)])
                pre_gates = [(gc, b) for b in range(NB) for gc in range(2) if 4 * b + 4 < 8]
                gate_units = [(gc, b) for b in range(NB) for gc in range(2) if 4 * b + 4 >= 8]
                k = 0
                for gc_, gb_ in pre_gates:
                    bank = 4 + k % 4
                    k += 1
                    proj_fm(sB, gc_, gb_, bank)
                    gate_from_bank(bank, gc_, gb_, use_act=True)
                if not gate_units:
                    issue_w(uB + 2)
                epi = []

                def make_epilogue(b, ssb):
                    cols = slice(b * 512, (b + 1) * 512)

                    def e1():
                        for c in range(2):
                            kb.op("pool", lambda e: e.tensor_tensor(out=spr[:, c, :], in0=o_sb[:, c, :], in1=o_sb[:, c, :],
                                                                    op=ALU.mult),
                                  reads=[B_o[c]], writes=[B_sp[c]])

                    def e1b():
                        for c in range(2):
                            kb.op("pe", lambda e: e.matmul(banks[ssb][:, :], lhsT=ones, rhs=spr[:, c, :], start=(c == 0), stop=(c == 1)),
                                  reads=[B_sp[c], B_c], writes=[bankB[ssb]], pe_acc=(c > 0))

                    def e2():
                        kb.op("act", lambda e: e.activation(out=etmp[:, 0, :], in_=banks[ssb][:, :], func=AF.Ln,
                                                            scale=1.0 / (2 * HEAD_DIM), bias=EPS_AP),
                              reads=[bankB[ssb], B_dpar], writes=[B_etmp[0]])
                        kb.op("act", lambda e: e.activation(out=etmp[:, 0, :], in_=etmp[:, 0, :], func=AF.Exp, scale=-0.5),
                              reads=[B_etmp[0]], writes=[B_etmp[0]])

                    def e3():
                        for c in range(2):
                            kb.op("dve", lambda e: e.scalar_tensor_tensor(out=wtmp[:, c, :], in0=o_sb[:, c, :],
                                                                          scalar=dpar[:, db + 1 + c:db + 2 + c], in1=etmp[:, 0, :],
                                                                          op0=ALU.mult, op1=ALU.mult),
                                  reads=[B_o[c], B_dpar, B_etmp[0]], writes=[B_wt[c]])
                            kb.op("pool", lambda e: e.tensor_tensor(out=gate[:, c, cols], in0=wtmp[:, c, :], in1=gate[:, c, cols],
                                                                    op=ALU.mult),
                                  reads=[B_wt[c], B_gate[c][b]], writes=[B_gate[c][b]])
                    return [e1, e1b, e2, e3]

                tiles = []
                for b in range(NB):
                    for m in range(2):
                        n = 4 * b + 4
                        for i in range(n):
                            tiles.append((b, m, i, n))
                BZ3 = [0, 1, 2]
                OSETS = [[3, 4], [5, 6]]
                BSS1 = 7

                def gidx(b, m):
                    return 2 * b + m

                def qk_mm(t):
                    b, m, i, n = tiles[t]
                    off, diag = tile_geom(b, i)
                    zb = BZ3[t % 3]
                    qcols = b * 512
                    kb.op("pe", lambda e: e.matmul(banks[zb][:, off:512], lhsT=qk[2 + m][:, i * 128:(i + 1) * 128],
                                                   rhs=qk[m][:, qcols + off:qcols + 512], start=True, stop=(not diag)),
                          reads=[B_qk[2 + m][i // 4], B_qk[m][b]], writes=[bankB[zb]], inc=(not diag))
                    if diag:
                        kb.op("pe", lambda e: e.matmul(banks[zb][:, off:off + 128], lhsT=ident, rhs=mneg_i,
                                                       start=False, stop=True),
                              reads=[B_c], writes=[bankB[zb]], pe_acc=True)

                def act_p(t):
                    b, m, i, n = tiles[t]
                    off, diag = tile_geom(b, i)
                    zb = BZ3[t % 3]
                    kb.op("act", lambda e: e.activation(out=attn[:, t % NATT, off:512], in_=banks[zb][:, off:512], func=AF.Exp),
                          reads=[bankB[zb]], writes=[B_attn[t % NATT]])

                def av_mm(t):
                    b, m, i, n = tiles[t]
                    off, diag = tile_geom(b, i)
                    BOa = OSETS[gidx(b, m) % 2]
                    for c in range(2):
                        kb.op("pe", lambda e: e.matmul(banks[BOa[c]][:, off:512],
                                                       lhsT=vbuf[vp][:, i, c * 128:(c + 1) * 128],
                                                       rhs=attn[:, t % NATT, off:512], start=(i == 0), stop=(i == n - 1)),
                              reads=[B_v[vp][i], B_attn[t % NATT]], writes=[bankB[BOa[c]]], pe_acc=(i > 0), inc=(c == 1))
                    p_ = gidx(b, m) % 2
                    for hf, eng_ in ((0, "dve"), (1, "pool")):
                        lo, hi = (max(off, 0), 384) if hf == 0 else (max(off, 384), 512)
                        if lo >= hi:
                            continue
                        if i == 0:
                            kb.op(eng_, lambda e: e.tensor_copy(out=sacc[:, p_, lo:hi], in_=attn[:, t % NATT, lo:hi]),
                                  reads=[B_attn[t % NATT]], writes=[B_sacc[p_][hf]])
                        else:
                            kb.op(eng_, lambda e: e.tensor_tensor(out=sacc[:, p_, lo:hi], in0=sacc[:, p_, lo:hi],
                                                                  in1=attn[:, t % NATT, lo:hi], op=ALU.add),
                                  reads=[B_attn[t % NATT], B_sacc[p_][hf]], writes=[B_sacc[p_][hf]])

                def make_evac(b, m):
                    BOa = OSETS[gidx(b, m) % 2]

                    def ev():
                        p_ = gidx(b, m) % 2
                        kb.op("dve", lambda e: e.tensor_copy(out=spr[:, 2, :], in_=sacc[:, p_, :]),
                              reads=B_sacc[p_], writes=[B_sp[2]])
                        kb.op("pe", lambda e: e.matmul(banks[BSS1][:, :], lhsT=ones, rhs=spr[:, 2, :], start=True, stop=True),
                              reads=[B_sp[2], B_c], writes=[bankB[BSS1]])
                        kb.op("act", lambda e: e.activation(out=etmp[:, 1, :], in_=banks[BSS1][:, :], func=AF.Ln),
                              reads=[bankB[BSS1]], writes=[B_etmp[1]])
                        kb.op("act", lambda e: e.activation(out=etmp[:, 1, :], in_=etmp[:, 1, :], func=AF.Exp, scale=-1.0),
                              reads=[B_etmp[1]], writes=[B_etmp[1]])
                        for c in range(2):
                            if m == 0:
                                kb.op("dve", lambda e: e.tensor_tensor(out=o_sb[:, c, :], in0=banks[BOa[c]][:, :],
                                                                       in1=etmp[:, 1, :], op=ALU.mult),
                                      reads=[bankB[BOa[c]], B_etmp[1]], writes=[B_o[c]])
                            else:
                                kb.op("dve", lambda e: e.tensor_tensor(out=wtmp[:, c, :], in0=banks[BOa[c]][:, :],
                                                                       in1=etmp[:, 1, :], op=ALU.mult),
                                      reads=[bankB[BOa[c]], B_etmp[1]], writes=[B_wt[c]])
                                kb.op("dve", lambda e: e.scalar_tensor_tensor(out=o_sb[:, c, :], in0=wtmp[:, c, :], scalar=nlam,
                                                                              in1=o_sb[:, c, :], op0=ALU.mult, op1=ALU.add),
                                      reads=[B_wt[c], B_o[c], B_dpar], writes=[B_o[c]])
                        if m == 1:
                            epi.extend(make_epilogue(b, BOa[0] if b < NB - 1 else 3))
                    return ev

                pend_evac = []
                gchain = []
                gtmp = cs_sb[:, :, :].rearrange("p a n -> p (a n)").bitcast(F32)

                def gate_chain(bank, gc, b_):
                    cols_ = slice(b_ * 512, (b_ + 1) * 512)

                    def g1():
                        kb.op("act", lambda e: e.activation(out=gtmp, in_=banks[bank][:, :], func=AF.Exp, scale=-1.0),
                              reads=[bankB[bank]], writes=B_cs)

                    def g2():
                        kb.op("act", lambda e: e.activation(out=gtmp, in_=gtmp, func=AF.Ln, bias=ONE_AP),
                              reads=B_cs + [B_dpar], writes=B_cs)

                    def g3():
                        kb.op("act", lambda e: e.activation(out=gtmp, in_=gtmp, func=AF.Exp, scale=-1.0),
                              reads=B_cs, writes=B_cs)
                        kb.op("dve", lambda e: e.tensor_tensor(out=gate[:, gc, cols_], in0=banks[bank][:, :], in1=gtmp,
                                                               op=ALU.mult),
                              reads=[bankB[bank]] + B_cs, writes=[B_gate[gc][b_]])
                    return [g1, g2, g3]

                epi_age = {"n": 0}
                NTL = len(tiles)
                qk_mm(0)
                if NTL > 1:
                    qk_mm(1)
                for t in range(NTL):
                    b, m, i, n = tiles[t]
                    if t + 2 < NTL:
                        qk_mm(t + 2)
                    act_p(t)
                    if pend_evac and (i >= 2 or i == n - 1):
                        while pend_evac:
                            pend_evac.pop(0)()
                    av_mm(t)
                    if epi:
                        epi_age["n"] += 1
                        if epi_age["n"] >= 3:
                            epi.pop(0)()
                    else:
                        epi_age["n"] = 0
                    if gchain:
                        gchain.pop(0)()
                        if i == n - 1:
                            while gchain:
                                gchain.pop(0)()
                    elif gate_units and n >= 8 and (i == 6 or (n >= 16 and i == 10)):
                        gc_, gb_ = gate_units.pop(0)
                        gbank = OSETS[(gidx(b, m) + 1) % 2][1]
                        proj_fm(sB, gc_, gb_, gbank)
                        gchain.extend(gate_chain(gbank, gc_, gb_))
                        if not gate_units:
                            issue_w(uB + 2)
                    if i == n - 1:
                        pend_evac.append(make_evac(b, m))
                while pend_evac:
                    pend_evac.pop(0)()
                while gchain:
                    gchain.pop(0)()
                assert not gate_units
                assert len(epi) == 4 and not epi_carry
                epi.pop(0)()
                if hd + 1 < H:
                    e1b_, e2_, e3_ = epi
                    epi_carry.extend([lambda e1b_=e1b_, e2_=e2_: (e1b_(), e2_()), e3_])
                else:
                    while epi:
                        epi.pop(0)()
                del epi[:]
                op_pending.extend(make_op_units(hd))
            while op_pending:
                run_ops(2)
            for bk in (0, 1):
                op_flush(bk)

        for l in range(depth):
            if l % 2 == 0:
                sb_layer(l)
            else:
                df_layer(l)

        mmb = 0
        for si, t0 in enumerate(range(0, NT, tiles_per_stage)):
            nt = min(tiles_per_stage, NT - t0)
            sbuf_ap, scap, sbufs = stage_bufs[si % len(stage_bufs)]
            xv = sbuf_ap[:, 0:nt * D].rearrange("p (a d) -> p a d", a=nt)
            for a_ in range(nt):
                tt = t0 + a_
                b_ = tt // 4
                for c0 in range(0, NC, 4):
                    cn = min(4, NC - c0)
                    bk = 4 + mmb % 4
                    mmb += 1
                    for k_ in range(cn):
                        c = c0 + k_
                        kb.op("pe", lambda e: e.transpose(out=banks[bk][:, k_ * 128:(k_ + 1) * 128],
                                                          in_=hT[:, c, tt * 128:(tt + 1) * 128], identity=ident_f),
                              reads=[B_hT[c][b_], B_c], writes=[bankB[bk]], pe_acc=(k_ > 0))
                    kb.op("dve", lambda e: e.tensor_copy(out=xv[:, a_, c0 * 128:(c0 + cn) * 128], in_=banks[bk][:, 0:cn * 128]),
                          reads=[bankB[bk]], writes=sbufs)
            for a_ in range(nt):
                kb.dma("sp", out_d[(t0 + a_) * 128:(t0 + a_ + 1) * 128, :], xv[:, a_, :],
                       f"d_out{si % len(stage_bufs)}", reads=sbufs)
        kb.wait_all("sp", [f"d_out{i}" for i in range(len(stage_bufs))])
        print(f"[build] instructions={kb.n_ins} waits={kb.n_wait}")
    return nc


def prep_weights(cfg, inputs):
    D, E, NC = cfg.D, cfg.E, cfg.NC
    out = {}
    P = np.zeros((128, cfg.depth * cfg.PL), np.float32)
    for l in range(cfg.depth):
        j = l // 2
        pb = l * cfg.PL
        if l % 2 == 0:
            W = np.asarray(inputs["sb_w_in"][j], np.float32)
            H = cfg.H_SB
            Wh = W.reshape(NC, 128, 4, H, 128)
            sel = Wh[:, :, [0, 1, 3, 2]]
            win = np.ascontiguousarray(sel.transpose(3, 1, 0, 2, 4)).reshape(H, 128, NC * 4 * 128)
            out[f"win{l}"] = win
            out[f"wout{l}"] = np.ascontiguousarray(np.asarray(inputs["sb_w_out"][j], np.float32))
            g = np.asarray(inputs["sb_norm"][j], np.float32)
            P[:, pb:pb + NC] = g.reshape(NC, 128).T
        else:
            W = np.asarray(inputs["df_w_in"][j], np.float32)
            H = cfg.H_DF
            Wr = W.reshape(NC, 128, 4, H, 2, 128)
            chunks = np.stack([Wr[:, :, 0, :, 0], Wr[:, :, 0, :, 1], Wr[:, :, 1, :, 0], Wr[:, :, 1, :, 1],
                               Wr[:, :, 3, :, 0], Wr[:, :, 3, :, 1], Wr[:, :, 2, :, 0], Wr[:, :, 2, :, 1]], axis=0)
            chunks = chunks.reshape(2, 4, NC, 128, H, 128)
            win = np.ascontiguousarray(chunks.transpose(4, 0, 3, 2, 1, 5)).reshape(H * 2, 128, NC * 4 * 128)
            out[f"win{l}"] = win
            out[f"wout{l}"] = np.ascontiguousarray(np.asarray(inputs["df_w_out"][j], np.float32))
            g = np.asarray(inputs["df_norm"][j], np.float32)
            P[:, pb:pb + NC] = g.reshape(NC, 128).T
            P[:, pb + NC] = np.asarray(inputs["df_q_norm"][j], np.float32)
            P[:, pb + NC + 1] = np.asarray(inputs["df_k_norm"][j], np.float32)
            P[:, pb + NC + 2] = np.asarray(inputs["df_lam_q1"][j], np.float32)
            P[:, pb + NC + 3] = np.asarray(inputs["df_lam_k1"][j], np.float32)
            P[:, pb + NC + 4] = np.asarray(inputs["df_lam_q2"][j], np.float32)
            P[:, pb + NC + 5] = np.asarray(inputs["df_lam_k2"][j], np.float32)
            gs = np.asarray(inputs["df_sub_norm"][j], np.float32)
            P[:, pb + NC + 6] = gs[:128]
            P[:, pb + NC + 7] = gs[128:]
    out["params"] = P
    out["consts"] = make_consts()
    out["rope"] = make_rope(cfg.S)
    return out


_PROG_CACHE = {}


def run(cfg, inputs, n_cores, trace=False):
    key = (cfg.S, cfg.D, cfg.depth)
    if key not in _PROG_CACHE:
        _PROG_CACHE[key] = build_program(cfg)
    nc = _PROG_CACHE[key]
    shared = prep_weights(cfg, inputs)
    x = np.asarray(inputs["x"], np.float32)
    in_maps = []
    for b in range(n_cores):
        m = dict(shared)
        m["x"] = np.ascontiguousarray(x[b])
        in_maps.append(m)
    res = run_bass_kernel_spmd(nc, in_maps, core_ids=list(range(n_cores)), trace=trace)
    out = np.stack([np.asarray(r["out"], np.float32) for r in res.results], axis=0)
    return out, res


def kernel(x, sb_norm, sb_w_in, sb_w_out, df_norm, df_w_in, df_w_out,
           df_q_norm, df_k_norm, df_lam_q1, df_lam_k1, df_lam_q2, df_lam_k2, df_sub_norm):
    inputs = dict(x=x, sb_norm=sb_norm, sb_w_in=sb_w_in, sb_w_out=sb_w_out, df_norm=df_norm,
                  df_w_in=df_w_in, df_w_out=df_w_out, df_q_norm=df_q_norm, df_k_norm=df_k_norm,
                  df_lam_q1=df_lam_q1, df_lam_k1=df_lam_k1, df_lam_q2=df_lam_q2, df_lam_k2=df_lam_k2,
                  df_sub_norm=df_sub_norm)
    cfg = Cfg(S=2048, D=1024, depth=4)
    out, _ = run(cfg, inputs, 8)
    return out.astype(np.float32)
```

```python
import math
from contextlib import ExitStack

import numpy as np
import concourse.bass as bass
import concourse.mybir as mybir
from concourse.bass_utils import run_bass_kernel_spmd

F32 = mybir.dt.float32
BF16 = mybir.dt.bfloat16
AF = mybir.ActivationFunctionType
ALU = mybir.AluOpType

HEAD_DIM = 128
EPS = 1e-6
ROPE_THETA = 10000.0
MASKV = 30000.0


class Cfg:
    def __init__(self, S=2048, D=1024, depth=4):
        self.S, self.D, self.depth = S, D, depth
        self.E = 2 * D
        self.NC = D // 128
        self.NB = S // 512
        self.NT = S // 128
        self.H_SB = self.E // 128
        self.H_DF = self.E // 256
        self.PL = self.NC + 8
        self.UW = self.NC * 4 * 128


class Buf:
    __slots__ = ("name", "w", "r")

    def __init__(self, name):
        self.name = name
        self.w = None
        self.r = {}


class KB:
    def __init__(self, nc, es):
        self.nc = nc
        self.es = es
        self.eng = {"pe": nc.tensor, "act": nc.scalar, "dve": nc.vector,
                    "pool": nc.gpsimd, "sp": nc.sync}
        self.sems = {}
        self.cnt = {}
        self.seen = {e: {} for e in self.eng}
        for e in self.eng:
            self._sem(e)
        self.n_wait = 0
        self.n_ins = 0

    def _sem(self, key):
        if key not in self.sems:
            self.sems[key] = self.es.enter_context(self.nc.semaphore("s_" + key))
            self.cnt[key] = 0
        return self.sems[key]

    def _waits(self, eng, reads, writes, pe_acc):
        deps = {}

        def add(ev):
            if ev is None:
                return
            k, v = ev
            if deps.get(k, 0) < v:
                deps[k] = v

        for b in reads:
            add(b.w)
        for b in writes:
            if not (pe_acc and eng == "pe" and b.w is not None and b.w[0] == "pe"):
                add(b.w)
            for k, v in b.r.items():
                add((k, v))
        seen = self.seen[eng]
        for k, v in deps.items():
            if k.startswith("d_"):
                v = self.cnt[k]
            if seen.get(k, 0) < v:
                self.eng[eng].wait_ge(self.sems[k], v)
                seen[k] = v
                self.n_wait += 1

    def _record(self, ev, reads, writes):
        k, v = ev
        for b in reads:
            if b.r.get(k, 0) < v:
                b.r[k] = v
        for b in writes:
            b.w = ev
            b.r = {}

    def op(self, eng, fn, reads=(), writes=(), pe_acc=False, inc=True):
        self._waits(eng, reads, writes, pe_acc)
        ins = fn(self.eng[eng])
        self.n_ins += 1
        if inc:
            self.cnt[eng] += 1
            ins.then_inc(self.sems[eng], 1)
            ev = (eng, self.cnt[eng])
        else:
            ev = (eng, self.cnt[eng] + 1)
        self._record(ev, reads, writes)
        return ev

    def dma(self, q, out, in_, semkey, reads=(), writes=(), **kw):
        self._sem(semkey)
        self._waits(q, reads, writes, False)
        ins = self.eng[q].dma_start(out=out, in_=in_, **kw)
        self.n_ins += 1
        self.cnt[semkey] += 16
        ins.then_inc(self.sems[semkey], 16)
        ev = (semkey, self.cnt[semkey])
        self._record(ev, reads, writes)
        return ev

    def wait_all(self, eng, keys):
        for k in keys:
            v = self.cnt.get(k, 0)
            if v > 0 and self.seen[eng].get(k, 0) < v:
                self.eng[eng].wait_ge(self.sems[k], v)
                self.seen[eng][k] = v


C_IDENT, C_UINCL, C_MNEG_S, C_MPOS_S, C_MNEG_I, C_ONES, C_PSWAP, C_EBIG = (
    0, 128, 256, 384, 512, 640, 768, 896)
C_BSEL = 896 + 144
C_TOTAL = C_BSEL + 128


def make_consts():
    c = np.zeros((128, C_TOTAL), np.float32)
    p = np.arange(128)[:, None]
    f = np.arange(128)[None, :]
    c[:, C_IDENT:C_IDENT + 128] = (p == f)
    c[:, C_UINCL:C_UINCL + 128] = (p >= f)
    c[:, C_MNEG_S:C_MNEG_S + 128] = np.where(p >= f, -MASKV, 0.0)
    c[:, C_MPOS_S:C_MPOS_S + 128] = np.where(p >= f, MASKV, 0.0)
    c[:, C_MNEG_I:C_MNEG_I + 128] = np.where(p > f, -MASKV, 0.0)
    c[:, C_ONES:C_ONES + 128] = 1.0
    c[:, C_PSWAP:C_PSWAP + 128] = (p == (f + 64) % 128)
    c[:, C_EBIG + 15] = 1.0
    c[0, C_BSEL:C_BSEL + 128] = 1.0
    return c


def make_rope(S):
    inv = (1.0 / (ROPE_THETA ** (np.arange(0, HEAD_DIM, 2, dtype=np.float32) / np.float32(HEAD_DIM)))).astype(np.float32)
    ang = (np.arange(S, dtype=np.float32)[:, None] * inv[None, :]).astype(np.float32)
    cos = np.cos(ang).astype(np.float32).T
    sin = np.sin(ang).astype(np.float32).T
    out = np.zeros((128, 2, S), np.float32)
    out[:64, 0] = cos
    out[64:, 0] = cos
    out[:64, 1] = -sin
    out[64:, 1] = sin
    return out.reshape(128, 2 * S)


def build_program(cfg):
    S, D, E, NC, NB, NT = cfg.S, cfg.D, cfg.E, cfg.NC, cfg.NB, cfg.NT
    depth, PL, UW = cfg.depth, cfg.PL, cfg.UW
    scale = 1.0 / math.sqrt(HEAD_DIM)

    nc = bass.Bass("TRN2", target_bir_lowering=False)
    es = ExitStack()
    with es:
        x_d = nc.dram_tensor("x", [S, D], F32, kind="ExternalInput").ap()
        out_d = nc.dram_tensor("out", [S, D], F32, kind="ExternalOutput").ap()
        consts_d = nc.dram_tensor("consts", [128, C_TOTAL], F32, kind="ExternalInput").ap()
        rope_d = nc.dram_tensor("rope", [128, 2 * S], F32, kind="ExternalInput").ap()
        par_d = nc.dram_tensor("params", [128, depth * PL], F32, kind="ExternalInput").ap()
        win_d, wout_d = [], []
        for l in range(depth):
            nu = cfg.H_SB if l % 2 == 0 else 2 * cfg.H_DF
            win_d.append(nc.dram_tensor(f"win{l}", [nu, 128, UW], F32, kind="ExternalInput").ap())
            wout_d.append(nc.dram_tensor(f"wout{l}", [E, D], F32, kind="ExternalInput").ap())

        def sb(name, shape, dt):
            return es.enter_context(nc.sbuf_tensor(name, shape, dt))

        hT = sb("hT", [128, NC, S], F32)
        uT = sb("uT", [128, NC, S], BF16)
        vbuf = [sb(f"vbuf{i}", [128, NT, 256], BF16) for i in range(2)]
        qk = [sb(f"qk{i}", [128, S], BF16) for i in range(4)]
        gate = sb("gate", [128, 2, S], BF16)
        scr = sb("scr", [128, 4096], F32)
        etmp = sb("etmp", [128, 2, 512], F32)
        NSP = 4
        spr = sb("spr", [128, NSP, 512], BF16)
        wtmp = sb("wtmp", [128, 2, 512], F32)
        NATT = 5
        attn = sb("attn", [128, NATT, 512], BF16)
        cs_sb = sb("cs_sb", [128, 2, 512], BF16)
        NSLOT = 2
        wslot = [sb(f"wslot{i}", [128, UW], BF16) for i in range(NSLOT)]
        NOSLOT = 3
        woslot = [sb(f"woslot{i}", [128, D], BF16) for i in range(NOSLOT)]
        cbf = sb("cbf", [128, C_TOTAL], BF16)
        cf32 = sb("cf32", [128, 256], F32)
        par = sb("par", [128, depth * PL], F32)
        dpar = sb("dpar", [128, depth * 4 + 4], F32)
        o_sb = sb("o_sb", [128, 2, 512], F32)
        sacc = sb("sacc", [128, 2, 512], F32)

        NE = 7
        ering = scr[:, 0:NE * 512].rearrange("p (i n) -> p i n", n=512)
        scr_f = scr[:, :]
        print("[build] SBUF bytes/partition remaining:", nc.sbuf_bytes_remaining)

        banks = [es.enter_context(nc.psum_tensor(f"bank{i}", [128, 512], F32)) for i in range(8)]
        bankB = [Buf(f"bank{i}") for i in range(8)]

        kb = KB(nc, es)

        B_hT = [[Buf(f"hT{c}_{b}") for b in range(NB)] for c in range(NC)]
        B_uT = [[Buf(f"uT{c}_{b}") for b in range(NB)] for c in range(NC)]
        B_v = [[Buf(f"v{p}_{t}") for t in range(NT)] for p in range(2)]
        B_qk = [[Buf(f"qk{i}_{b}") for b in range(NB)] for i in range(4)]
        B_gate = [[Buf(f"gate{c}_{b}") for b in range(NB)] for c in range(2)]
        B_E = [Buf(f"E{i}") for i in range(NE)]
        B_scr1 = Buf("scr_rest")
        SCR_ALL = B_E + [B_scr1]
        B_sp = [Buf(f"sp{i}") for i in range(NSP)]
        B_etmp = [Buf("etmp0"), Buf("etmp1")]
        B_wt = [Buf("wt0"), Buf("wt1")]
        B_attn = [Buf(f"attn{i}") for i in range(NATT)]
        B_cs = [Buf("cs_sb0"), Buf("cs_sb1")]
        B_w = [Buf(f"w{i}") for i in range(NSLOT)]
        B_wo = [Buf(f"wo{i}") for i in range(NOSLOT)]
        B_c = Buf("consts")
        B_par = Buf("par")
        B_dpar = Buf("dpar")
        B_o = [Buf("o_sb0"), Buf("o_sb1")]
        B_sacc = [[Buf(f"sacc{p}_{h}") for h in range(2)] for p in range(2)]

        def cc(off, n=128):
            return cbf[:, off:off + n]

        ident = cc(C_IDENT)
        uincl = cc(C_UINCL)
        mneg_s = cc(C_MNEG_S)
        mneg_i = cc(C_MNEG_I)
        ones = cc(C_ONES)
        pswap = cc(C_PSWAP)
        ident_f = cf32[:, 0:128]
        ones_f = cf32[:, 128:256]

        cst = scr_f[:, 0:C_TOTAL]
        kb.dma("sp", cst, consts_d[:, :], "d_misc", writes=SCR_ALL)
        kb.dma("sp", par[:, :], par_d[:, :], "d_misc", writes=[B_par])
        kb.op("dve", lambda e: e.tensor_copy(out=cbf[:, :], in_=cst), reads=SCR_ALL, writes=[B_c])
        kb.op("dve", lambda e: e.tensor_copy(out=cf32[:, 0:128], in_=cst[:, C_IDENT:C_IDENT + 128]),
              reads=SCR_ALL, writes=[B_c])
        kb.op("dve", lambda e: e.tensor_copy(out=cf32[:, 128:256], in_=cst[:, C_ONES:C_ONES + 128]),
              reads=SCR_ALL, writes=[B_c])
        kb.op("dve", lambda e: e.memset(dpar[:, depth * 4:depth * 4 + 1], EPS), writes=[B_dpar])
        kb.op("dve", lambda e: e.memset(dpar[:, depth * 4 + 1:depth * 4 + 2], 1.0), writes=[B_dpar])
        EPS_AP = dpar[:, depth * 4:depth * 4 + 1]
        ONE_AP = dpar[:, depth * 4 + 1:depth * 4 + 2]

        for l in range(depth):
            if l % 2 == 0:
                continue
            pb = l * PL
            db = l * 4
            lam_init = 0.8 - 0.6 * math.exp(-0.3 * l)
            kb.op("dve", lambda e: e.tensor_scalar(out=dpar[:, db:db + 1], in0=par[:, pb + NC:pb + NC + 1],
                                                   scalar1=scale, scalar2=None, op0=ALU.mult),
                  reads=[B_par], writes=[B_dpar])
            kb.op("dve", lambda e: e.tensor_scalar(out=dpar[:, db + 1:db + 3], in0=par[:, pb + NC + 6:pb + NC + 8],
                                                   scalar1=(1.0 - lam_init), scalar2=None, op0=ALU.mult),
                  reads=[B_par], writes=[B_dpar])
            kb.op("dve", lambda e: e.tensor_tensor(out=etmp[:, 0, 0:1], in0=par[:, pb + NC + 2:pb + NC + 3],
                                                   in1=par[:, pb + NC + 3:pb + NC + 4], op=ALU.mult),
                  reads=[B_par], writes=[B_etmp[0]])
            kb.op("dve", lambda e: e.tensor_tensor(out=etmp[:, 0, 1:2], in0=par[:, pb + NC + 4:pb + NC + 5],
                                                   in1=par[:, pb + NC + 5:pb + NC + 6], op=ALU.mult),
                  reads=[B_par], writes=[B_etmp[0]])
            kb.op("pe", lambda e: e.matmul(banks[0][:, 0:2], lhsT=ones_f, rhs=etmp[:, 0, 0:2], start=True, stop=True),
                  reads=[B_c, B_etmp[0]], writes=[bankB[0]])
            kb.op("act", lambda e: e.activation(out=etmp[:, 1, 0:2], in_=banks[0][:, 0:2], func=AF.Exp),
                  reads=[bankB[0]], writes=[B_etmp[1]])
            kb.op("dve", lambda e: e.tensor_tensor(out=etmp[:, 1, 2:3], in0=etmp[:, 1, 1:2], in1=etmp[:, 1, 0:1],
                                                   op=ALU.subtract),
                  reads=[B_etmp[1]], writes=[B_etmp[1]])
            kb.op("dve", lambda e: e.tensor_scalar(out=dpar[:, db + 3:db + 4], in0=etmp[:, 1, 2:3],
                                                   scalar1=-lam_init, scalar2=None, op0=ALU.add),
                  reads=[B_etmp[1]], writes=[B_dpar])

        unit_list = []
        for l in range(depth):
            nu = cfg.H_SB if l % 2 == 0 else 2 * cfg.H_DF
            for u in range(nu):
                unit_list.append((l, u))
        unit_pos = {lu: i for i, lu in enumerate(unit_list)}
        wo_list = list(unit_list)
        wo_pos = {lu: i for i, lu in enumerate(wo_list)}
        state = {"w_next": 0, "wo_next": 0}

        def issue_w(upto):
            while state["w_next"] <= min(upto, len(unit_list) - 1):
                i = state["w_next"]
                l, u = unit_list[i]
                s = i % NSLOT
                nparts = 1
                pw = UW // nparts
                for k in range(nparts):
                    kb.dma("pool", wslot[s][:, k * pw:(k + 1) * pw], win_d[l][u, :, k * pw:(k + 1) * pw],
                           f"d_w{s}", writes=[B_w[s]])
                state["w_next"] += 1

        def issue_wo(upto):
            while state["wo_next"] <= min(upto, len(wo_list) - 1):
                i = state["wo_next"]
                l, u = wo_list[i]
                s = i % NOSLOT
                kb.dma("pool", woslot[s][:, :], wout_d[l][u * 128:(u + 1) * 128, :], f"d_wo{s}", writes=[B_wo[s]])
                state["wo_next"] += 1

        uT_f = uT[:, :, :].rearrange("p c s -> p (c s)").bitcast(F32)
        ALL_UT = [B_uT[c][b_] for c in range(NC) for b_ in range(NB)]
        half_ut = (NC * S // 2) // 2
        stage_bufs = [(scr_f, 4096, SCR_ALL)]
        if half_ut >= D:
            stage_bufs.append((uT_f[:, 0:half_ut], half_ut, ALL_UT))
            stage_bufs.append((uT_f[:, half_ut:2 * half_ut], half_ut, ALL_UT))
        tiles_per_stage = max(1, min(4096, half_ut if half_ut >= D else 4096) // D)
        tiles_per_stage = max(1, 4096 // D) if tiles_per_stage * D > 4096 else tiles_per_stage
        mmb = 0
        for si, t0 in enumerate(range(0, NT, tiles_per_stage)):
            nt = min(tiles_per_stage, NT - t0)
            sbuf_ap, scap, sbufs = stage_bufs[si % len(stage_bufs)]
            xv = sbuf_ap[:, 0:nt * D].rearrange("p (a d) -> p a d", a=nt)
            for a_ in range(nt):
                kb.dma("sp", xv[:, a_, :], x_d[(t0 + a_) * 128:(t0 + a_ + 1) * 128, :],
                       f"d_x{si % len(stage_bufs)}", writes=sbufs)
            for g0 in range(0, nt, 4):
                gn = min(4, nt - g0)
                tt0 = t0 + g0
                for c in range(NC):
                    bk = mmb % 4
                    mmb += 1
                    for k_ in range(gn):
                        kb.op("pe", lambda e: e.transpose(out=banks[bk][:, k_ * 128:(k_ + 1) * 128],
                                                          in_=xv[:, g0 + k_, c * 128:(c + 1) * 128], identity=ident_f),
                              reads=sbufs + [B_c], writes=[bankB[bk]], pe_acc=(k_ > 0))
                    kb.op("dve", lambda e: e.tensor_copy(out=hT[:, c, tt0 * 128:(tt0 + gn) * 128], in_=banks[bk][:, 0:gn * 128]),
                          reads=[bankB[bk]], writes=[B_hT[c][tt0 // 4]])

        def rmsnorm_to_uT(l):
            pb = l * PL
            for b in range(NB):
                cols = slice(b * 512, (b + 1) * 512)
                ssb = 2 + b % 2
                for c in range(NC):
                    ai = c % 3
                    kb.op("act", lambda e: e.activation(out=attn[:, ai, :], in_=hT[:, c, cols], func=AF.Square),
                          reads=[B_hT[c][b]], writes=[B_attn[ai]])
                    kb.op("pe", lambda e: e.matmul(banks[ssb][:, :], lhsT=ones, rhs=attn[:, ai, :],
                                                   start=(c == 0), stop=(c == NC - 1)),
                          reads=[B_attn[ai], B_c], writes=[bankB[ssb]], pe_acc=(c > 0))
                eb = b % 2
                kb.op("act", lambda e: e.activation(out=etmp[:, eb, :], in_=banks[ssb][:, :], func=AF.Ln,
                                                    scale=1.0 / D, bias=EPS_AP),
                      reads=[bankB[ssb], B_dpar], writes=[B_etmp[eb]])
                kb.op("act", lambda e: e.activation(out=etmp[:, eb, :], in_=etmp[:, eb, :], func=AF.Exp, scale=-0.5),
                      reads=[B_etmp[eb]], writes=[B_etmp[eb]])
                for c in range(NC):
                    kb.op("dve", lambda e: e.scalar_tensor_tensor(out=uT[:, c, cols], in0=hT[:, c, cols],
                                                                  scalar=par[:, pb + c:pb + c + 1], in1=etmp[:, eb, :],
                                                                  op0=ALU.mult, op1=ALU.mult),
                          reads=[B_hT[c][b], B_par, B_etmp[eb]], writes=[B_uT[c][b]])

        def proj_fm(slot, chunk, b, bank):
            for c in range(NC):
                woff = (c * 4 + chunk) * 128
                kb.op("pe", lambda e: e.matmul(banks[bank][:, :], lhsT=wslot[slot][:, woff:woff + 128],
                                               rhs=uT[:, c, b * 512:(b + 1) * 512],
                                               start=(c == 0), stop=(c == NC - 1)),
                      reads=[B_w[slot], B_uT[c][b]], writes=[bankB[bank]], pe_acc=(c > 0), inc=(c == NC - 1))

        def proj_v(slot, chunk0, width, vp, t0, ntile, bank, defer_fn=None, evac_eng="dve"):
            for k in range(ntile):
                tt = t0 + k
                for c in range(NC):
                    woff = (c * 4 + chunk0) * 128
                    kb.op("pe", lambda e: e.matmul(banks[bank][:, k * width:(k + 1) * width],
                                                   lhsT=uT[:, c, tt * 128:(tt + 1) * 128],
                                                   rhs=wslot[slot][:, woff:woff + width],
                                                   start=(c == 0), stop=(c == NC - 1)),
                          reads=[B_w[slot], B_uT[c][tt // 4]], writes=[bankB[bank]], pe_acc=(c > 0 or k > 0),
                          inc=(c == NC - 1 and k == ntile - 1))
            def ev():
                src_ = banks[bank][:, 0:ntile * width].rearrange("p (k w) -> p k w", w=width)
                dst_ = vbuf[vp][:, t0:t0 + ntile, 0:width]
                if evac_eng == "act":
                    kb.op("act", lambda e: e.copy(out=dst_, in_=src_),
                          reads=[bankB[bank]], writes=[B_v[vp][t0 + k] for k in range(ntile)])
                else:
                    kb.op("dve", lambda e: e.tensor_copy(out=dst_, in_=src_),
                          reads=[bankB[bank]], writes=[B_v[vp][t0 + k] for k in range(ntile)])
            if defer_fn is None:
                ev()
            else:
                defer_fn(bank, ev)

        def gate_from_bank(bank, gc, b, use_act=False, part="all"):
            cols = slice(b * 512, (b + 1) * 512)
            if part == "dve":
                kb.op("dve", lambda e: e.tensor_tensor(out=gate[:, gc, cols], in0=banks[bank][:, :], in1=etmp[:, 1, :],
                                                       op=ALU.mult),
                      reads=[bankB[bank], B_etmp[1]], writes=[B_gate[gc][b]])
                return
            kb.op("act", lambda e: e.activation(out=etmp[:, 1, :], in_=banks[bank][:, :], func=AF.Exp, scale=-1.0),
                  reads=[bankB[bank]], writes=[B_etmp[1]])
            if use_act:
                kb.op("act", lambda e: e.activation(out=etmp[:, 1, :], in_=etmp[:, 1, :], func=AF.Ln, bias=ONE_AP),
                      reads=[B_etmp[1], B_dpar], writes=[B_etmp[1]])
                kb.op("act", lambda e: e.activation(out=etmp[:, 1, :], in_=etmp[:, 1, :], func=AF.Exp, scale=-1.0),
                      reads=[B_etmp[1]], writes=[B_etmp[1]])
            else:
                kb.op("dve", lambda e: e.tensor_scalar(out=etmp[:, 1, :], in0=etmp[:, 1, :], scalar1=1.0, scalar2=None,
                                                       op0=ALU.add),
                      reads=[B_etmp[1]], writes=[B_etmp[1]])
                kb.op("dve", lambda e: e.reciprocal(out=etmp[:, 1, :], in_=etmp[:, 1, :]),
                      reads=[B_etmp[1]], writes=[B_etmp[1]])
            if part == "act":
                return
            kb.op("dve", lambda e: e.tensor_tensor(out=gate[:, gc, cols], in0=banks[bank][:, :], in1=etmp[:, 1, :],
                                                   op=ALU.mult),
                  reads=[bankB[bank], B_etmp[1]], writes=[B_gate[gc][b]])

        def out_proj_unit(l, wo_units, gcs, m, b, bank, defer_fn=None):
            n = len(wo_units)
            for j in range(n):
                s = wo_pos[(l, wo_units[j])] % NOSLOT
                kb.op("pe", lambda e: e.matmul(banks[bank][:, :], lhsT=woslot[s][:, m * 128:(m + 1) * 128],
                                               rhs=gate[:, gcs[j], b * 512:(b + 1) * 512],
                                               start=(j == 0), stop=(j == n - 1)),
                      reads=[B_wo[s], B_gate[gcs[j]][b]], writes=[bankB[bank]], pe_acc=(j > 0), inc=(j == n - 1))
            def ev():
                kb.op("dve", lambda e: e.tensor_tensor(out=hT[:, m, b * 512:(b + 1) * 512],
                                                       in0=hT[:, m, b * 512:(b + 1) * 512],
                                                       in1=banks[bank][:, :], op=ALU.add),
                      reads=[bankB[bank], B_hT[m][b]], writes=[B_hT[m][b]])
            if defer_fn is None:
                ev()
            else:
                defer_fn(bank, ev)

        def tile_geom(b, i):
            if i >= 4 * b:
                return 128 * (i - 4 * b), True
            return 0, False

        def sb_layer(l):
            rmsnorm_to_uT(l)
            BZ = [0, 1]
            B_ser = [Buf("ser0"), Buf("ser1"), Buf("ser2")]
            BC = [2, 3]
            BO = 5
            BMM = [4, 6, 7]
            H = cfg.H_SB
            mmrr = {"i": 0}

            bank_evac = {bk: None for bk in BMM}
            cur_step = {"s": -1}

            def flush_bank(bk):
                while bank_evac[bk] is not None:
                    f = bank_evac[bk][1]
                    bank_evac[bk] = None
                    f()

            def flush_old_evacs():
                for bk in BMM:
                    if bank_evac[bk] is not None and bank_evac[bk][0] < cur_step["s"]:
                        flush_bank(bk)

            def next_mm():
                mmrr["i"] += 1
                bk = BMM[mmrr["i"] % len(BMM)]
                flush_bank(bk)
                return bk

            def mm_available():
                bk = BMM[(mmrr["i"] + 1) % len(BMM)]
                return bank_evac[bk] is None

            def defer(bk, f, delay=1):
                bank_evac[bk] = (cur_step["s"] + delay - 1, f)

            issue_w(unit_pos[(l, 0)])
            issue_wo(wo_pos[(l, 0)])

            def proj_units(h):
                u = unit_pos[(l, h)]
                slot = u % NSLOT
                par_ = h % 2
                qs, ks = [], []
                for b in range(NB):
                    def fq(b=b):
                        cols = slice(b * 512, (b + 1) * 512)
                        bank = next_mm()
                        proj_fm(slot, 0, b, bank)
                        defer(bank, lambda: kb.op("dve", lambda e: e.tensor_scalar(
                            out=qk[par_][:, cols], in0=banks[bank][:, :], scalar1=scale, scalar2=None, op0=ALU.mult),
                            reads=[bankB[bank]], writes=[B_qk[par_][b]]), delay=2)
                    qs.append(fq)

                    def fk(b=b):
                        cols = slice(b * 512, (b + 1) * 512)
                        bank = next_mm()
                        proj_fm(slot, 1, b, bank)
                        defer(bank, lambda: kb.op("dve", lambda e: e.tensor_copy(out=qk[2 + par_][:, cols], in_=banks[bank][:, :]),
                                                  reads=[bankB[bank]], writes=[B_qk[2 + par_][b]]), delay=2)
                    ks.append(fk)
                gs, vs = [], []
                for b in range(NB):
                    def fg(b=b):
                        bank = next_mm()
                        proj_fm(slot, 2, b, bank)
                        def gA(bank=bank, b=b):
                            for obk in BMM:
                                if obk != bank and bank_evac[obk] is not None and getattr(bank_evac[obk][1], "is_gate_dve", False):
                                    flush_bank(obk)
                            gate_from_bank(bank, par_, b, use_act=True, part="act")
                            def gB():
                                gate_from_bank(bank, par_, b, use_act=True, part="dve")
                            gB.is_gate_dve = True
                            defer(bank, gB, delay=1)
                        defer(bank, gA, delay=2)
                    gs.append(fg)
                for t0 in range(0, NT, 4):
                    def fv(t0=t0):
                        bank = next_mm()
                        proj_v(slot, 3, 128, par_, t0, 4, bank, defer_fn=defer)
                    vs.append(fv)
                return qs, ks, gs, vs

            def outproj_units(h):
                us = []
                for b in range(NB):
                    for m in range(NC):
                        def fo(b=b, m=m):
                            bank = next_mm()
                            out_proj_unit(l, [h], [h % 2], m, b, bank, defer_fn=defer)
                        us.append(fo)
                return us

            qs, ks, gs, vs = proj_units(0)
            for f in qs + ks + gs + vs:
                f()
            for bk in BMM:
                flush_bank(bk)
            issue_w(unit_pos[(l, 0)] + 1)

            items = []
            for h in range(H):
                for b in range(NB):
                    n = 4 * b + 4
                    for i in range(n - 1, -1, -1):
                        off, diag = tile_geom(b, i)
                        items.append(dict(h=h, b=b, i=i, n=n, off=off, diag=diag))
            for j, it in enumerate(items):
                it["j"] = j
            nitems_head = len(items) // H
            flags = {"p4_done_head": -1, "y_done_head": -1}

            def P1(it):
                j, h, b, i, off, diag = it["j"], it["h"], it["b"], it["i"], it["off"], it["diag"]
                zb = BZ[j % 2]
                qb, kbuf = h % 2, 2 + h % 2
                qc = b * 512
                kb.op("pe", lambda e: e.matmul(banks[zb][:, off:512], lhsT=qk[kbuf][:, i * 128:(i + 1) * 128],
                                               rhs=qk[qb][:, qc + off:qc + 512], start=True, stop=(not diag)),
                      reads=[B_qk[kbuf][i // 4], B_qk[qb][b]], writes=[bankB[zb]], inc=(not diag))
                if diag:
                    kb.op("pe", lambda e: e.matmul(banks[zb][:, off:off + 128], lhsT=ident, rhs=mneg_s,
                                                   start=False, stop=True),
                          reads=[B_c], writes=[bankB[zb]], pe_acc=True)

            def A1(it):
                j, off = it["j"], it["off"]
                zb = BZ[j % 2]
                kb.op("act", lambda e: e.activation(out=ering[:, j % NE, off:512], in_=banks[zb][:, off:512], func=AF.Exp),
                      reads=[bankB[zb]], writes=[B_E[j % NE]])

            def A2(it):
                j, off = it["j"], it["off"]
                kb.op("act", lambda e: e.activation(out=spr[:, j % NSP, off:512], in_=ering[:, j % NE, off:512],
                                                    func=AF.Ln, bias=ONE_AP),
                      reads=[B_E[j % NE], B_dpar], writes=[B_sp[j % NSP]])

            def D1(it):
                j, i, n, off = it["j"], it["i"], it["n"], it["off"]
                if i == 0:
                    return
                cb = BC[j % 2]
                kb.op("dve", lambda e: e.tensor_copy(out=cs_sb[:, j % 2, off:512], in_=banks[cb][:, off:512]),
                      reads=[bankB[cb]], writes=[B_cs[j % 2], B_ser[j % 3]])

            def P3(it):
                j, i, n, off = it["j"], it["i"], it["n"], it["off"]
                cb = BC[j % 2]
                first = (i == n - 1)
                kb.op("pe", lambda e: e.matmul(banks[cb][:, off:512], lhsT=uincl, rhs=spr[:, j % NSP, off:512],
                                               start=True, stop=first, skip_group_check=True),
                      reads=[B_sp[j % NSP], B_c], writes=[bankB[cb]], inc=first)
                if not first:
                    poff = tile_geom(it["b"], i + 1)[0]
                    kb.op("pe", lambda e: e.matmul(banks[cb][:, poff:512], lhsT=cbf[:, C_BSEL:C_BSEL + 128],
                                                   rhs=cs_sb[:, (j - 1) % 2, poff:512], start=False, stop=True,
                                                   skip_group_check=True),
                          reads=[B_cs[(j - 1) % 2], B_c], writes=[bankB[cb]], pe_acc=True)

            def A3(it):
                j, off = it["j"], it["off"]
                cb = BC[j % 2]
                kb.op("act", lambda e: e.activation(out=wtmp[:, j % 2, off:512], in_=banks[cb][:, off:512],
                                                    func=AF.Exp, scale=-1.0),
                      reads=[bankB[cb], B_ser[j % 3]], writes=[B_wt[j % 2]])

            def D2(it):
                j, off = it["j"], it["off"]
                kb.op("pool", lambda e: e.tensor_tensor(out=attn[:, j % 4, off:512], in0=ering[:, j % NE, off:512],
                                                        in1=wtmp[:, j % 2, off:512], op=ALU.mult),
                      reads=[B_E[j % NE], B_wt[j % 2]], writes=[B_attn[j % 4]])

            def P4(it):
                j, h, b, i, n, off = it["j"], it["h"], it["b"], it["i"], it["n"], it["off"]
                vp = h % 2
                kb.op("pe", lambda e: e.matmul(banks[BO][:, off:512], lhsT=vbuf[vp][:, i, 0:128],
                                               rhs=attn[:, j % 4, off:512], start=(i == n - 1), stop=(i == 0),
                                               skip_group_check=True),
                      reads=[B_v[vp][i], B_attn[j % 4]], writes=[bankB[BO]], pe_acc=(i < n - 1))
                if i == 0:
                    cols = slice(b * 512, (b + 1) * 512)
                    gc = h % 2
                    kb.op("dve", lambda e: e.tensor_tensor(out=gate[:, gc, cols], in0=banks[BO][:, :], in1=gate[:, gc, cols],
                                                           op=ALU.mult),
                          reads=[bankB[BO], B_gate[gc][b]], writes=[B_gate[gc][b]])
                    if b == NB - 1:
                        flags["p4_done_head"] = h
                        flags["y_done_head"] = h

            stages = [(4, D1), (5, D2), (1, A1), (2, A2), (4, A3), (0, P1), (8, P4), (3, P3)]
            maxoff = 8
            nsteps = len(items) + maxoff

            bgq = []
            bg_head = {"h": -1}

            def enqueue_bg(h):
                if h + 1 < H:
                    qs, ks, gs, vs = proj_units(h + 1)
                else:
                    qs, ks, gs, vs = [], [], [], []
                ops = outproj_units(h - 1) if h - 1 >= 0 else []
                pre_y = (lambda hh=h - 1: flags["y_done_head"] >= hh)
                pre_v = (lambda hh=h - 1: flags["p4_done_head"] >= hh)
                lst = [(f, None, 8.0) for f in qs]
                heavy = [(f, None, 8.0) for f in ks] + [(f, pre_v, 8.0) for f in vs]
                opl = [(f, pre_y, 3.0) for f in ops]
                per = 4
                while heavy or opl:
                    if heavy:
                        lst.append(heavy.pop(0))
                    for _ in range(per):
                        if opl:
                            lst.append(opl.pop(0))
                lst += [(f, None, 8.0) for f in gs]
                return lst

            def run_bg(budget, force=False):
                spent = 0.0
                while bgq and (force or spent + 0.5 * bgq[0][2] <= budget):
                    if not force and not mm_available():
                        break
                    f, pre, cost = bgq[0]
                    if pre is not None and not pre():
                        if force:
                            raise RuntimeError("background precondition not met at forced drain")
                        break
                    bgq.pop(0)
                    f()
                    spent += cost
                return spent

            credit = 0.0
            rate = 0.0
            pend_w = {"step": -1, "idx": 0}
            for s in range(nsteps):
                if s < len(items):
                    h = items[s]["h"]
                    if h != bg_head["h"]:
                        run_bg(0, force=True)
                        for bk in BMM:
                            flush_bank(bk)
                        bg_head["h"] = h
                        bgq.extend(enqueue_bg(h))
                        pend_w["step"] = s + 8
                        pend_w["idx"] = unit_pos[(l, h)] + 2
                        issue_wo(wo_pos[(l, h)] + 1)
                        rate = sum(c for _, _, c in bgq) / float(nitems_head - 9)
                        credit = 0.0
                cur_step["s"] = s
                if pend_w["step"] == s:
                    issue_w(pend_w["idx"])
                for off_, fn in stages:
                    j = s - off_
                    if 0 <= j < len(items):
                        fn(items[j])
                    if fn is D1:
                        flush_old_evacs()
                credit += rate
                credit -= run_bg(credit)
                credit = min(credit, 3.0 * rate)
            run_bg(0, force=True)
            for bk in BMM:
                flush_bank(bk)
            issue_w(pend_w["idx"])
            for f in outproj_units(H - 1):
                f()
            for bk in BMM:
                flush_bank(bk)


        def df_layer(l):
            rmsnorm_to_uT(l)
            pb = l * PL
            db = l * 4
            BZ = [0, 1]
            SETS = [dict(O=[2, 3], SS=4), dict(O=[5, 6], SS=7)]
            H = cfg.H_DF
            kb.dma("sp", scr_f[:, 0:2 * S], rope_d[:, :], "d_misc", writes=SCR_ALL)
            rope3 = scr_f[:, 0:2 * S].rearrange("p (a s) -> p a s", a=2)
            issue_w(unit_pos[(l, 0)] + 1)
            gq_s = dpar[:, db:db + 1]
            gk = par[:, pb + NC + 1:pb + NC + 2]
            nlam = dpar[:, db + 3:db + 4]
            zc = {"i": 0}
            grp = {"i": 0}

            op_pending = []
            op_evac = {0: None, 1: None}
            op_rr = {"i": 0}

            def op_flush(bk):
                if op_evac[bk] is not None:
                    f = op_evac[bk]
                    op_evac[bk] = None
                    f()

            def make_op_units(hd):
                us = []
                for b_ in range(NB):
                    for m_ in range(NC):
                        def fo(b_=b_, m_=m_):
                            op_rr["i"] += 1
                            bk = op_rr["i"] % 2
                            op_flush(bk)
                            out_proj_unit(l, [2 * hd, 2 * hd + 1], [0, 1], m_, b_, bk,
                                          defer_fn=lambda bank, ev: op_evac.__setitem__(bank, ev))
                        us.append(fo)
                return us

            def run_ops(k, flush=True):
                if flush:
                    for bk in (0, 1):
                        op_flush(bk)
                for _ in range(min(k, 2)):
                    if op_pending:
                        op_pending.pop(0)()

            epi_carry = []
            for hd in range(H):
                vp = hd % 2
                uA = unit_pos[(l, 2 * hd)]
                uB = uA + 1
                sA, sB = uA % NSLOT, uB % NSLOT
                def make_v_units():
                    us = []
                    for t0_ in range(0, NT, 2):
                        def fv(t0_=t0_):
                            op_rr["i"] += 1
                            bk = op_rr["i"] % 2
                            op_flush(bk)
                            proj_v(sB, 2, 256, vp, t0_, 2, bk,
                                   defer_fn=lambda bank, ev: op_evac.__setitem__(bank, ev), evac_eng="act")
                        us.append(fv)
                    return us

                vus = make_v_units()
                merged = []
                while op_pending or vus:
                    for _ in range(4):
                        if op_pending:
                            merged.append(op_pending.pop(0))
                    if vus:
                        merged.append(vus.pop(0))
                op_pending.extend(merged)
                qitems = [(qi, b) for qi in range(4) for b in range(NB)]
                MMR = [4, 5, 6, 7]
                BSQ = 2
                BRB = 3

                def q_proj(j, qi, b):
                    proj_fm(sA, qi, b, MMR[j % 4])

                def q_sq(j, qi, b):
                    kb.op("act", lambda e: e.activation(out=attn[:, j % 2, :], in_=banks[MMR[j % 4]][:, :], func=AF.Square),
                          reads=[bankB[MMR[j % 4]]], writes=[B_attn[j % 2]])

                def q_ss(j, qi, b):
                    kb.op("pe", lambda e: e.matmul(banks[BSQ][:, :], lhsT=ones, rhs=attn[:, j % 2, :], start=True, stop=True),
                          reads=[B_attn[j % 2], B_c], writes=[bankB[BSQ]])

                def q_ln(j, qi, b):
                    kb.op("act", lambda e: e.activation(out=etmp[:, j % 2, :], in_=banks[BSQ][:, :], func=AF.Ln,
                                                        scale=1.0 / HEAD_DIM, bias=EPS_AP),
                          reads=[bankB[BSQ], B_dpar], writes=[B_etmp[j % 2]])
                    kb.op("act", lambda e: e.activation(out=etmp[:, j % 2, :], in_=etmp[:, j % 2, :], func=AF.Exp, scale=-0.5),
                          reads=[B_etmp[j % 2]], writes=[B_etmp[j % 2]])

                def q_qg(j, qi, b):
                    gcol = gq_s if qi < 2 else gk
                    kb.op("dve", lambda e: e.scalar_tensor_tensor(out=spr[:, j % NSP, :], in0=banks[MMR[j % 4]][:, :], scalar=gcol,
                                                                  in1=etmp[:, j % 2, :], op0=ALU.mult, op1=ALU.mult),
                          reads=[bankB[MMR[j % 4]], B_etmp[j % 2], B_dpar, B_par], writes=[B_sp[j % NSP]])

                def q_rot(j, qi, b):
                    kb.op("pe", lambda e: e.matmul(banks[BRB][:, :], lhsT=pswap, rhs=spr[:, j % NSP, :], start=True, stop=True),
                          reads=[B_sp[j % NSP], B_c], writes=[bankB[BRB]])

                def q_t(j, qi, b):
                    cols = slice(b * 512, (b + 1) * 512)
                    kb.op("pool", lambda e: e.tensor_tensor(out=wtmp[:, j % 2, :], in0=spr[:, j % NSP, :], in1=rope3[:, 0, cols],
                                                            op=ALU.mult),
                          reads=[B_sp[j % NSP]] + SCR_ALL, writes=[B_wt[j % 2]])
                    kb.op("dve", lambda e: e.tensor_tensor(out=o_sb[:, j % 2, :], in0=banks[BRB][:, :], in1=rope3[:, 1, cols],
                                                           op=ALU.mult),
                          reads=[bankB[BRB]] + SCR_ALL, writes=[B_o[j % 2]])

                def q_out(j, qi, b):
                    cols = slice(b * 512, (b + 1) * 512)
                    kb.op("pool", lambda e: e.tensor_tensor(out=qk[qi][:, cols], in0=o_sb[:, j % 2, :], in1=wtmp[:, j % 2, :],
                                                            op=ALU.add),
                          reads=[B_o[j % 2], B_wt[j % 2]], writes=[B_qk[qi][b]])

                qst = [(5, q_out), (4, q_rot), (4, q_t), (3, q_qg), (2, q_ss), (2, q_ln), (1, q_sq), (0, q_proj)]
                nq = len(qitems) + 5
                for s_ in range(nq):
                    if s_ >= 1 and epi_carry:
                        epi_carry.pop(0)()
                    for bk in (0, 1):
                        op_flush(bk)
                    for off_, fn in qst:
                        j = s_ - off_
                        if 0 <= j < len(qitems):
                            fn(j, *qitems[j])
                    run_ops(2 if len(op_pending) > (nq - s_) else 1, flush=False)
                issue_w(uA + 2)
                while op_pending:
                    run_ops(2)
                for bk in (0, 1):
                    op_flush(bk)
                issue_wo(wo_pos[(l, 2 * hd + 1)])
                pre_gates = [(gc, b) for b in range(NB) for gc in range(2) if 4 * b + 4 < 8]
                gate_units = [(gc, b) for b in range(NB) for gc in range(2) if 4 * b + 4 >= 8]
                k = 0
                for gc_, gb_ in pre_gates:
                    bank = 4 + k % 4
                    k += 1
                    proj_fm(sB, gc_, gb_, bank)
                    gate_from_bank(bank, gc_, gb_, use_act=True)
                if not gate_units:
                    issue_w(uB + 2)
                epi = []

                def make_epilogue(b, ssb):
                    cols = slice(b * 512, (b + 1) * 512)

                    def e1():
                        for c in range(2):
                            kb.op("pool", lambda e: e.tensor_tensor(out=spr[:, c, :], in0=o_sb[:, c, :], in1=o_sb[:, c, :],
                                                                    op=ALU.mult),
                                  reads=[B_o[c]], writes=[B_sp[c]])

                    def e1b():
                        for c in range(2):
                            kb.op("pe", lambda e: e.matmul(banks[ssb][:, :], lhsT=ones, rhs=spr[:, c, :], start=(c == 0), stop=(c == 1)),
                                  reads=[B_sp[c], B_c], writes=[bankB[ssb]], pe_acc=(c > 0))

                    def e2():
                        kb.op("act", lambda e: e.activation(out=etmp[:, 0, :], in_=banks[ssb][:, :], func=AF.Ln,
                                                            scale=1.0 / (2 * HEAD_DIM), bias=EPS_AP),
                              reads=[bankB[ssb], B_dpar], writes=[B_etmp[0]])
                        kb.op("act", lambda e: e.activation(out=etmp[:, 0, :], in_=etmp[:, 0, :], func=AF.Exp, scale=-0.5),
                              reads=[B_etmp[0]], writes=[B_etmp[0]])

                    def e3():
                        for c in range(2):
                            kb.op("dve", lambda e: e.scalar_tensor_tensor(out=wtmp[:, c, :], in0=o_sb[:, c, :],
                                                                          scalar=dpar[:, db + 1 + c:db + 2 + c], in1=etmp[:, 0, :],
                                                                          op0=ALU.mult, op1=ALU.mult),
                                  reads=[B_o[c], B_dpar, B_etmp[0]], writes=[B_wt[c]])
                            kb.op("pool", lambda e: e.tensor_tensor(out=gate[:, c, cols], in0=wtmp[:, c, :], in1=gate[:, c, cols],
                                                                    op=ALU.mult),
                                  reads=[B_wt[c], B_gate[c][b]], writes=[B_gate[c][b]])
                    return [e1, e1b, e2, e3]

                tiles = []
                for b in range(NB):
                    for m in range(2):
                        n = 4 * b + 4
                        for i in range(n):
                            tiles.append((b, m, i, n))
                BZ3 = [0, 1, 2]
                OSETS = [[3, 4], [5, 6]]
                BSS1 = 7

                def gidx(b, m):
                    return 2 * b + m

                def qk_mm(t):
                    b, m, i, n = tiles[t]
                    off, diag = tile_geom(b, i)
                    zb = BZ3[t % 3]
                    qcols = b * 512
                    kb.op("pe", lambda e: e.matmul(banks[zb][:, off:512], lhsT=qk[2 + m][:, i * 128:(i + 1) * 128],
                                                   rhs=qk[m][:, qcols + off:qcols + 512], start=True, stop=(not diag)),
                          reads=[B_qk[2 + m][i // 4], B_qk[m][b]], writes=[bankB[zb]], inc=(not diag))
                    if diag:
                        kb.op("pe", lambda e: e.matmul(banks[zb][:, off:off + 128], lhsT=ident, rhs=mneg_i,
                                                       start=False, stop=True),
                              reads=[B_c], writes=[bankB[zb]], pe_acc=True)

                def act_p(t):
                    b, m, i, n = tiles[t]
                    off, diag = tile_geom(b, i)
                    zb = BZ3[t % 3]
                    kb.op("act", lambda e: e.activation(out=attn[:, t % NATT, off:512], in_=banks[zb][:, off:512], func=AF.Exp),
                          reads=[bankB[zb]], writes=[B_attn[t % NATT]])

                def av_mm(t):
                    b, m, i, n = tiles[t]
                    off, diag = tile_geom(b, i)
                    BOa = OSETS[gidx(b, m) % 2]
                    for c in range(2):
                        kb.op("pe", lambda e: e.matmul(banks[BOa[c]][:, off:512],
                                                       lhsT=vbuf[vp][:, i, c * 128:(c + 1) * 128],
                                                       rhs=attn[:, t % NATT, off:512], start=(i == 0), stop=(i == n - 1)),
                              reads=[B_v[vp][i], B_attn[t % NATT]], writes=[bankB[BOa[c]]], pe_acc=(i > 0), inc=(c == 1))
                    p_ = gidx(b, m) % 2
                    for hf, eng_ in ((0, "dve"), (1, "pool")):
                        lo, hi = (max(off, 0), 384) if hf == 0 else (max(off, 384), 512)
                        if lo >= hi:
                            continue
                        if i == 0:
                            kb.op(eng_, lambda e: e.tensor_copy(out=sacc[:, p_, lo:hi], in_=attn[:, t % NATT, lo:hi]),
                                  reads=[B_attn[t % NATT]], writes=[B_sacc[p_][hf]])
                        else:
                            kb.op(eng_, lambda e: e.tensor_tensor(out=sacc[:, p_, lo:hi], in0=sacc[:, p_, lo:hi],
                                                                  in1=attn[:, t % NATT, lo:hi], op=ALU.add),
                                  reads=[B_attn[t % NATT], B_sacc[p_][hf]], writes=[B_sacc[p_][hf]])

                def make_evac(b, m):
                    BOa = OSETS[gidx(b, m) % 2]

                    def ev():
                        p_ = gidx(b, m) % 2
                        kb.op("dve", lambda e: e.tensor_copy(out=spr[:, 2, :], in_=sacc[:, p_, :]),
                              reads=B_sacc[p_], writes=[B_sp[2]])
                        kb.op("pe", lambda e: e.matmul(banks[BSS1][:, :], lhsT=ones, rhs=spr[:, 2, :], start=True, stop=True),
                              reads=[B_sp[2], B_c], writes=[bankB[BSS1]])
                        kb.op("act", lambda e: e.activation(out=etmp[:, 1, :], in_=banks[BSS1][:, :], func=AF.Ln),
                              reads=[bankB[BSS1]], writes=[B_etmp[1]])
                        kb.op("act", lambda e: e.activation(out=etmp[:, 1, :], in_=etmp[:, 1, :], func=AF.Exp, scale=-1.0),
                              reads=[B_etmp[1]], writes=[B_etmp[1]])
                        for c in range(2):
                            if m == 0:
                                kb.op("dve", lambda e: e.tensor_tensor(out=o_sb[:, c, :], in0=banks[BOa[c]][:, :],
                                                                       in1=etmp[:, 1, :], op=ALU.mult),
                                      reads=[bankB[BOa[c]], B_etmp[1]], writes=[B_o[c]])
                            else:
                                kb.op("dve", lambda e: e.tensor_tensor(out=wtmp[:, c, :], in0=banks[BOa[c]][:, :],
                                                                       in1=etmp[:, 1, :], op=ALU.mult),
                                      reads=[bankB[BOa[c]], B_etmp[1]], writes=[B_wt[c]])
                                kb.op("dve", lambda e: e.scalar_tensor_tensor(out=o_sb[:, c, :], in0=wtmp[:, c, :], scalar=nlam,
                                                                              in1=o_sb[:, c, :], op0=ALU.mult, op1=ALU.add),
                                      reads=[B_wt[c], B_o[c], B_dpar], writes=[B_o[c]])
                        if m == 1:
                            epi.extend(make_epilogue(b, BOa[0] if b < NB - 1 else 3))
                    return ev

                pend_evac = []
                gchain = []
                gtmp = cs_sb[:, :, :].rearrange("p a n -> p (a n)").bitcast(F32)

                def gate_chain(bank, gc, b_):
                    cols_ = slice(b_ * 512, (b_ + 1) * 512)

                    def g1():
                        kb.op("act", lambda e: e.activation(out=gtmp, in_=banks[bank][:, :], func=AF.Exp, scale=-1.0),
                              reads=[bankB[bank]], writes=B_cs)

                    def g2():
                        kb.op("act", lambda e: e.activation(out=gtmp, in_=gtmp, func=AF.Ln, bias=ONE_AP),
                              reads=B_cs + [B_dpar], writes=B_cs)

                    def g3():
                        kb.op("act", lambda e: e.activation(out=gtmp, in_=gtmp, func=AF.Exp, scale=-1.0),
                              reads=B_cs, writes=B_cs)
                        kb.op("dve", lambda e: e.tensor_tensor(out=gate[:, gc, cols_], in0=banks[bank][:, :], in1=gtmp,
                                                               op=ALU.mult),
                              reads=[bankB[bank]] + B_cs, writes=[B_gate[gc][b_]])
                    return [g1, g2, g3]

                epi_age = {"n": 0}
                NTL = len(tiles)
                qk_mm(0)
                if NTL > 1:
                    qk_mm(1)
                for t in range(NTL):
                    b, m, i, n = tiles[t]
                    if t + 2 < NTL:
                        qk_mm(t + 2)
                    act_p(t)
                    if pend_evac and (i >= 2 or i == n - 1):
                        while pend_evac:
                            pend_evac.pop(0)()
                    av_mm(t)
                    if epi:
                        epi_age["n"] += 1
                        if epi_age["n"] >= 3:
                            epi.pop(0)()
                    else:
                        epi_age["n"] = 0
                    if gchain:
                        gchain.pop(0)()
                        if i == n - 1:
                            while gchain:
                                gchain.pop(0)()
                    elif gate_units and n >= 8 and (i == 6 or (n >= 16 and i == 10)):
                        gc_, gb_ = gate_units.pop(0)
                        gbank = OSETS[(gidx(b, m) + 1) % 2][1]
                        proj_fm(sB, gc_, gb_, gbank)
                        gchain.extend(gate_chain(gbank, gc_, gb_))
                        if not gate_units:
                            issue_w(uB + 2)
                    if i == n - 1:
                        pend_evac.append(make_evac(b, m))
                while pend_evac:
                    pend_evac.pop(0)()
                while gchain:
                    gchain.pop(0)()
                assert not gate_units
                assert len(epi) == 4 and not epi_carry
                epi.pop(0)()
                if hd + 1 < H:
                    e1b_, e2_, e3_ = epi
                    epi_carry.extend([lambda e1b_=e1b_, e2_=e2_: (e1b_(), e2_()), e3_])
                else:
                    while epi:
                        epi.pop(0)()
                del epi[:]
                op_pending.extend(make_op_units(hd))
            while op_pending:
                run_ops(2)
            for bk in (0, 1):
                op_flush(bk)

        for l in range(depth):
            if l % 2 == 0:
                sb_layer(l)
            else:
                df_layer(l)

        mmb = 0
        for si, t0 in enumerate(range(0, NT, tiles_per_stage)):
            nt = min(tiles_per_stage, NT - t0)
            sbuf_ap, scap, sbufs = stage_bufs[si % len(stage_bufs)]
            xv = sbuf_ap[:, 0:nt * D].rearrange("p (a d) -> p a d", a=nt)
            for a_ in range(nt):
                tt = t0 + a_
                b_ = tt // 4
                for c0 in range(0, NC, 4):
                    cn = min(4, NC - c0)
                    bk = 4 + mmb % 4
                    mmb += 1
                    for k_ in range(cn):
                        c = c0 + k_
                        kb.op("pe", lambda e: e.transpose(out=banks[bk][:, k_ * 128:(k_ + 1) * 128],
                                                          in_=hT[:, c, tt * 128:(tt + 1) * 128], identity=ident_f),
                              reads=[B_hT[c][b_], B_c], writes=[bankB[bk]], pe_acc=(k_ > 0))
                    kb.op("dve", lambda e: e.tensor_copy(out=xv[:, a_, c0 * 128:(c0 + cn) * 128], in_=banks[bk][:, 0:cn * 128]),
                          reads=[bankB[bk]], writes=sbufs)
            for a_ in range(nt):
                kb.dma("sp", out_d[(t0 + a_) * 128:(t0 + a_ + 1) * 128, :], xv[:, a_, :],
                       f"d_out{si % len(stage_bufs)}", reads=sbufs)
        kb.wait_all("sp", [f"d_out{i}" for i in range(len(stage_bufs))])
        print(f"[build] instructions={kb.n_ins} waits={kb.n_wait}")
    return nc


def prep_weights(cfg, inputs):
    D, E, NC = cfg.D, cfg.E, cfg.NC
    out = {}
    P = np.zeros((128, cfg.depth * cfg.PL), np.float32)
    for l in range(cfg.depth):
        j = l // 2
        pb = l * cfg.PL
        if l % 2 == 0:
            W = np.asarray(inputs["sb_w_in"][j], np.float32)
            H = cfg.H_SB
            Wh = W.reshape(NC, 128, 4, H, 128)
            sel = Wh[:, :, [0, 1, 3, 2]]
            win = np.ascontiguousarray(sel.transpose(3, 1, 0, 2, 4)).reshape(H, 128, NC * 4 * 128)
            out[f"win{l}"] = win
            out[f"wout{l}"] = np.ascontiguousarray(np.asarray(inputs["sb_w_out"][j], np.float32))
            g = np.asarray(inputs["sb_norm"][j], np.float32)
            P[:, pb:pb + NC] = g.reshape(NC, 128).T
        else:
            W = np.asarray(inputs["df_w_in"][j], np.float32)
            H = cfg.H_DF
            Wr = W.reshape(NC, 128, 4, H, 2, 128)
            chunks = np.stack([Wr[:, :, 0, :, 0], Wr[:, :, 0, :, 1], Wr[:, :, 1, :, 0], Wr[:, :, 1, :, 1],
                               Wr[:, :, 3, :, 0], Wr[:, :, 3, :, 1], Wr[:, :, 2, :, 0], Wr[:, :, 2, :, 1]], axis=0)
            chunks = chunks.reshape(2, 4, NC, 128, H, 128)
            win = np.ascontiguousarray(chunks.transpose(4, 0, 3, 2, 1, 5)).reshape(H * 2, 128, NC * 4 * 128)
            out[f"win{l}"] = win
            out[f"wout{l}"] = np.ascontiguousarray(np.asarray(inputs["df_w_out"][j], np.float32))
            g = np.asarray(inputs["df_norm"][j], np.float32)
            P[:, pb:pb + NC] = g.reshape(NC, 128).T
            P[:, pb + NC] = np.asarray(inputs["df_q_norm"][j], np.float32)
            P[:, pb + NC + 1] = np.asarray(inputs["df_k_norm"][j], np.float32)
            P[:, pb + NC + 2] = np.asarray(inputs["df_lam_q1"][j], np.float32)
            P[:, pb + NC + 3] = np.asarray(inputs["df_lam_k1"][j], np.float32)
            P[:, pb + NC + 4] = np.asarray(inputs["df_lam_q2"][j], np.float32)
            P[:, pb + NC + 5] = np.asarray(inputs["df_lam_k2"][j], np.float32)
            gs = np.asarray(inputs["df_sub_norm"][j], np.float32)
            P[:, pb + NC + 6] = gs[:128]
            P[:, pb + NC + 7] = gs[128:]
    out["params"] = P
    out["consts"] = make_consts()
    out["rope"] = make_rope(cfg.S)
    return out


_PROG_CACHE = {}


def run(cfg, inputs, n_cores, trace=False):
    key = (cfg.S, cfg.D, cfg.depth)
    if key not in _PROG_CACHE:
        _PROG_CACHE[key] = build_program(cfg)
    nc = _PROG_CACHE[key]
    shared = prep_weights(cfg, inputs)
    x = np.asarray(inputs["x"], np.float32)
    in_maps = []
    for b in range(n_cores):
        m = dict(shared)
        m["x"] = np.ascontiguousarray(x[b])
        in_maps.append(m)
    res = run_bass_kernel_spmd(nc, in_maps, core_ids=list(range(n_cores)), trace=trace)
    out = np.stack([np.asarray(r["out"], np.float32) for r in res.results], axis=0)
    return out, res


def kernel(x, sb_norm, sb_w_in, sb_w_out, df_norm, df_w_in, df_w_out,
           df_q_norm, df_k_norm, df_lam_q1, df_lam_k1, df_lam_q2, df_lam_k2, df_sub_norm):
    inputs = dict(x=x, sb_norm=sb_norm, sb_w_in=sb_w_in, sb_w_out=sb_w_out, df_norm=df_norm,
                  df_w_in=df_w_in, df_w_out=df_w_out, df_q_norm=df_q_norm, df_k_norm=df_k_norm,
                  df_lam_q1=df_lam_q1, df_lam_k1=df_lam_k1, df_lam_q2=df_lam_q2, df_lam_k2=df_lam_k2,
                  df_sub_norm=df_sub_norm)
    cfg = Cfg(S=2048, D=1024, depth=4)
    out, _ = run(cfg, inputs, 8)
    return out.astype(np.float32)
```

```python
import math
from contextlib import ExitStack

import numpy as np
import concourse.bass as bass
import concourse.mybir as mybir
from concourse.bass_utils import run_bass_kernel_spmd

F32 = mybir.dt.float32
BF16 = mybir.dt.bfloat16
AF = mybir.ActivationFunctionType
ALU = mybir.AluOpType

HEAD_DIM = 128
EPS = 1e-6
ROPE_THETA = 10000.0
MASKV = 30000.0


class Cfg:
    def __init__(self, S=2048, D=1024, depth=4):
        self.S, self.D, self.depth = S, D, depth
        self.E = 2 * D
        self.NC = D // 128
        self.NB = S // 512
        self.NT = S // 128
        self.H_SB = self.E // 128
        self.H_DF = self.E // 256
        self.PL = self.NC + 8
        self.UW = self.NC * 4 * 128


class Buf:
    __slots__ = ("name", "w", "r")

    def __init__(self, name):
        self.name = name
        self.w = None
        self.r = {}


class KB:
    def __init__(self, nc, es):
        self.nc = nc
        self.es = es
        self.eng = {"pe": nc.tensor, "act": nc.scalar, "dve": nc.vector,
                    "pool": nc.gpsimd, "sp": nc.sync}
        self.sems = {}
        self.cnt = {}
        self.seen = {e: {} for e in self.eng}
        for e in self.eng:
            self._sem(e)
        self.n_wait = 0
        self.n_ins = 0

    def _sem(self, key):
        if key not in self.sems:
            self.sems[key] = self.es.enter_context(self.nc.semaphore("s_" + key))
            self.cnt[key] = 0
        return self.sems[key]

    def _waits(self, eng, reads, writes, pe_acc):
        deps = {}

        def add(ev):
            if ev is None:
                return
            k, v = ev
            if deps.get(k, 0) < v:
                deps[k] = v

        for b in reads:
            add(b.w)
        for b in writes:
            if not (pe_acc and eng == "pe" and b.w is not None and b.w[0] == "pe"):
                add(b.w)
            for k, v in b.r.items():
                add((k, v))
        seen = self.seen[eng]
        for k, v in deps.items():
            if k.startswith("d_"):
                v = self.cnt[k]
            if seen.get(k, 0) < v:
                self.eng[eng].wait_ge(self.sems[k], v)
                seen[k] = v
                self.n_wait += 1

    def _record(self, ev, reads, writes):
        k, v = ev
        for b in reads:
            if b.r.get(k, 0) < v:
                b.r[k] = v
        for b in writes:
            b.w = ev
            b.r = {}

    def op(self, eng, fn, reads=(), writes=(), pe_acc=False, inc=True):
        self._waits(eng, reads, writes, pe_acc)
        ins = fn(self.eng[eng])
        self.n_ins += 1
        if inc:
            self.cnt[eng] += 1
            ins.then_inc(self.sems[eng], 1)
            ev = (eng, self.cnt[eng])
        else:
            ev = (eng, self.cnt[eng] + 1)
        self._record(ev, reads, writes)
        return ev

    def dma(self, q, out, in_, semkey, reads=(), writes=(), **kw):
        self._sem(semkey)
        self._waits(q, reads, writes, False)
        ins = self.eng[q].dma_start(out=out, in_=in_, **kw)
        self.n_ins += 1
        self.cnt[semkey] += 16
        ins.then_inc(self.sems[semkey], 16)
        ev = (semkey, self.cnt[semkey])
        self._record(ev, reads, writes)
        return ev

    def wait_all(self, eng, keys):
        for k in keys:
            v = self.cnt.get(k, 0)
            if v > 0 and self.seen[eng].get(k, 0) < v:
                self.eng[eng].wait_ge(self.sems[k], v)
                self.seen[eng][k] = v


C_IDENT, C_UINCL, C_MNEG_S, C_MPOS_S, C_MNEG_I, C_ONES, C_PSWAP, C_EBIG = (
    0, 128, 256, 384, 512, 640, 768, 896)
C_BSEL = 896 + 144
C_TOTAL = C_BSEL + 128


def make_consts():
    c = np.zeros((128, C_TOTAL), np.float32)
    p = np.arange(128)[:, None]
    f = np.arange(128)[None, :]
    c[:, C_IDENT:C_IDENT + 128] = (p == f)
    c[:, C_UINCL:C_UINCL + 128] = (p >= f)
    c[:, C_MNEG_S:C_MNEG_S + 128] = np.where(p >= f, -MASKV, 0.0)
    c[:, C_MPOS_S:C_MPOS_S + 128] = np.where(p >= f, MASKV, 0.0)
    c[:, C_MNEG_I:C_MNEG_I + 128] = np.where(p > f, -MASKV, 0.0)
    c[:, C_ONES:C_ONES + 128] = 1.0
    c[:, C_PSWAP:C_PSWAP + 128] = (p == (f + 64) % 128)
    c[:, C_EBIG + 15] = 1.0
    c[0, C_BSEL:C_BSEL + 128] = 1.0
    return c


def make_rope(S):
    inv = (1.0 / (ROPE_THETA ** (np.arange(0, HEAD_DIM, 2, dtype=np.float32) / np.float32(HEAD_DIM)))).astype(np.float32)
    ang = (np.arange(S, dtype=np.float32)[:, None] * inv[None, :]).astype(np.float32)
    cos = np.cos(ang).astype(np.float32).T
    sin = np.sin(ang).astype(np.float32).T
    out = np.zeros((128, 2, S), np.float32)
    out[:64, 0] = cos
    out[64:, 0] = cos
    out[:64, 1] = -sin
    out[64:, 1] = sin
    return out.reshape(128, 2 * S)


def build_program(cfg):
    S, D, E, NC, NB, NT = cfg.S, cfg.D, cfg.E, cfg.NC, cfg.NB, cfg.NT
    depth, PL, UW = cfg.depth, cfg.PL, cfg.UW
    scale = 1.0 / math.sqrt(HEAD_DIM)

    nc = bass.Bass("TRN2", target_bir_lowering=False)
    es = ExitStack()
    with es:
        x_d = nc.dram_tensor("x", [S, D], F32, kind="ExternalInput").ap()
        out_d = nc.dram_tensor("out", [S, D], F32, kind="ExternalOutput").ap()
        consts_d = nc.dram_tensor("consts", [128, C_TOTAL], F32, kind="ExternalInput").ap()
        rope_d = nc.dram_tensor("rope", [128, 2 * S], F32, kind="ExternalInput").ap()
        par_d = nc.dram_tensor("params", [128, depth * PL], F32, kind="ExternalInput").ap()
        win_d, wout_d = [], []
        for l in range(depth):
            nu = cfg.H_SB if l % 2 == 0 else 2 * cfg.H_DF
            win_d.append(nc.dram_tensor(f"win{l}", [nu, 128, UW], F32, kind="ExternalInput").ap())
            wout_d.append(nc.dram_tensor(f"wout{l}", [E, D], F32, kind="ExternalInput").ap())

        def sb(name, shape, dt):
            return es.enter_context(nc.sbuf_tensor(name, shape, dt))

        hT = sb("hT", [128, NC, S], F32)
        uT = sb("uT", [128, NC, S], BF16)
        vbuf = [sb(f"vbuf{i}", [128, NT, 256], BF16) for i in range(2)]
        qk = [sb(f"qk{i}", [128, S], BF16) for i in range(4)]
        gate = sb("gate", [128, 2, S], BF16)
        scr = sb("scr", [128, 4096], F32)
        etmp = sb("etmp", [128, 2, 512], F32)
        NSP = 4
        spr = sb("spr", [128, NSP, 512], BF16)
        wtmp = sb("wtmp", [128, 2, 512], F32)
        NATT = 5
        attn = sb("attn", [128, NATT, 512], BF16)
        cs_sb = sb("cs_sb", [128, 2, 512], BF16)
        NSLOT = 2
        wslot = [sb(f"wslot{i}", [128, UW], BF16) for i in range(NSLOT)]
        NOSLOT = 3
        woslot = [sb(f"woslot{i}", [128, D], BF16) for i in range(NOSLOT)]
        cbf = sb("cbf", [128, C_TOTAL], BF16)
        cf32 = sb("cf32", [128, 256], F32)
        par = sb("par", [128, depth * PL], F32)
        dpar = sb("dpar", [128, depth * 4 + 4], F32)
        o_sb = sb("o_sb", [128, 2, 512], F32)
        sacc = sb("sacc", [128, 2, 512], F32)

        NE = 7
        ering = scr[:, 0:NE * 512].rearrange("p (i n) -> p i n", n=512)
        scr_f = scr[:, :]
        print("[build] SBUF bytes/partition remaining:", nc.sbuf_bytes_remaining)

        banks = [es.enter_context(nc.psum_tensor(f"bank{i}", [128, 512], F32)) for i in range(8)]
        bankB = [Buf(f"bank{i}") for i in range(8)]

        kb = KB(nc, es)

        B_hT = [[Buf(f"hT{c}_{b}") for b in range(NB)] for c in range(NC)]
        B_uT = [[Buf(f"uT{c}_{b}") for b in range(NB)] for c in range(NC)]
        B_v = [[Buf(f"v{p}_{t}") for t in range(NT)] for p in range(2)]
        B_qk = [[Buf(f"qk{i}_{b}") for b in range(NB)] for i in range(4)]
        B_gate = [[Buf(f"gate{c}_{b}") for b in range(NB)] for c in range(2)]
        B_E = [Buf(f"E{i}") for i in range(NE)]
        B_scr1 = Buf("scr_rest")
        SCR_ALL = B_E + [B_scr1]
        B_sp = [Buf(f"sp{i}") for i in range(NSP)]
        B_etmp = [Buf("etmp0"), Buf("etmp1")]
        B_wt = [Buf("wt0"), Buf("wt1")]
        B_attn = [Buf(f"attn{i}") for i in range(NATT)]
        B_cs = [Buf("cs_sb0"), Buf("cs_sb1")]
        B_w = [Buf(f"w{i}") for i in range(NSLOT)]
        B_wo = [Buf(f"wo{i}") for i in range(NOSLOT)]
        B_c = Buf("consts")
        B_par = Buf("par")
        B_dpar = Buf("dpar")
        B_o = [Buf("o_sb0"), Buf("o_sb1")]
        B_sacc = [[Buf(f"sacc{p}_{h}") for h in range(2)] for p in range(2)]

        def cc(off, n=128):
            return cbf[:, off:off + n]

        ident = cc(C_IDENT)
        uincl = cc(C_UINCL)
        mneg_s = cc(C_MNEG_S)
        mneg_i = cc(C_MNEG_I)
        ones = cc(C_ONES)
        pswap = cc(C_PSWAP)
        ident_f = cf32[:, 0:128]
        ones_f = cf32[:, 128:256]

        cst = scr_f[:, 0:C_TOTAL]
        kb.dma("sp", cst, consts_d[:, :], "d_misc", writes=SCR_ALL)
        kb.dma("sp", par[:, :], par_d[:, :], "d_misc", writes=[B_par])
        kb.op("dve", lambda e: e.tensor_copy(out=cbf[:, :], in_=cst), reads=SCR_ALL, writes=[B_c])
        kb.op("dve", lambda e: e.tensor_copy(out=cf32[:, 0:128], in_=cst[:, C_IDENT:C_IDENT + 128]),
              reads=SCR_ALL, writes=[B_c])
        kb.op("dve", lambda e: e.tensor_copy(out=cf32[:, 128:256], in_=cst[:, C_ONES:C_ONES + 128]),
              reads=SCR_ALL, writes=[B_c])
        kb.op("dve", lambda e: e.memset(dpar[:, depth * 4:depth * 4 + 1], EPS), writes=[B_dpar])
        kb.op("dve", lambda e: e.memset(dpar[:, depth * 4 + 1:depth * 4 + 2], 1.0), writes=[B_dpar])
        EPS_AP = dpar[:, depth * 4:depth * 4 + 1]
        ONE_AP = dpar[:, depth * 4 + 1:depth * 4 + 2]

        for l in range(depth):
            if l % 2 == 0:
                continue
            pb = l * PL
            db = l * 4
            lam_init = 0.8 - 0.6 * math.exp(-0.3 * l)
            kb.op("dve", lambda e: e.tensor_scalar(out=dpar[:, db:db + 1], in0=par[:, pb + NC:pb + NC + 1],
                                                   scalar1=scale, scalar2=None, op0=ALU.mult),
                  reads=[B_par], writes=[B_dpar])
            kb.op("dve", lambda e: e.tensor_scalar(out=dpar[:, db + 1:db + 3], in0=par[:, pb + NC + 6:pb + NC + 8],
                                                   scalar1=(1.0 - lam_init), scalar2=None, op0=ALU.mult),
                  reads=[B_par], writes=[B_dpar])
            kb.op("dve", lambda e: e.tensor_tensor(out=etmp[:, 0, 0:1], in0=par[:, pb + NC + 2:pb + NC + 3],
                                                   in1=par[:, pb + NC + 3:pb + NC + 4], op=ALU.mult),
                  reads=[B_par], writes=[B_etmp[0]])
            kb.op("dve", lambda e: e.tensor_tensor(out=etmp[:, 0, 1:2], in0=par[:, pb + NC + 4:pb + NC + 5],
                                                   in1=par[:, pb + NC + 5:pb + NC + 6], op=ALU.mult),
                  reads=[B_par], writes=[B_etmp[0]])
            kb.op("pe", lambda e: e.matmul(banks[0][:, 0:2], lhsT=ones_f, rhs=etmp[:, 0, 0:2], start=True, stop=True),
                  reads=[B_c, B_etmp[0]], writes=[bankB[0]])
            kb.op("act", lambda e: e.activation(out=etmp[:, 1, 0:2], in_=banks[0][:, 0:2], func=AF.Exp),
                  reads=[bankB[0]], writes=[B_etmp[1]])
            kb.op("dve", lambda e: e.tensor_tensor(out=etmp[:, 1, 2:3], in0=etmp[:, 1, 1:2], in1=etmp[:, 1, 0:1],
                                                   op=ALU.subtract),
                  reads=[B_etmp[1]], writes=[B_etmp[1]])
            kb.op("dve", lambda e: e.tensor_scalar(out=dpar[:, db + 3:db + 4], in0=etmp[:, 1, 2:3],
                                                   scalar1=-lam_init, scalar2=None, op0=ALU.add),
                  reads=[B_etmp[1]], writes=[B_dpar])

        unit_list = []
        for l in range(depth):
            nu = cfg.H_SB if l % 2 == 0 else 2 * cfg.H_DF
            for u in range(nu):
                unit_list.append((l, u))
        unit_pos = {lu: i for i, lu in enumerate(unit_list)}
        wo_list = list(unit_list)
        wo_pos = {lu: i for i, lu in enumerate(wo_list)}
        state = {"w_next": 0, "wo_next": 0}

        def issue_w(upto):
            while state["w_next"] <= min(upto, len(unit_list) - 1):
                i = state["w_next"]
                l, u = unit_list[i]
                s = i % NSLOT
                nparts = 1
                pw = UW // nparts
                for k in range(nparts):
                    kb.dma("pool", wslot[s][:, k * pw:(k + 1) * pw], win_d[l][u, :, k * pw:(k + 1) * pw],
                           f"d_w{s}", writes=[B_w[s]])
                state["w_next"] += 1

        def issue_wo(upto):
            while state["wo_next"] <= min(upto, len(wo_list) - 1):
                i = state["wo_next"]
                l, u = wo_list[i]
                s = i % NOSLOT
                kb.dma("pool", woslot[s][:, :], wout_d[l][u * 128:(u + 1) * 128, :], f"d_wo{s}", writes=[B_wo[s]])
                state["wo_next"] += 1

        uT_f = uT[:, :, :].rearrange("p c s -> p (c s)").bitcast(F32)
        ALL_UT = [B_uT[c][b_] for c in range(NC) for b_ in range(NB)]
        half_ut = (NC * S // 2) // 2
        stage_bufs = [(scr_f, 4096, SCR_ALL)]
        if half_ut >= D:
            stage_bufs.append((uT_f[:, 0:half_ut], half_ut, ALL_UT))
            stage_bufs.append((uT_f[:, half_ut:2 * half_ut], half_ut, ALL_UT))
        tiles_per_stage = max(1, min(4096, half_ut if half_ut >= D else 4096) // D)
        tiles_per_stage = max(1, 4096 // D) if tiles_per_stage * D > 4096 else tiles_per_stage
        mmb = 0
        for si, t0 in enumerate(range(0, NT, tiles_per_stage)):
            nt = min(tiles_per_stage, NT - t0)
            sbuf_ap, scap, sbufs = stage_bufs[si % len(stage_bufs)]
            xv = sbuf_ap[:, 0:nt * D].rearrange("p (a d) -> p a d", a=nt)
            for a_ in range(nt):
                kb.dma("sp", xv[:, a_, :], x_d[(t0 + a_) * 128:(t0 + a_ + 1) * 128, :],
                       f"d_x{si % len(stage_bufs)}", writes=sbufs)
            for g0 in range(0, nt, 4):
                gn = min(4, nt - g0)
                tt0 = t0 + g0
                for c in range(NC):
                    bk = mmb % 4
                    mmb += 1
                    for k_ in range(gn):
                        kb.op("pe", lambda e: e.transpose(out=banks[bk][:, k_ * 128:(k_ + 1) * 128],
                                                          in_=xv[:, g0 + k_, c * 128:(c + 1) * 128], identity=ident_f),
                              reads=sbufs + [B_c], writes=[bankB[bk]], pe_acc=(k_ > 0))
                    kb.op("dve", lambda e: e.tensor_copy(out=hT[:, c, tt0 * 128:(tt0 + gn) * 128], in_=banks[bk][:, 0:gn * 128]),
                          reads=[bankB[bk]], writes=[B_hT[c][tt0 // 4]])

        def rmsnorm_to_uT(l):
            pb = l * PL
            for b in range(NB):
                cols = slice(b * 512, (b + 1) * 512)
                ssb = 2 + b % 2
                for c in range(NC):
                    ai = c % 3
                    kb.op("act", lambda e: e.activation(out=attn[:, ai, :], in_=hT[:, c, cols], func=AF.Square),
                          reads=[B_hT[c][b]], writes=[B_attn[ai]])
                    kb.op("pe", lambda e: e.matmul(banks[ssb][:, :], lhsT=ones, rhs=attn[:, ai, :],
                                                   start=(c == 0), stop=(c == NC - 1)),
                          reads=[B_attn[ai], B_c], writes=[bankB[ssb]], pe_acc=(c > 0))
                eb = b % 2
                kb.op("act", lambda e: e.activation(out=etmp[:, eb, :], in_=banks[ssb][:, :], func=AF.Ln,
                                                    scale=1.0 / D, bias=EPS_AP),
                      reads=[bankB[ssb], B_dpar], writes=[B_etmp[eb]])
                kb.op("act", lambda e: e.activation(out=etmp[:, eb, :], in_=etmp[:, eb, :], func=AF.Exp, scale=-0.5),
                      reads=[B_etmp[eb]], writes=[B_etmp[eb]])
                for c in range(NC):
                    kb.op("dve", lambda e: e.scalar_tensor_tensor(out=uT[:, c, cols], in0=hT[:, c, cols],
                                                                  scalar=par[:, pb + c:pb + c + 1], in1=etmp[:, eb, :],
                                                                  op0=ALU.mult, op1=ALU.mult),
                          reads=[B_hT[c][b], B_par, B_etmp[eb]], writes=[B_uT[c][b]])

        def proj_fm(slot, chunk, b, bank):
            for c in range(NC):
                woff = (c * 4 + chunk) * 128
                kb.op("pe", lambda e: e.matmul(banks[bank][:, :], lhsT=wslot[slot][:, woff:woff + 128],
                                               rhs=uT[:, c, b * 512:(b + 1) * 512],
                                               start=(c == 0), stop=(c == NC - 1)),
                      reads=[B_w[slot], B_uT[c][b]], writes=[bankB[bank]], pe_acc=(c > 0), inc=(c == NC - 1))

        def proj_v(slot, chunk0, width, vp, t0, ntile, bank, defer_fn=None, evac_eng="dve"):
            for k in range(ntile):
                tt = t0 + k
                for c in range(NC):
                    woff = (c * 4 + chunk0) * 128
                    kb.op("pe", lambda e: e.matmul(banks[bank][:, k * width:(k + 1) * width],
                                                   lhsT=uT[:, c, tt * 128:(tt + 1) * 128],
                                                   rhs=wslot[slot][:, woff:woff + width],
                                                   start=(c == 0), stop=(c == NC - 1)),
                          reads=[B_w[slot], B_uT[c][tt // 4]], writes=[bankB[bank]], pe_acc=(c > 0 or k > 0),
                          inc=(c == NC - 1 and k == ntile - 1))
            def ev():
                src_ = banks[bank][:, 0:ntile * width].rearrange("p (k w) -> p k w", w=width)
                dst_ = vbuf[vp][:, t0:t0 + ntile, 0:width]
                if evac_eng == "act":
                    kb.op("act", lambda e: e.copy(out=dst_, in_=src_),
                          reads=[bankB[bank]], writes=[B_v[vp][t0 + k] for k in range(ntile)])
                else:
                    kb.op("dve", lambda e: e.tensor_copy(out=dst_, in_=src_),
                          reads=[bankB[bank]], writes=[B_v[vp][t0 + k] for k in range(ntile)])
            if defer_fn is None:
                ev()
            else:
                defer_fn(bank, ev)

        def gate_from_bank(bank, gc, b, use_act=False, part="all"):
            cols = slice(b * 512, (b + 1) * 512)
            if part == "dve":
                kb.op("dve", lambda e: e.tensor_tensor(out=gate[:, gc, cols], in0=banks[bank][:, :], in1=etmp[:, 1, :],
                                                       op=ALU.mult),
                      reads=[bankB[bank], B_etmp[1]], writes=[B_gate[gc][b]])
                return
            kb.op("act", lambda e: e.activation(out=etmp[:, 1, :], in_=banks[bank][:, :], func=AF.Exp, scale=-1.0),
                  reads=[bankB[bank]], writes=[B_etmp[1]])
            if use_act:
                kb.op("act", lambda e: e.activation(out=etmp[:, 1, :], in_=etmp[:, 1, :], func=AF.Ln, bias=ONE_AP),
                      reads=[B_etmp[1], B_dpar], writes=[B_etmp[1]])
                kb.op("act", lambda e: e.activation(out=etmp[:, 1, :], in_=etmp[:, 1, :], func=AF.Exp, scale=-1.0),
                      reads=[B_etmp[1]], writes=[B_etmp[1]])
            else:
                kb.op("dve", lambda e: e.tensor_scalar(out=etmp[:, 1, :], in0=etmp[:, 1, :], scalar1=1.0, scalar2=None,
                                                       op0=ALU.add),
                      reads=[B_etmp[1]], writes=[B_etmp[1]])
                kb.op("dve", lambda e: e.reciprocal(out=etmp[:, 1, :], in_=etmp[:, 1, :]),
                      reads=[B_etmp[1]], writes=[B_etmp[1]])
            if part == "act":
                return
            kb.op("dve", lambda e: e.tensor_tensor(out=gate[:, gc, cols], in0=banks[bank][:, :], in1=etmp[:, 1, :],
                                                   op=ALU.mult),
                  reads=[bankB[bank], B_etmp[1]], writes=[B_gate[gc][b]])

        def out_proj_unit(l, wo_units, gcs, m, b, bank, defer_fn=None):
            n = len(wo_units)
            for j in range(n):
                s = wo_pos[(l, wo_units[j])] % NOSLOT
                kb.op("pe", lambda e: e.matmul(banks[bank][:, :], lhsT=woslot[s][:, m * 128:(m + 1) * 128],
                                               rhs=gate[:, gcs[j], b * 512:(b + 1) * 512],
                                               start=(j == 0), stop=(j == n - 1)),
                      reads=[B_wo[s], B_gate[gcs[j]][b]], writes=[bankB[bank]], pe_acc=(j > 0), inc=(j == n - 1))
            def ev():
                kb.op("dve", lambda e: e.tensor_tensor(out=hT[:, m, b * 512:(b + 1) * 512],
                                                       in0=hT[:, m, b * 512:(b + 1) * 512],
                                                       in1=banks[bank][:, :], op=ALU.add),
                      reads=[bankB[bank], B_hT[m][b]], writes=[B_hT[m][b]])
            if defer_fn is None:
                ev()
            else:
                defer_fn(bank, ev)

        def tile_geom(b, i):
            if i >= 4 * b:
                return 128 * (i - 4 * b), True
            return 0, False

        def sb_layer(l):
            rmsnorm_to_uT(l)
            BZ = [0, 1]
            B_ser = [Buf("ser0"), Buf("ser1"), Buf("ser2")]
            BC = [2, 3]
            BO = 5
            BMM = [4, 6, 7]
            H = cfg.H_SB
            mmrr = {"i": 0}

            bank_evac = {bk: None for bk in BMM}
            cur_step = {"s": -1}

            def flush_bank(bk):
                while bank_evac[bk] is not None:
                    f = bank_evac[bk][1]
                    bank_evac[bk] = None
                    f()

            def flush_old_evacs():
                for bk in BMM:
                    if bank_evac[bk] is not None and bank_evac[bk][0] < cur_step["s"]:
                        flush_bank(bk)

            def next_mm():
                mmrr["i"] += 1
                bk = BMM[mmrr["i"] % len(BMM)]
                flush_bank(bk)
                return bk

            def mm_available():
                bk = BMM[(mmrr["i"] + 1) % len(BMM)]
                return bank_evac[bk] is None

            def defer(bk, f, delay=1):
                bank_evac[bk] = (cur_step["s"] + delay - 1, f)

            issue_w(unit_pos[(l, 0)])
            issue_wo(wo_pos[(l, 0)])

            def proj_units(h):
                u = unit_pos[(l, h)]
                slot = u % NSLOT
                par_ = h % 2
                qs, ks = [], []
                for b in range(NB):
                    def fq(b=b):
                        cols = slice(b * 512, (b + 1) * 512)
                        bank = next_mm()
                        proj_fm(slot, 0, b, bank)
                        defer(bank, lambda: kb.op("dve", lambda e: e.tensor_scalar(
                            out=qk[par_][:, cols], in0=banks[bank][:, :], scalar1=scale, scalar2=None, op0=ALU.mult),
                            reads=[bankB[bank]], writes=[B_qk[par_][b]]), delay=2)
                    qs.append(fq)

                    def fk(b=b):
                        cols = slice(b * 512, (b + 1) * 512)
                        bank = next_mm()
                        proj_fm(slot, 1, b, bank)
                        defer(bank, lambda: kb.op("act", lambda e: e.copy(out=qk[2 + par_][:, cols], in_=banks[bank][:, :]),
                                                  reads=[bankB[bank]], writes=[B_qk[2 + par_][b]]), delay=2)
                    ks.append(fk)
                gs, vs = [], []
                for b in range(NB):
                    def fg(b=b):
                        bank = next_mm()
                        proj_fm(slot, 2, b, bank)
                        def gA(bank=bank, b=b):
                            for obk in BMM:
                                if obk != bank and bank_evac[obk] is not None and getattr(bank_evac[obk][1], "is_gate_dve", False):
                                    flush_bank(obk)
                            gate_from_bank(bank, par_, b, use_act=True, part="act")
                            def gB():
                                gate_from_bank(bank, par_, b, use_act=True, part="dve")
                            gB.is_gate_dve = True
                            defer(bank, gB, delay=1)
                        defer(bank, gA, delay=2)
                    gs.append(fg)
                for t0 in range(0, NT, 4):
                    def fv(t0=t0):
                        bank = next_mm()
                        proj_v(slot, 3, 128, par_, t0, 4, bank, defer_fn=defer)
                    vs.append(fv)
                return qs, ks, gs, vs

            def outproj_units(h):
                us = []
                for b in range(NB):
                    for m in range(NC):
                        def fo(b=b, m=m):
                            bank = next_mm()
                            out_proj_unit(l, [h], [h % 2], m, b, bank, defer_fn=defer)
                        us.append(fo)
                return us

            qs, ks, gs, vs = proj_units(0)
            for f in qs + ks + gs + vs:
                f()
            for bk in BMM:
                flush_bank(bk)
            issue_w(unit_pos[(l, 0)] + 1)

            items = []
            for h in range(H):
                for b in range(NB):
                    n = 4 * b + 4
                    for i in range(n - 1, -1, -1):
                        off, diag = tile_geom(b, i)
                        items.append(dict(h=h, b=b, i=i, n=n, off=off, diag=diag))
            for j, it in enumerate(items):
                it["j"] = j
            nitems_head = len(items) // H
            flags = {"p4_done_head": -1, "y_done_head": -1}

            def P1(it):
                j, h, b, i, off, diag = it["j"], it["h"], it["b"], it["i"], it["off"], it["diag"]
                zb = BZ[j % 2]
                qb, kbuf = h % 2, 2 + h % 2
                qc = b * 512
                kb.op("pe", lambda e: e.matmul(banks[zb][:, off:512], lhsT=qk[kbuf][:, i * 128:(i + 1) * 128],
                                               rhs=qk[qb][:, qc + off:qc + 512], start=True, stop=(not diag)),
                      reads=[B_qk[kbuf][i // 4], B_qk[qb][b]], writes=[bankB[zb]], inc=(not diag))
                if diag:
                    kb.op("pe", lambda e: e.matmul(banks[zb][:, off:off + 128], lhsT=ident, rhs=mneg_s,
                                                   start=False, stop=True),
                          reads=[B_c], writes=[bankB[zb]], pe_acc=True)

            def A1(it):
                j, off = it["j"], it["off"]
                zb = BZ[j % 2]
                kb.op("act", lambda e: e.activation(out=ering[:, j % NE, off:512], in_=banks[zb][:, off:512], func=AF.Exp),
                      reads=[bankB[zb]], writes=[B_E[j % NE]])

            def A2(it):
                j, off = it["j"], it["off"]
                kb.op("act", lambda e: e.activation(out=spr[:, j % NSP, off:512], in_=ering[:, j % NE, off:512],
                                                    func=AF.Ln, bias=ONE_AP),
                      reads=[B_E[j % NE], B_dpar], writes=[B_sp[j % NSP]])

            def D1(it):
                j, i, n, off = it["j"], it["i"], it["n"], it["off"]
                if i == 0:
                    return
                cb = BC[j % 2]
                kb.op("dve", lambda e: e.tensor_copy(out=cs_sb[:, j % 2, off:512], in_=banks[cb][:, off:512]),
                      reads=[bankB[cb]], writes=[B_cs[j % 2], B_ser[j % 3]])

            def P3(it):
                j, i, n, off = it["j"], it["i"], it["n"], it["off"]
                cb = BC[j % 2]
                first = (i == n - 1)
                kb.op("pe", lambda e: e.matmul(banks[cb][:, off:512], lhsT=uincl, rhs=spr[:, j % NSP, off:512],
                                               start=True, stop=first, skip_group_check=True),
                      reads=[B_sp[j % NSP], B_c], writes=[bankB[cb]], inc=first)
                if not first:
                    poff = tile_geom(it["b"], i + 1)[0]
                    kb.op("pe", lambda e: e.matmul(banks[cb][:, poff:512], lhsT=cbf[:, C_BSEL:C_BSEL + 128],
                                                   rhs=cs_sb[:, (j - 1) % 2, poff:512], start=False, stop=True,
                                                   skip_group_check=True),
                          reads=[B_cs[(j - 1) % 2], B_c], writes=[bankB[cb]], pe_acc=True)

            def A3(it):
                j, off = it["j"], it["off"]
                cb = BC[j % 2]
                kb.op("act", lambda e: e.activation(out=wtmp[:, j % 2, off:512], in_=banks[cb][:, off:512],
                                                    func=AF.Exp, scale=-1.0),
                      reads=[bankB[cb], B_ser[j % 3]], writes=[B_wt[j % 2]])

            def D2(it):
                j, off = it["j"], it["off"]
                kb.op("pool", lambda e: e.tensor_tensor(out=attn[:, j % 3, off:512], in0=ering[:, j % NE, off:512],
                                                        in1=wtmp[:, j % 2, off:512], op=ALU.mult),
                      reads=[B_E[j % NE], B_wt[j % 2]], writes=[B_attn[j % 3]])

            def P4(it):
                j, h, b, i, n, off = it["j"], it["h"], it["b"], it["i"], it["n"], it["off"]
                vp = h % 2
                kb.op("pe", lambda e: e.matmul(banks[BO][:, off:512], lhsT=vbuf[vp][:, i, 0:128],
                                               rhs=attn[:, j % 3, off:512], start=(i == n - 1), stop=(i == 0),
                                               skip_group_check=True),
                      reads=[B_v[vp][i], B_attn[j % 3]], writes=[bankB[BO]], pe_acc=(i < n - 1))
                if i == 0:
                    cols = slice(b * 512, (b + 1) * 512)
                    gc = h % 2
                    kb.op("dve", lambda e: e.tensor_tensor(out=gate[:, gc, cols], in0=banks[BO][:, :], in1=gate[:, gc, cols],
                                                           op=ALU.mult),
                          reads=[bankB[BO], B_gate[gc][b]], writes=[B_gate[gc][b]])
                    if b == NB - 1:
                        flags["p4_done_head"] = h
                        flags["y_done_head"] = h

            stages = [(4, D1), (5, D2), (1, A1), (2, A2), (4, A3), (0, P1), (7, P4), (3, P3)]
            maxoff = 7
            nsteps = len(items) + maxoff

            bgq = []
            bg_head = {"h": -1}

            def enqueue_bg(h):
                if h + 1 < H:
                    qs, ks, gs, vs = proj_units(h + 1)
                else:
                    qs, ks, gs, vs = [], [], [], []
                ops = outproj_units(h - 1) if h - 1 >= 0 else []
                pre_y = (lambda hh=h - 1: flags["y_done_head"] >= hh)
                pre_v = (lambda hh=h - 1: flags["p4_done_head"] >= hh)
                lst = [(f, None, 8.0) for f in qs]
                heavy = [(f, None, 8.0) for f in ks] + [(f, pre_v, 8.0) for f in vs]
                opl = [(f, pre_y, 3.0) for f in ops]
                per = 4
                while heavy or opl:
                    if heavy:
                        lst.append(heavy.pop(0))
                    for _ in range(per):
                        if opl:
                            lst.append(opl.pop(0))
                lst += [(f, None, 8.0) for f in gs]
                return lst

            def run_bg(budget, force=False):
                spent = 0.0
                while bgq and (force or spent + 0.5 * bgq[0][2] <= budget):
                    if not force and not mm_available():
                        break
                    f, pre, cost = bgq[0]
                    if pre is not None and not pre():
                        if force:
                            raise RuntimeError("background precondition not met at forced drain")
                        break
                    bgq.pop(0)
                    f()
                    spent += cost
                return spent

            credit = 0.0
            rate = 0.0
            pend_w = {"step": -1, "idx": 0}
            for s in range(nsteps):
                if s < len(items):
                    h = items[s]["h"]
                    if h != bg_head["h"]:
                        run_bg(0, force=True)
                        for bk in BMM:
                            flush_bank(bk)
                        bg_head["h"] = h
                        bgq.extend(enqueue_bg(h))
                        pend_w["step"] = s + 8
                        pend_w["idx"] = unit_pos[(l, h)] + 2
                        issue_wo(wo_pos[(l, h)] + 1)
                        rate = sum(c for _, _, c in bgq) / float(nitems_head - 9)
                        credit = 0.0
                cur_step["s"] = s
                if pend_w["step"] == s:
                    issue_w(pend_w["idx"])
                for off_, fn in stages:
                    j = s - off_
                    if 0 <= j < len(items):
                        fn(items[j])
                    if fn is D1:
                        flush_old_evacs()
                credit += rate
                credit -= run_bg(credit)
                credit = min(credit, 3.0 * rate)
            run_bg(0, force=True)
            for bk in BMM:
                flush_bank(bk)
            issue_w(pend_w["idx"])
            for f in outproj_units(H - 1):
                f()
            for bk in BMM:
                flush_bank(bk)


        def df_layer(l):
            rmsnorm_to_uT(l)
            pb = l * PL
            db = l * 4
            BZ = [0, 1]
            SETS = [dict(O=[2, 3], SS=4), dict(O=[5, 6], SS=7)]
            H = cfg.H_DF
            kb.dma("sp", scr_f[:, 0:2 * S], rope_d[:, :], "d_misc", writes=SCR_ALL)
            rope3 = scr_f[:, 0:2 * S].rearrange("p (a s) -> p a s", a=2)
            issue_w(unit_pos[(l, 0)] + 1)
            gq_s = dpar[:, db:db + 1]
            gk = par[:, pb + NC + 1:pb + NC + 2]
            nlam = dpar[:, db + 3:db + 4]
            zc = {"i": 0}
            grp = {"i": 0}

            op_pending = []
            op_evac = {0: None, 1: None}
            op_rr = {"i": 0}

            def op_flush(bk):
                if op_evac[bk] is not None:
                    f = op_evac[bk]
                    op_evac[bk] = None
                    f()

            def make_op_units(hd):
                us = []
                for b_ in range(NB):
                    for m_ in range(NC):
                        def fo(b_=b_, m_=m_):
                            op_rr["i"] += 1
                            bk = op_rr["i"] % 2
                            op_flush(bk)
                            out_proj_unit(l, [2 * hd, 2 * hd + 1], [0, 1], m_, b_, bk,
                                          defer_fn=lambda bank, ev: op_evac.__setitem__(bank, ev))
                        us.append(fo)
                return us

            def run_ops(k, flush=True):
                if flush:
                    for bk in (0, 1):
                        op_flush(bk)
                for _ in range(min(k, 2)):
                    if op_pending:
                        op_pending.pop(0)()

            epi_carry = []
            for hd in range(H):
                vp = hd % 2
                uA = unit_pos[(l, 2 * hd)]
                uB = uA + 1
                sA, sB = uA % NSLOT, uB % NSLOT
                def make_v_units():
                    us = []
                    for t0_ in range(0, NT, 2):
                        def fv(t0_=t0_):
                            op_rr["i"] += 1
                            bk = op_rr["i"] % 2
                            op_flush(bk)
                            proj_v(sB, 2, 256, vp, t0_, 2, bk,
                                   defer_fn=lambda bank, ev: op_evac.__setitem__(bank, ev), evac_eng="act")
                        us.append(fv)
                    return us

                vus = make_v_units()
                merged = []
                while op_pending or vus:
                    for _ in range(4):
                        if op_pending:
                            merged.append(op_pending.pop(0))
                    if vus:
                        merged.append(vus.pop(0))
                op_pending.extend(merged)
                qitems = [(qi, b) for qi in range(4) for b in range(NB)]
                MMR = [4, 5, 6, 7]
                BSQ = 2
                BRB = 3

                def q_proj(j, qi, b):
                    proj_fm(sA, qi, b, MMR[j % 4])

                def q_sq(j, qi, b):
                    kb.op("act", lambda e: e.activation(out=attn[:, j % 2, :], in_=banks[MMR[j % 4]][:, :], func=AF.Square),
                          reads=[bankB[MMR[j % 4]]], writes=[B_attn[j % 2]])

                def q_ss(j, qi, b):
                    kb.op("pe", lambda e: e.matmul(banks[BSQ][:, :], lhsT=ones, rhs=attn[:, j % 2, :], start=True, stop=True),
                          reads=[B_attn[j % 2], B_c], writes=[bankB[BSQ]])

                def q_ln(j, qi, b):
                    kb.op("act", lambda e: e.activation(out=etmp[:, j % 2, :], in_=banks[BSQ][:, :], func=AF.Ln,
                                                        scale=1.0 / HEAD_DIM, bias=EPS_AP),
                          reads=[bankB[BSQ], B_dpar], writes=[B_etmp[j % 2]])
                    kb.op("act", lambda e: e.activation(out=etmp[:, j % 2, :], in_=etmp[:, j % 2, :], func=AF.Exp, scale=-0.5),
                          reads=[B_etmp[j % 2]], writes=[B_etmp[j % 2]])

                def q_qg(j, qi, b):
                    gcol = gq_s if qi < 2 else gk
                    kb.op("dve", lambda e: e.scalar_tensor_tensor(out=spr[:, j % NSP, :], in0=banks[MMR[j % 4]][:, :], scalar=gcol,
                                                                  in1=etmp[:, j % 2, :], op0=ALU.mult, op1=ALU.mult),
                          reads=[bankB[MMR[j % 4]], B_etmp[j % 2], B_dpar, B_par], writes=[B_sp[j % NSP]])

                def q_rot(j, qi, b):
                    kb.op("pe", lambda e: e.matmul(banks[BRB][:, :], lhsT=pswap, rhs=spr[:, j % NSP, :], start=True, stop=True),
                          reads=[B_sp[j % NSP], B_c], writes=[bankB[BRB]])

                def q_t(j, qi, b):
                    cols = slice(b * 512, (b + 1) * 512)
                    kb.op("pool", lambda e: e.tensor_tensor(out=wtmp[:, j % 2, :], in0=spr[:, j % NSP, :], in1=rope3[:, 0, cols],
                                                            op=ALU.mult),
                          reads=[B_sp[j % NSP]] + SCR_ALL, writes=[B_wt[j % 2]])
                    kb.op("dve", lambda e: e.tensor_tensor(out=o_sb[:, j % 2, :], in0=banks[BRB][:, :], in1=rope3[:, 1, cols],
                                                           op=ALU.mult),
                          reads=[bankB[BRB]] + SCR_ALL, writes=[B_o[j % 2]])

                def q_out(j, qi, b):
                    cols = slice(b * 512, (b + 1) * 512)
                    kb.op("pool", lambda e: e.tensor_tensor(out=qk[qi][:, cols], in0=o_sb[:, j % 2, :], in1=wtmp[:, j % 2, :],
                                                            op=ALU.add),
                          reads=[B_o[j % 2], B_wt[j % 2]], writes=[B_qk[qi][b]])

                qst = [(5, q_out), (4, q_rot), (4, q_t), (3, q_qg), (2, q_ss), (2, q_ln), (1, q_sq), (0, q_proj)]
                nq = len(qitems) + 5
                for s_ in range(nq):
                    if s_ >= 1 and epi_carry:
                        epi_carry.pop(0)()
                    for bk in (0, 1):
                        op_flush(bk)
                    for off_, fn in qst:
                        j = s_ - off_
                        if 0 <= j < len(qitems):
                            fn(j, *qitems[j])
                    run_ops(2 if len(op_pending) > (nq - s_) else 1, flush=False)
                issue_w(uA + 2)
                while op_pending:
                    run_ops(2)
                for bk in (0, 1):
                    op_flush(bk)
                issue_wo(wo_pos[(l, 2 * hd + 1)])
                pre_gates = [(gc, b) for b in range(NB) for gc in range(2) if 4 * b + 4 < 8]
                gate_units = [(gc, b) for b in range(NB) for gc in range(2) if 4 * b + 4 >= 8]
                k = 0
                for gc_, gb_ in pre_gates:
                    bank = 4 + k % 4
                    k += 1
                    proj_fm(sB, gc_, gb_, bank)
                    gate_from_bank(bank, gc_, gb_, use_act=True)
                if not gate_units:
                    issue_w(uB + 2)
                epi = []

                def make_epilogue(b, ssb):
                    cols = slice(b * 512, (b + 1) * 512)

                    def e1():
                        for c in range(2):
                            kb.op("pool", lambda e: e.tensor_tensor(out=spr[:, c, :], in0=o_sb[:, c, :], in1=o_sb[:, c, :],
                                                                    op=ALU.mult),
                                  reads=[B_o[c]], writes=[B_sp[c]])

                    def e1b():
                        for c in range(2):
                            kb.op("pe", lambda e: e.matmul(banks[ssb][:, :], lhsT=ones, rhs=spr[:, c, :], start=(c == 0), stop=(c == 1)),
                                  reads=[B_sp[c], B_c], writes=[bankB[ssb]], pe_acc=(c > 0))

                    def e2():
                        kb.op("act", lambda e: e.activation(out=etmp[:, 0, :], in_=banks[ssb][:, :], func=AF.Ln,
                                                            scale=1.0 / (2 * HEAD_DIM), bias=EPS_AP),
                              reads=[bankB[ssb], B_dpar], writes=[B_etmp[0]])
                        kb.op("act", lambda e: e.activation(out=etmp[:, 0, :], in_=etmp[:, 0, :], func=AF.Exp, scale=-0.5),
                              reads=[B_etmp[0]], writes=[B_etmp[0]])

                    def e3():
                        for c in range(2):
                            kb.op("dve", lambda e: e.scalar_tensor_tensor(out=wtmp[:, c, :], in0=o_sb[:, c, :],
                                                                          scalar=dpar[:, db + 1 + c:db + 2 + c], in1=etmp[:, 0, :],
                                                                          op0=ALU.mult, op1=ALU.mult),
                                  reads=[B_o[c], B_dpar, B_etmp[0]], writes=[B_wt[c]])
                            kb.op("pool", lambda e: e.tensor_tensor(out=gate[:, c, cols], in0=wtmp[:, c, :], in1=gate[:, c, cols],
                                                                    op=ALU.mult),
                                  reads=[B_wt[c], B_gate[c][b]], writes=[B_gate[c][b]])
                    return [e1, e1b, e2, e3]

                tiles = []
                for b in range(NB):
                    for m in range(2):
                        n = 4 * b + 4
                        for i in range(n):
                            tiles.append((b, m, i, n))
                BZ3 = [0, 1, 2]
                OSETS = [[3, 4], [5, 6]]
                BSS1 = 7

                def gidx(b, m):
                    return 2 * b + m

                def qk_mm(t):
                    b, m, i, n = tiles[t]
                    off, diag = tile_geom(b, i)
                    zb = BZ3[t % 3]
                    qcols = b * 512
                    kb.op("pe", lambda e: e.matmul(banks[zb][:, off:512], lhsT=qk[2 + m][:, i * 128:(i + 1) * 128],
                                                   rhs=qk[m][:, qcols + off:qcols + 512], start=True, stop=(not diag)),
                          reads=[B_qk[2 + m][i // 4], B_qk[m][b]], writes=[bankB[zb]], inc=(not diag))
                    if diag:
                        kb.op("pe", lambda e: e.matmul(banks[zb][:, off:off + 128], lhsT=ident, rhs=mneg_i,
                                                       start=False, stop=True),
                              reads=[B_c], writes=[bankB[zb]], pe_acc=True)

                def act_p(t):
                    b, m, i, n = tiles[t]
                    off, diag = tile_geom(b, i)
                    zb = BZ3[t % 3]
                    kb.op("act", lambda e: e.activation(out=attn[:, t % NATT, off:512], in_=banks[zb][:, off:512], func=AF.Exp),
                          reads=[bankB[zb]], writes=[B_attn[t % NATT]])

                def av_mm(t):
                    b, m, i, n = tiles[t]
                    off, diag = tile_geom(b, i)
                    BOa = OSETS[gidx(b, m) % 2]
                    for c in range(2):
                        kb.op("pe", lambda e: e.matmul(banks[BOa[c]][:, off:512],
                                                       lhsT=vbuf[vp][:, i, c * 128:(c + 1) * 128],
                                                       rhs=attn[:, t % NATT, off:512], start=(i == 0), stop=(i == n - 1)),
                              reads=[B_v[vp][i], B_attn[t % NATT]], writes=[bankB[BOa[c]]], pe_acc=(i > 0), inc=(c == 1))
                    p_ = gidx(b, m) % 2
                    for hf, eng_ in ((0, "dve"), (1, "pool")):
                        lo, hi = (max(off, 0), 384) if hf == 0 else (max(off, 384), 512)
                        if lo >= hi:
                            continue
                        if i == 0:
                            kb.op(eng_, lambda e: e.tensor_copy(out=sacc[:, p_, lo:hi], in_=attn[:, t % NATT, lo:hi]),
                                  reads=[B_attn[t % NATT]], writes=[B_sacc[p_][hf]])
                        else:
                            kb.op(eng_, lambda e: e.tensor_tensor(out=sacc[:, p_, lo:hi], in0=sacc[:, p_, lo:hi],
                                                                  in1=attn[:, t % NATT, lo:hi], op=ALU.add),
                                  reads=[B_attn[t % NATT], B_sacc[p_][hf]], writes=[B_sacc[p_][hf]])

                def make_evac(b, m):
                    BOa = OSETS[gidx(b, m) % 2]

                    def ev():
                        p_ = gidx(b, m) % 2
                        kb.op("dve", lambda e: e.tensor_copy(out=spr[:, 2, :], in_=sacc[:, p_, :]),
                              reads=B_sacc[p_], writes=[B_sp[2]])
                        kb.op("pe", lambda e: e.matmul(banks[BSS1][:, :], lhsT=ones, rhs=spr[:, 2, :], start=True, stop=True),
                              reads=[B_sp[2], B_c], writes=[bankB[BSS1]])
                        kb.op("act", lambda e: e.activation(out=etmp[:, 1, :], in_=banks[BSS1][:, :], func=AF.Ln),
                              reads=[bankB[BSS1]], writes=[B_etmp[1]])
                        kb.op("act", lambda e: e.activation(out=etmp[:, 1, :], in_=etmp[:, 1, :], func=AF.Exp, scale=-1.0),
                              reads=[B_etmp[1]], writes=[B_etmp[1]])
                        for c in range(2):
                            if m == 0:
                                kb.op("dve", lambda e: e.tensor_tensor(out=o_sb[:, c, :], in0=banks[BOa[c]][:, :],
                                                                       in1=etmp[:, 1, :], op=ALU.mult),
                                      reads=[bankB[BOa[c]], B_etmp[1]], writes=[B_o[c]])
                            else:
                                kb.op("dve", lambda e: e.tensor_tensor(out=wtmp[:, c, :], in0=banks[BOa[c]][:, :],
                                                                       in1=etmp[:, 1, :], op=ALU.mult),
                                      reads=[bankB[BOa[c]], B_etmp[1]], writes=[B_wt[c]])
                                kb.op("dve", lambda e: e.scalar_tensor_tensor(out=o_sb[:, c, :], in0=wtmp[:, c, :], scalar=nlam,
                                                                              in1=o_sb[:, c, :], op0=ALU.mult, op1=ALU.add),
                                      reads=[B_wt[c], B_o[c], B_dpar], writes=[B_o[c]])
                        if m == 1:
                            epi.extend(make_epilogue(b, BOa[0] if b < NB - 1 else 3))
                    return ev

                pend_evac = []
                gchain = []
                gtmp = cs_sb[:, :, :].rearrange("p a n -> p (a n)").bitcast(F32)

                def gate_chain(bank, gc, b_):
                    cols_ = slice(b_ * 512, (b_ + 1) * 512)

                    def g1():
                        kb.op("act", lambda e: e.activation(out=gtmp, in_=banks[bank][:, :], func=AF.Exp, scale=-1.0),
                              reads=[bankB[bank]], writes=B_cs)

                    def g2():
                        kb.op("act", lambda e: e.activation(out=gtmp, in_=gtmp, func=AF.Ln, bias=ONE_AP),
                              reads=B_cs + [B_dpar], writes=B_cs)

                    def g3():
                        kb.op("act", lambda e: e.activation(out=gtmp, in_=gtmp, func=AF.Exp, scale=-1.0),
                              reads=B_cs, writes=B_cs)
                        kb.op("dve", lambda e: e.tensor_tensor(out=gate[:, gc, cols_], in0=banks[bank][:, :], in1=gtmp,
                                                               op=ALU.mult),
                              reads=[bankB[bank]] + B_cs, writes=[B_gate[gc][b_]])
                    return [g1, g2, g3]

                epi_age = {"n": 0}
                NTL = len(tiles)
                qk_mm(0)
                if NTL > 1:
                    qk_mm(1)
                for t in range(NTL):
                    b, m, i, n = tiles[t]
                    if t + 2 < NTL:
                        qk_mm(t + 2)
                    act_p(t)
                    if pend_evac and (i >= 2 or i == n - 1):
                        while pend_evac:
                            pend_evac.pop(0)()
                    av_mm(t)
                    if epi:
                        epi_age["n"] += 1
                        if epi_age["n"] >= 3:
                            epi.pop(0)()
                    else:
                        epi_age["n"] = 0
                    if gchain:
                        gchain.pop(0)()
                        if i == n - 1:
                            while gchain:
                                gchain.pop(0)()
                    elif gate_units and n >= 8 and (i == 6 or (n >= 16 and i == 10)):
                        gc_, gb_ = gate_units.pop(0)
                        gbank = OSETS[(gidx(b, m) + 1) % 2][1]
                        proj_fm(sB, gc_, gb_, gbank)
                        gchain.extend(gate_chain(gbank, gc_, gb_))
                        if not gate_units:
                            issue_w(uB + 2)
                    if i == n - 1:
                        pend_evac.append(make_evac(b, m))
                while pend_evac:
                    pend_evac.pop(0)()
                while gchain:
                    gchain.pop(0)()
                assert not gate_units
                assert len(epi) == 4 and not epi_carry
                epi.pop(0)()
                if hd + 1 < H:
                    e1b_, e2_, e3_ = epi
                    epi_carry.extend([lambda e1b_=e1b_, e2_=e2_: (e1b_(), e2_()), e3_])
                else:
                    while epi:
                        epi.pop(0)()
                del epi[:]
                op_pending.extend(make_op_units(hd))
            while op_pending:
                run_ops(2)
            for bk in (0, 1):
                op_flush(bk)

        for l in range(depth):
            if l % 2 == 0:
                sb_layer(l)
            else:
                df_layer(l)

        mmb = 0
        for si, t0 in enumerate(range(0, NT, tiles_per_stage)):
            nt = min(tiles_per_stage, NT - t0)
            sbuf_ap, scap, sbufs = stage_bufs[si % len(stage_bufs)]
            xv = sbuf_ap[:, 0:nt * D].rearrange("p (a d) -> p a d", a=nt)
            for a_ in range(nt):
                tt = t0 + a_
                b_ = tt // 4
                for c0 in range(0, NC, 4):
                    cn = min(4, NC - c0)
                    bk = 4 + mmb % 4
                    mmb += 1
                    for k_ in range(cn):
                        c = c0 + k_
                        kb.op("pe", lambda e: e.transpose(out=banks[bk][:, k_ * 128:(k_ + 1) * 128],
                                                          in_=hT[:, c, tt * 128:(tt + 1) * 128], identity=ident_f),
                              reads=[B_hT[c][b_], B_c], writes=[bankB[bk]], pe_acc=(k_ > 0))
                    kb.op("dve", lambda e: e.tensor_copy(out=xv[:, a_, c0 * 128:(c0 + cn) * 128], in_=banks[bk][:, 0:cn * 128]),
                          reads=[bankB[bk]], writes=sbufs)
            for a_ in range(nt):
                kb.dma("sp", out_d[(t0 + a_) * 128:(t0 + a_ + 1) * 128, :], xv[:, a_, :],
                       f"d_out{si % len(stage_bufs)}", reads=sbufs)
        kb.wait_all("sp", [f"d_out{i}" for i in range(len(stage_bufs))])
        print(f"[build] instructions={kb.n_ins} waits={kb.n_wait}")
    return nc


def prep_weights(cfg, inputs):
    D, E, NC = cfg.D, cfg.E, cfg.NC
    out = {}
    P = np.zeros((128, cfg.depth * cfg.PL), np.float32)
    for l in range(cfg.depth):
        j = l // 2
        pb = l * cfg.PL
        if l % 2 == 0:
            W = np.asarray(inputs["sb_w_in"][j], np.float32)
            H = cfg.H_SB
            Wh = W.reshape(NC, 128, 4, H, 128)
            sel = Wh[:, :, [0, 1, 3, 2]]
            win = np.ascontiguousarray(sel.transpose(3, 1, 0, 2, 4)).reshape(H, 128, NC * 4 * 128)
            out[f"win{l}"] = win
            out[f"wout{l}"] = np.ascontiguousarray(np.asarray(inputs["sb_w_out"][j], np.float32))
            g = np.asarray(inputs["sb_norm"][j], np.float32)
            P[:, pb:pb + NC] = g.reshape(NC, 128).T
        else:
            W = np.asarray(inputs["df_w_in"][j], np.float32)
            H = cfg.H_DF
            Wr = W.reshape(NC, 128, 4, H, 2, 128)
            chunks = np.stack([Wr[:, :, 0, :, 0], Wr[:, :, 0, :, 1], Wr[:, :, 1, :, 0], Wr[:, :, 1, :, 1],
                               Wr[:, :, 3, :, 0], Wr[:, :, 3, :, 1], Wr[:, :, 2, :, 0], Wr[:, :, 2, :, 1]], axis=0)
            chunks = chunks.reshape(2, 4, NC, 128, H, 128)
            win = np.ascontiguousarray(chunks.transpose(4, 0, 3, 2, 1, 5)).reshape(H * 2, 128, NC * 4 * 128)
            out[f"win{l}"] = win
            out[f"wout{l}"] = np.ascontiguousarray(np.asarray(inputs["df_w_out"][j], np.float32))
            g = np.asarray(inputs["df_norm"][j], np.float32)
            P[:, pb:pb + NC] = g.reshape(NC, 128).T
            P[:, pb + NC] = np.asarray(inputs["df_q_norm"][j], np.float32)
            P[:, pb + NC + 1] = np.asarray(inputs["df_k_norm"][j], np.float32)
            P[:, pb + NC + 2] = np.asarray(inputs["df_lam_q1"][j], np.float32)
            P[:, pb + NC + 3] = np.asarray(inputs["df_lam_k1"][j], np.float32)
            P[:, pb + NC + 4] = np.asarray(inputs["df_lam_q2"][j], np.float32)
            P[:, pb + NC + 5] = np.asarray(inputs["df_lam_k2"][j], np.float32)
            gs = np.asarray(inputs["df_sub_norm"][j], np.float32)
            P[:, pb + NC + 6] = gs[:128]
            P[:, pb + NC + 7] = gs[128:]
    out["params"] = P
    out["consts"] = make_consts()
    out["rope"] = make_rope(cfg.S)
    return out


_PROG_CACHE = {}


def run(cfg, inputs, n_cores, trace=False):
    key = (cfg.S, cfg.D, cfg.depth)
    if key not in _PROG_CACHE:
        _PROG_CACHE[key] = build_program(cfg)
    nc = _PROG_CACHE[key]
    shared = prep_weights(cfg, inputs)
    x = np.asarray(inputs["x"], np.float32)
    in_maps = []
    for b in range(n_cores):
        m = dict(shared)
        m["x"] = np.ascontiguousarray(x[b])
        in_maps.append(m)
    res = run_bass_kernel_spmd(nc, in_maps, core_ids=list(range(n_cores)), trace=trace)
    out = np.stack([np.asarray(r["out"], np.float32) for r in res.results], axis=0)
    return out, res


def kernel(x, sb_norm, sb_w_in, sb_w_out, df_norm, df_w_in, df_w_out,
           df_q_norm, df_k_norm, df_lam_q1, df_lam_k1, df_lam_q2, df_lam_k2, df_sub_norm):
    inputs = dict(x=x, sb_norm=sb_norm, sb_w_in=sb_w_in, sb_w_out=sb_w_out, df_norm=df_norm,
                  df_w_in=df_w_in, df_w_out=df_w_out, df_q_norm=df_q_norm, df_k_norm=df_k_norm,
                  df_lam_q1=df_lam_q1, df_lam_k1=df_lam_k1, df_lam_q2=df_lam_q2, df_lam_k2=df_lam_k2,
                  df_sub_norm=df_sub_norm)
    cfg = Cfg(S=2048, D=1024, depth=4)
    out, _ = run(cfg, inputs, 8)
    return out.astype(np.float32)
```

```python
import math
from contextlib import ExitStack

import numpy as np
import concourse.bass as bass
import concourse.mybir as mybir
from concourse.bass_utils import run_bass_kernel_spmd

F32 = mybir.dt.float32
BF16 = mybir.dt.bfloat16
AF = mybir.ActivationFunctionType
ALU = mybir.AluOpType

HEAD_DIM = 128
EPS = 1e-6
ROPE_THETA = 10000.0
MASKV = 30000.0


class Cfg:
    def __init__(self, S=2048, D=1024, depth=4):
        self.S, self.D, self.depth = S, D, depth
        self.E = 2 * D
        self.NC = D // 128
        self.NB = S // 512
        self.NT = S // 128
        self.H_SB = self.E // 128
        self.H_DF = self.E // 256
        self.PL = self.NC + 8
        self.UW = self.NC * 4 * 128


class Buf:
    __slots__ = ("name", "w", "r")

    def __init__(self, name):
        self.name = name
        self.w = None
        self.r = {}


class KB:
    def __init__(self, nc, es):
        self.nc = nc
        self.es = es
        self.eng = {"pe": nc.tensor, "act": nc.scalar, "dve": nc.vector,
                    "pool": nc.gpsimd, "sp": nc.sync}
        self.sems = {}
        self.cnt = {}
        self.seen = {e: {} for e in self.eng}
        for e in self.eng:
            self._sem(e)
        self.n_wait = 0
        self.n_ins = 0

    def _sem(self, key):
        if key not in self.sems:
            self.sems[key] = self.es.enter_context(self.nc.semaphore("s_" + key))
            self.cnt[key] = 0
        return self.sems[key]

    def _waits(self, eng, reads, writes, pe_acc):
        deps = {}

        def add(ev):
            if ev is None:
                return
            k, v = ev
            if deps.get(k, 0) < v:
                deps[k] = v

        for b in reads:
            add(b.w)
        for b in writes:
            if not (pe_acc and eng == "pe" and b.w is not None and b.w[0] == "pe"):
                add(b.w)
            for k, v in b.r.items():
                add((k, v))
        seen = self.seen[eng]
        for k, v in deps.items():
            if k.startswith("d_"):
                v = self.cnt[k]
            if seen.get(k, 0) < v:
                self.eng[eng].wait_ge(self.sems[k], v)
                seen[k] = v
                self.n_wait += 1

    def _record(self, ev, reads, writes):
        k, v = ev
        for b in reads:
            if b.r.get(k, 0) < v:
                b.r[k] = v
        for b in writes:
            b.w = ev
            b.r = {}

    def op(self, eng, fn, reads=(), writes=(), pe_acc=False, inc=True):
        self._waits(eng, reads, writes, pe_acc)
        ins = fn(self.eng[eng])
        self.n_ins += 1
        if inc:
            self.cnt[eng] += 1
            ins.then_inc(self.sems[eng], 1)
            ev = (eng, self.cnt[eng])
        else:
            ev = (eng, self.cnt[eng] + 1)
        self._record(ev, reads, writes)
        return ev

    def dma(self, q, out, in_, semkey, reads=(), writes=(), **kw):
        self._sem(semkey)
        self._waits(q, reads, writes, False)
        ins = self.eng[q].dma_start(out=out, in_=in_, **kw)
        self.n_ins += 1
        self.cnt[semkey] += 16
        ins.then_inc(self.sems[semkey], 16)
        ev = (semkey, self.cnt[semkey])
        self._record(ev, reads, writes)
        return ev

    def wait_all(self, eng, keys):
        for k in keys:
            v = self.cnt.get(k, 0)
            if v > 0 and self.seen[eng].get(k, 0) < v:
                self.eng[eng].wait_ge(self.sems[k], v)
                self.seen[eng][k] = v


C_IDENT, C_UINCL, C_MNEG_S, C_MPOS_S, C_MNEG_I, C_ONES, C_PSWAP, C_EBIG = (
    0, 128, 256, 384, 512, 640, 768, 896)
C_BSEL = 896 + 144
C_TOTAL = C_BSEL + 128


def make_consts():
    c = np.zeros((128, C_TOTAL), np.float32)
    p = np.arange(128)[:, None]
    f = np.arange(128)[None, :]
    c[:, C_IDENT:C_IDENT + 128] = (p == f)
    c[:, C_UINCL:C_UINCL + 128] = (p >= f)
    c[:, C_MNEG_S:C_MNEG_S + 128] = np.where(p >= f, -MASKV, 0.0)
    c[:, C_MPOS_S:C_MPOS_S + 128] = np.where(p >= f, MASKV, 0.0)
    c[:, C_MNEG_I:C_MNEG_I + 128] = np.where(p > f, -MASKV, 0.0)
    c[:, C_ONES:C_ONES + 128] = 1.0
    c[:, C_PSWAP:C_PSWAP + 128] = (p == (f + 64) % 128)
    c[:, C_EBIG + 15] = 1.0
    c[0, C_BSEL:C_BSEL + 128] = 1.0
    return c


def make_rope(S):
    inv = (1.0 / (ROPE_THETA ** (np.arange(0, HEAD_DIM, 2, dtype=np.float32) / np.float32(HEAD_DIM)))).astype(np.float32)
    ang = (np.arange(S, dtype=np.float32)[:, None] * inv[None, :]).astype(np.float32)
    cos = np.cos(ang).astype(np.float32).T
    sin = np.sin(ang).astype(np.float32).T
    out = np.zeros((128, 2, S), np.float32)
    out[:64, 0] = cos
    out[64:, 0] = cos
    out[:64, 1] = -sin
    out[64:, 1] = sin
    return out.reshape(128, 2 * S)


def build_program(cfg):
    S, D, E, NC, NB, NT = cfg.S, cfg.D, cfg.E, cfg.NC, cfg.NB, cfg.NT
    depth, PL, UW = cfg.depth, cfg.PL, cfg.UW
    scale = 1.0 / math.sqrt(HEAD_DIM)

    nc = bass.Bass("TRN2", target_bir_lowering=False)
    es = ExitStack()
    with es:
        x_d = nc.dram_tensor("x", [S, D], F32, kind="ExternalInput").ap()
        out_d = nc.dram_tensor("out", [S, D], F32, kind="ExternalOutput").ap()
        consts_d = nc.dram_tensor("consts", [128, C_TOTAL], F32, kind="ExternalInput").ap()
        rope_d = nc.dram_tensor("rope", [128, 2 * S], F32, kind="ExternalInput").ap()
        par_d = nc.dram_tensor("params", [128, depth * PL], F32, kind="ExternalInput").ap()
        win_d, wout_d = [], []
        for l in range(depth):
            nu = cfg.H_SB if l % 2 == 0 else 2 * cfg.H_DF
            win_d.append(nc.dram_tensor(f"win{l}", [nu, 128, UW], F32, kind="ExternalInput").ap())
            wout_d.append(nc.dram_tensor(f"wout{l}", [E, D], F32, kind="ExternalInput").ap())

        def sb(name, shape, dt):
            return es.enter_context(nc.sbuf_tensor(name, shape, dt))

        hT = sb("hT", [128, NC, S], F32)
        uT = sb("uT", [128, NC, S], BF16)
        vbuf = [sb(f"vbuf{i}", [128, NT, 256], BF16) for i in range(2)]
        qk = [sb(f"qk{i}", [128, S], BF16) for i in range(4)]
        gate = sb("gate", [128, 2, S], BF16)
        scr = sb("scr", [128, 4096], F32)
        etmp = sb("etmp", [128, 2, 512], F32)
        NSP = 4
        spr = sb("spr", [128, NSP, 512], BF16)
        wtmp = sb("wtmp", [128, 2, 512], F32)
        NATT = 5
        attn = sb("attn", [128, NATT, 512], BF16)
        cs_sb = sb("cs_sb", [128, 2, 512], BF16)
        NSLOT = 2
        wslot = [sb(f"wslot{i}", [128, UW], BF16) for i in range(NSLOT)]
        NOSLOT = 3
        woslot = [sb(f"woslot{i}", [128, D], BF16) for i in range(NOSLOT)]
        cbf = sb("cbf", [128, C_TOTAL], BF16)
        cf32 = sb("cf32", [128, 256], F32)
        par = sb("par", [128, depth * PL], F32)
        dpar = sb("dpar", [128, depth * 4 + 4], F32)
        o_sb = sb("o_sb", [128, 2, 512], F32)
        sacc = sb("sacc", [128, 2, 512], F32)

        NE = 7
        ering = scr[:, 0:NE * 512].rearrange("p (i n) -> p i n", n=512)
        scr_f = scr[:, :]
        print("[build] SBUF bytes/partition remaining:", nc.sbuf_bytes_remaining)

        banks = [es.enter_context(nc.psum_tensor(f"bank{i}", [128, 512], F32)) for i in range(8)]
        bankB = [Buf(f"bank{i}") for i in range(8)]

        kb = KB(nc, es)

        B_hT = [[Buf(f"hT{c}_{b}") for b in range(NB)] for c in range(NC)]
        B_uT = [[Buf(f"uT{c}_{b}") for b in range(NB)] for c in range(NC)]
        B_v = [[Buf(f"v{p}_{t}") for t in range(NT)] for p in range(2)]
        B_qk = [[Buf(f"qk{i}_{b}") for b in range(NB)] for i in range(4)]
        B_gate = [[Buf(f"gate{c}_{b}") for b in range(NB)] for c in range(2)]
        B_E = [Buf(f"E{i}") for i in range(NE)]
        B_scr1 = Buf("scr_rest")
        SCR_ALL = B_E + [B_scr1]
        B_sp = [Buf(f"sp{i}") for i in range(NSP)]
        B_etmp = [Buf("etmp0"), Buf("etmp1")]
        B_wt = [Buf("wt0"), Buf("wt1")]
        B_attn = [Buf(f"attn{i}") for i in range(NATT)]
        B_cs = [Buf("cs_sb0"), Buf("cs_sb1")]
        B_w = [Buf(f"w{i}") for i in range(NSLOT)]
        B_wo = [Buf(f"wo{i}") for i in range(NOSLOT)]
        B_c = Buf("consts")
        B_par = Buf("par")
        B_dpar = Buf("dpar")
        B_o = [Buf("o_sb0"), Buf("o_sb1")]
        B_sacc = [[Buf(f"sacc{p}_{h}") for h in range(2)] for p in range(2)]

        def cc(off, n=128):
            return cbf[:, off:off + n]

        ident = cc(C_IDENT)
        uincl = cc(C_UINCL)
        mneg_s = cc(C_MNEG_S)
        mneg_i = cc(C_MNEG_I)
        ones = cc(C_ONES)
        pswap = cc(C_PSWAP)
        ident_f = cf32[:, 0:128]
        ones_f = cf32[:, 128:256]

        cst = scr_f[:, 0:C_TOTAL]
        kb.dma("sp", cst, consts_d[:, :], "d_misc", writes=SCR_ALL)
        kb.dma("sp", par[:, :], par_d[:, :], "d_misc", writes=[B_par])
        kb.op("dve", lambda e: e.tensor_copy(out=cbf[:, :], in_=cst), reads=SCR_ALL, writes=[B_c])
        kb.op("dve", lambda e: e.tensor_copy(out=cf32[:, 0:128], in_=cst[:, C_IDENT:C_IDENT + 128]),
              reads=SCR_ALL, writes=[B_c])
        kb.op("dve", lambda e: e.tensor_copy(out=cf32[:, 128:256], in_=cst[:, C_ONES:C_ONES + 128]),
              reads=SCR_ALL, writes=[B_c])
        kb.op("dve", lambda e: e.memset(dpar[:, depth * 4:depth * 4 + 1], EPS), writes=[B_dpar])
        kb.op("dve", lambda e: e.memset(dpar[:, depth * 4 + 1:depth * 4 + 2], 1.0), writes=[B_dpar])
        EPS_AP = dpar[:, depth * 4:depth * 4 + 1]
        ONE_AP = dpar[:, depth * 4 + 1:depth * 4 + 2]

        for l in range(depth):
            if l % 2 == 0:
                continue
            pb = l * PL
            db = l * 4
            lam_init = 0.8 - 0.6 * math.exp(-0.3 * l)
            kb.op("dve", lambda e: e.tensor_scalar(out=dpar[:, db:db + 1], in0=par[:, pb + NC:pb + NC + 1],
                                                   scalar1=scale, scalar2=None, op0=ALU.mult),
                  reads=[B_par], writes=[B_dpar])
            kb.op("dve", lambda e: e.tensor_scalar(out=dpar[:, db + 1:db + 3], in0=par[:, pb + NC + 6:pb + NC + 8],
                                                   scalar1=(1.0 - lam_init), scalar2=None, op0=ALU.mult),
                  reads=[B_par], writes=[B_dpar])
            kb.op("dve", lambda e: e.tensor_tensor(out=etmp[:, 0, 0:1], in0=par[:, pb + NC + 2:pb + NC + 3],
                                                   in1=par[:, pb + NC + 3:pb + NC + 4], op=ALU.mult),
                  reads=[B_par], writes=[B_etmp[0]])
            kb.op("dve", lambda e: e.tensor_tensor(out=etmp[:, 0, 1:2], in0=par[:, pb + NC + 4:pb + NC + 5],
                                                   in1=par[:, pb + NC + 5:pb + NC + 6], op=ALU.mult),
                  reads=[B_par], writes=[B_etmp[0]])
            kb.op("pe", lambda e: e.matmul(banks[0][:, 0:2], lhsT=ones_f, rhs=etmp[:, 0, 0:2], start=True, stop=True),
                  reads=[B_c, B_etmp[0]], writes=[bankB[0]])
            kb.op("act", lambda e: e.activation(out=etmp[:, 1, 0:2], in_=banks[0][:, 0:2], func=AF.Exp),
                  reads=[bankB[0]], writes=[B_etmp[1]])
            kb.op("dve", lambda e: e.tensor_tensor(out=etmp[:, 1, 2:3], in0=etmp[:, 1, 1:2], in1=etmp[:, 1, 0:1],
                                                   op=ALU.subtract),
                  reads=[B_etmp[1]], writes=[B_etmp[1]])
            kb.op("dve", lambda e: e.tensor_scalar(out=dpar[:, db + 3:db + 4], in0=etmp[:, 1, 2:3],
                                                   scalar1=-lam_init, scalar2=None, op0=ALU.add),
                  reads=[B_etmp[1]], writes=[B_dpar])

        unit_list = []
        for l in range(depth):
            nu = cfg.H_SB if l % 2 == 0 else 2 * cfg.H_DF
            for u in range(nu):
                unit_list.append((l, u))
        unit_pos = {lu: i for i, lu in enumerate(unit_list)}
        wo_list = list(unit_list)
        wo_pos = {lu: i for i, lu in enumerate(wo_list)}
        state = {"w_next": 0, "wo_next": 0}

        def issue_w(upto):
            while state["w_next"] <= min(upto, len(unit_list) - 1):
                i = state["w_next"]
                l, u = unit_list[i]
                s = i % NSLOT
                nparts = 1
                pw = UW // nparts
                for k in range(nparts):
                    kb.dma("pool", wslot[s][:, k * pw:(k + 1) * pw], win_d[l][u, :, k * pw:(k + 1) * pw],
                           f"d_w{s}", writes=[B_w[s]])
                state["w_next"] += 1

        def issue_wo(upto):
            while state["wo_next"] <= min(upto, len(wo_list) - 1):
                i = state["wo_next"]
                l, u = wo_list[i]
                s = i % NOSLOT
                kb.dma("pool", woslot[s][:, :], wout_d[l][u * 128:(u + 1) * 128, :], f"d_wo{s}", writes=[B_wo[s]])
                state["wo_next"] += 1

        uT_f = uT[:, :, :].rearrange("p c s -> p (c s)").bitcast(F32)
        ALL_UT = [B_uT[c][b_] for c in range(NC) for b_ in range(NB)]
        half_ut = (NC * S // 2) // 2
        stage_bufs = [(scr_f, 4096, SCR_ALL)]
        if half_ut >= D:
            stage_bufs.append((uT_f[:, 0:half_ut], half_ut, ALL_UT))
            stage_bufs.append((uT_f[:, half_ut:2 * half_ut], half_ut, ALL_UT))
        tiles_per_stage = max(1, min(4096, half_ut if half_ut >= D else 4096) // D)
        tiles_per_stage = max(1, 4096 // D) if tiles_per_stage * D > 4096 else tiles_per_stage
        mmb = 0
        for si, t0 in enumerate(range(0, NT, tiles_per_stage)):
            nt = min(tiles_per_stage, NT - t0)
            sbuf_ap, scap, sbufs = stage_bufs[si % len(stage_bufs)]
            xv = sbuf_ap[:, 0:nt * D].rearrange("p (a d) -> p a d", a=nt)
            for a_ in range(nt):
                kb.dma("sp", xv[:, a_, :], x_d[(t0 + a_) * 128:(t0 + a_ + 1) * 128, :],
                       f"d_x{si % len(stage_bufs)}", writes=sbufs)
            for g0 in range(0, nt, 4):
                gn = min(4, nt - g0)
                tt0 = t0 + g0
                for c in range(NC):
                    bk = mmb % 4
                    mmb += 1
                    for k_ in range(gn):
                        kb.op("pe", lambda e: e.transpose(out=banks[bk][:, k_ * 128:(k_ + 1) * 128],
                                                          in_=xv[:, g0 + k_, c * 128:(c + 1) * 128], identity=ident_f),
                              reads=sbufs + [B_c], writes=[bankB[bk]], pe_acc=(k_ > 0))
                    kb.op("dve", lambda e: e.tensor_copy(out=hT[:, c, tt0 * 128:(tt0 + gn) * 128], in_=banks[bk][:, 0:gn * 128]),
                          reads=[bankB[bk]], writes=[B_hT[c][tt0 // 4]])

        def rmsnorm_to_uT(l):
            pb = l * PL
            for b in range(NB):
                cols = slice(b * 512, (b + 1) * 512)
                ssb = 2 + b % 2
                for c in range(NC):
                    ai = c % 3
                    kb.op("act", lambda e: e.activation(out=attn[:, ai, :], in_=hT[:, c, cols], func=AF.Square),
                          reads=[B_hT[c][b]], writes=[B_attn[ai]])
                    kb.op("pe", lambda e: e.matmul(banks[ssb][:, :], lhsT=ones, rhs=attn[:, ai, :],
                                                   start=(c == 0), stop=(c == NC - 1)),
                          reads=[B_attn[ai], B_c], writes=[bankB[ssb]], pe_acc=(c > 0))
                eb = b % 2
                kb.op("act", lambda e: e.activation(out=etmp[:, eb, :], in_=banks[ssb][:, :], func=AF.Ln,
                                                    scale=1.0 / D, bias=EPS_AP),
                      reads=[bankB[ssb], B_dpar], writes=[B_etmp[eb]])
                kb.op("act", lambda e: e.activation(out=etmp[:, eb, :], in_=etmp[:, eb, :], func=AF.Exp, scale=-0.5),
                      reads=[B_etmp[eb]], writes=[B_etmp[eb]])
                for c in range(NC):
                    kb.op("dve", lambda e: e.scalar_tensor_tensor(out=uT[:, c, cols], in0=hT[:, c, cols],
                                                                  scalar=par[:, pb + c:pb + c + 1], in1=etmp[:, eb, :],
                                                                  op0=ALU.mult, op1=ALU.mult),
                          reads=[B_hT[c][b], B_par, B_etmp[eb]], writes=[B_uT[c][b]])

        def proj_fm(slot, chunk, b, bank):
            for c in range(NC):
                woff = (c * 4 + chunk) * 128
                kb.op("pe", lambda e: e.matmul(banks[bank][:, :], lhsT=wslot[slot][:, woff:woff + 128],
                                               rhs=uT[:, c, b * 512:(b + 1) * 512],
                                               start=(c == 0), stop=(c == NC - 1)),
                      reads=[B_w[slot], B_uT[c][b]], writes=[bankB[bank]], pe_acc=(c > 0), inc=(c == NC - 1))

        def proj_v(slot, chunk0, width, vp, t0, ntile, bank, defer_fn=None, evac_eng="dve"):
            for k in range(ntile):
                tt = t0 + k
                for c in range(NC):
                    woff = (c * 4 + chunk0) * 128
                    kb.op("pe", lambda e: e.matmul(banks[bank][:, k * width:(k + 1) * width],
                                                   lhsT=uT[:, c, tt * 128:(tt + 1) * 128],
                                                   rhs=wslot[slot][:, woff:woff + width],
                                                   start=(c == 0), stop=(c == NC - 1)),
                          reads=[B_w[slot], B_uT[c][tt // 4]], writes=[bankB[bank]], pe_acc=(c > 0 or k > 0),
                          inc=(c == NC - 1 and k == ntile - 1))
            def ev():
                src_ = banks[bank][:, 0:ntile * width].rearrange("p (k w) -> p k w", w=width)
                dst_ = vbuf[vp][:, t0:t0 + ntile, 0:width]
                if evac_eng == "act":
                    kb.op("act", lambda e: e.copy(out=dst_, in_=src_),
                          reads=[bankB[bank]], writes=[B_v[vp][t0 + k] for k in range(ntile)])
                else:
                    kb.op("dve", lambda e: e.tensor_copy(out=dst_, in_=src_),
                          reads=[bankB[bank]], writes=[B_v[vp][t0 + k] for k in range(ntile)])
            if defer_fn is None:
                ev()
            else:
                defer_fn(bank, ev)

        def gate_from_bank(bank, gc, b, use_act=False, part="all"):
            cols = slice(b * 512, (b + 1) * 512)
            if part == "dve":
                kb.op("dve", lambda e: e.tensor_tensor(out=gate[:, gc, cols], in0=banks[bank][:, :], in1=etmp[:, 1, :],
                                                       op=ALU.mult),
                      reads=[bankB[bank], B_etmp[1]], writes=[B_gate[gc][b]])
                return
            kb.op("act", lambda e: e.activation(out=etmp[:, 1, :], in_=banks[bank][:, :], func=AF.Exp, scale=-1.0),
                  reads=[bankB[bank]], writes=[B_etmp[1]])
            if use_act:
                kb.op("act", lambda e: e.activation(out=etmp[:, 1, :], in_=etmp[:, 1, :], func=AF.Ln, bias=ONE_AP),
                      reads=[B_etmp[1], B_dpar], writes=[B_etmp[1]])
                kb.op("act", lambda e: e.activation(out=etmp[:, 1, :], in_=etmp[:, 1, :], func=AF.Exp, scale=-1.0),
                      reads=[B_etmp[1]], writes=[B_etmp[1]])
            else:
                kb.op("dve", lambda e: e.tensor_scalar(out=etmp[:, 1, :], in0=etmp[:, 1, :], scalar1=1.0, scalar2=None,
                                                       op0=ALU.add),
                      reads=[B_etmp[1]], writes=[B_etmp[1]])
                kb.op("dve", lambda e: e.reciprocal(out=etmp[:, 1, :], in_=etmp[:, 1, :]),
                      reads=[B_etmp[1]], writes=[B_etmp[1]])
            if part == "act":
                return
            kb.op("dve", lambda e: e.tensor_tensor(out=gate[:, gc, cols], in0=banks[bank][:, :], in1=etmp[:, 1, :],
                                                   op=ALU.mult),
                  reads=[bankB[bank], B_etmp[1]], writes=[B_gate[gc][b]])

        def out_proj_unit(l, wo_units, gcs, m, b, bank, defer_fn=None):
            n = len(wo_units)
            for j in range(n):
                s = wo_pos[(l, wo_units[j])] % NOSLOT
                kb.op("pe", lambda e: e.matmul(banks[bank][:, :], lhsT=woslot[s][:, m * 128:(m + 1) * 128],
                                               rhs=gate[:, gcs[j], b * 512:(b + 1) * 512],
                                               start=(j == 0), stop=(j == n - 1)),
                      reads=[B_wo[s], B_gate[gcs[j]][b]], writes=[bankB[bank]], pe_acc=(j > 0), inc=(j == n - 1))
            def ev():
                kb.op("dve", lambda e: e.tensor_tensor(out=hT[:, m, b * 512:(b + 1) * 512],
                                                       in0=hT[:, m, b * 512:(b + 1) * 512],
                                                       in1=banks[bank][:, :], op=ALU.add),
                      reads=[bankB[bank], B_hT[m][b]], writes=[B_hT[m][b]])
            if defer_fn is None:
                ev()
            else:
                defer_fn(bank, ev)

        def tile_geom(b, i):
            if i >= 4 * b:
                return 128 * (i - 4 * b), True
            return 0, False

        def sb_layer(l):
            rmsnorm_to_uT(l)
            BZ = [0, 1]
            B_ser = [Buf("ser0"), Buf("ser1"), Buf("ser2")]
            BC = [2, 3]
            BO = 5
            BMM = [4, 6, 7]
            H = cfg.H_SB
            mmrr = {"i": 0}

            bank_evac = {bk: None for bk in BMM}
            cur_step = {"s": -1}

            def flush_bank(bk):
                while bank_evac[bk] is not None:
                    f = bank_evac[bk][1]
                    bank_evac[bk] = None
                    f()

            def flush_old_evacs():
                for bk in BMM:
                    if bank_evac[bk] is not None and bank_evac[bk][0] < cur_step["s"]:
                        flush_bank(bk)

            def next_mm():
                mmrr["i"] += 1
                bk = BMM[mmrr["i"] % len(BMM)]
                flush_bank(bk)
                return bk

            def mm_available():
                bk = BMM[(mmrr["i"] + 1) % len(BMM)]
                return bank_evac[bk] is None

            def defer(bk, f, delay=1):
                bank_evac[bk] = (cur_step["s"] + delay - 1, f)

            issue_w(unit_pos[(l, 0)])
            issue_wo(wo_pos[(l, 0)])

            def proj_units(h):
                u = unit_pos[(l, h)]
                slot = u % NSLOT
                par_ = h % 2
                qs, ks = [], []
                for b in range(NB):
                    def fq(b=b):
                        cols = slice(b * 512, (b + 1) * 512)
                        bank = next_mm()
                        proj_fm(slot, 0, b, bank)
                        defer(bank, lambda: kb.op("act", lambda e: e.mul(out=qk[par_][:, cols], in_=banks[bank][:, :], mul=scale),
                            reads=[bankB[bank]], writes=[B_qk[par_][b]]), delay=2)
                    qs.append(fq)

                    def fk(b=b):
                        cols = slice(b * 512, (b + 1) * 512)
                        bank = next_mm()
                        proj_fm(slot, 1, b, bank)
                        defer(bank, lambda: kb.op("act", lambda e: e.copy(out=qk[2 + par_][:, cols], in_=banks[bank][:, :]),
                                                  reads=[bankB[bank]], writes=[B_qk[2 + par_][b]]), delay=2)
                    ks.append(fk)
                gs, vs = [], []
                for b in range(NB):
                    def fg(b=b):
                        bank = next_mm()
                        proj_fm(slot, 2, b, bank)
                        def gA(bank=bank, b=b):
                            for obk in BMM:
                                if obk != bank and bank_evac[obk] is not None and getattr(bank_evac[obk][1], "is_gate_dve", False):
                                    flush_bank(obk)
                            gate_from_bank(bank, par_, b, use_act=True, part="act")
                            def gB():
                                gate_from_bank(bank, par_, b, use_act=True, part="dve")
                            gB.is_gate_dve = True
                            defer(bank, gB, delay=1)
                        defer(bank, gA, delay=2)
                    gs.append(fg)
                for t0 in range(0, NT, 4):
                    def fv(t0=t0):
                        bank = next_mm()
                        proj_v(slot, 3, 128, par_, t0, 4, bank, defer_fn=defer)
                    vs.append(fv)
                return qs, ks, gs, vs

            def outproj_units(h):
                us = []
                for b in range(NB):
                    for m in range(NC):
                        def fo(b=b, m=m):
                            bank = next_mm()
                            out_proj_unit(l, [h], [h % 2], m, b, bank, defer_fn=defer)
                        us.append(fo)
                return us

            qs, ks, gs, vs = proj_units(0)
            for f in qs + ks + gs + vs:
                f()
            for bk in BMM:
                flush_bank(bk)
            issue_w(unit_pos[(l, 0)] + 1)

            items = []
            for h in range(H):
                for b in range(NB):
                    n = 4 * b + 4
                    for i in range(n - 1, -1, -1):
                        off, diag = tile_geom(b, i)
                        items.append(dict(h=h, b=b, i=i, n=n, off=off, diag=diag))
            for j, it in enumerate(items):
                it["j"] = j
            nitems_head = len(items) // H
            flags = {"p4_done_head": -1, "y_done_head": -1}

            def P1(it):
                j, h, b, i, off, diag = it["j"], it["h"], it["b"], it["i"], it["off"], it["diag"]
                zb = BZ[j % 2]
                qb, kbuf = h % 2, 2 + h % 2
                qc = b * 512
                kb.op("pe", lambda e: e.matmul(banks[zb][:, off:512], lhsT=qk[kbuf][:, i * 128:(i + 1) * 128],
                                               rhs=qk[qb][:, qc + off:qc + 512], start=True, stop=(not diag)),
                      reads=[B_qk[kbuf][i // 4], B_qk[qb][b]], writes=[bankB[zb]], inc=(not diag))
                if diag:
                    kb.op("pe", lambda e: e.matmul(banks[zb][:, off:off + 128], lhsT=ident, rhs=mneg_s,
                                                   start=False, stop=True),
                          reads=[B_c], writes=[bankB[zb]], pe_acc=True)

            def A1(it):
                j, off = it["j"], it["off"]
                zb = BZ[j % 2]
                kb.op("act", lambda e: e.activation(out=ering[:, j % NE, off:512], in_=banks[zb][:, off:512], func=AF.Exp),
                      reads=[bankB[zb]], writes=[B_E[j % NE]])

            def A2(it):
                j, off = it["j"], it["off"]
                kb.op("act", lambda e: e.activation(out=spr[:, j % NSP, off:512], in_=ering[:, j % NE, off:512],
                                                    func=AF.Ln, bias=ONE_AP),
                      reads=[B_E[j % NE], B_dpar], writes=[B_sp[j % NSP]])

            def D1(it):
                j, i, n, off = it["j"], it["i"], it["n"], it["off"]
                if i == 0:
                    return
                cb = BC[j % 2]
                kb.op("dve", lambda e: e.tensor_copy(out=cs_sb[:, j % 2, off:512], in_=banks[cb][:, off:512]),
                      reads=[bankB[cb]], writes=[B_cs[j % 2], B_ser[j % 3]])

            def P3(it):
                j, i, n, off = it["j"], it["i"], it["n"], it["off"]
                cb = BC[j % 2]
                first = (i == n - 1)
                kb.op("pe", lambda e: e.matmul(banks[cb][:, off:512], lhsT=uincl, rhs=spr[:, j % NSP, off:512],
                                               start=True, stop=first, skip_group_check=True),
                      reads=[B_sp[j % NSP], B_c], writes=[bankB[cb]], inc=first)
                if not first:
                    poff = tile_geom(it["b"], i + 1)[0]
                    kb.op("pe", lambda e: e.matmul(banks[cb][:, poff:512], lhsT=cbf[:, C_BSEL:C_BSEL + 128],
                                                   rhs=cs_sb[:, (j - 1) % 2, poff:512], start=False, stop=True,
                                                   skip_group_check=True),
                          reads=[B_cs[(j - 1) % 2], B_c], writes=[bankB[cb]], pe_acc=True)

            def A3(it):
                j, off = it["j"], it["off"]
                cb = BC[j % 2]
                kb.op("act", lambda e: e.activation(out=wtmp[:, j % 2, off:512], in_=banks[cb][:, off:512],
                                                    func=AF.Exp, scale=-1.0),
                      reads=[bankB[cb], B_ser[j % 3]], writes=[B_wt[j % 2]])

            def D2(it):
                j, off = it["j"], it["off"]
                kb.op("pool", lambda e: e.tensor_tensor(out=attn[:, j % 3, off:512], in0=ering[:, j % NE, off:512],
                                                        in1=wtmp[:, j % 2, off:512], op=ALU.mult),
                      reads=[B_E[j % NE], B_wt[j % 2]], writes=[B_attn[j % 3]])

            def P4(it):
                j, h, b, i, n, off = it["j"], it["h"], it["b"], it["i"], it["n"], it["off"]
                vp = h % 2
                kb.op("pe", lambda e: e.matmul(banks[BO][:, off:512], lhsT=vbuf[vp][:, i, 0:128],
                                               rhs=attn[:, j % 3, off:512], start=(i == n - 1), stop=(i == 0),
                                               skip_group_check=True),
                      reads=[B_v[vp][i], B_attn[j % 3]], writes=[bankB[BO]], pe_acc=(i < n - 1))
                if i == 0:
                    cols = slice(b * 512, (b + 1) * 512)
                    gc = h % 2
                    kb.op("dve", lambda e: e.tensor_tensor(out=gate[:, gc, cols], in0=banks[BO][:, :], in1=gate[:, gc, cols],
                                                           op=ALU.mult),
                          reads=[bankB[BO], B_gate[gc][b]], writes=[B_gate[gc][b]])
                    if b == NB - 1:
                        flags["p4_done_head"] = h
                        flags["y_done_head"] = h

            stages = [(4, D1), (5, D2), (1, A1), (2, A2), (4, A3), (0, P1), (7, P4), (3, P3)]
            maxoff = 7
            nsteps = len(items) + maxoff

            bgq = []
            bg_head = {"h": -1}

            def enqueue_bg(h):
                if h + 1 < H:
                    qs, ks, gs, vs = proj_units(h + 1)
                else:
                    qs, ks, gs, vs = [], [], [], []
                ops = outproj_units(h - 1) if h - 1 >= 0 else []
                pre_y = (lambda hh=h - 1: flags["y_done_head"] >= hh)
                pre_v = (lambda hh=h - 1: flags["p4_done_head"] >= hh)
                lst = [(f, None, 8.0) for f in qs]
                heavy = [(f, None, 8.0) for f in ks] + [(f, pre_v, 8.0) for f in vs]
                opl = [(f, pre_y, 3.0) for f in ops]
                per = 4
                while heavy or opl:
                    if heavy:
                        lst.append(heavy.pop(0))
                    for _ in range(per):
                        if opl:
                            lst.append(opl.pop(0))
                lst += [(f, None, 8.0) for f in gs]
                return lst

            def run_bg(budget, force=False):
                spent = 0.0
                while bgq and (force or spent + 0.5 * bgq[0][2] <= budget):
                    if not force and not mm_available():
                        break
                    f, pre, cost = bgq[0]
                    if pre is not None and not pre():
                        if force:
                            raise RuntimeError("background precondition not met at forced drain")
                        break
                    bgq.pop(0)
                    f()
                    spent += cost
                return spent

            credit = 0.0
            rate = 0.0
            pend_w = {"step": -1, "idx": 0}
            for s in range(nsteps):
                if s < len(items):
                    h = items[s]["h"]
                    if h != bg_head["h"]:
                        run_bg(0, force=True)
                        for bk in BMM:
                            flush_bank(bk)
                        bg_head["h"] = h
                        bgq.extend(enqueue_bg(h))
                        pend_w["step"] = s + 8
                        pend_w["idx"] = unit_pos[(l, h)] + 2
                        issue_wo(wo_pos[(l, h)] + 1)
                        rate = sum(c for _, _, c in bgq) / float(nitems_head - 9)
                        credit = 0.0
                cur_step["s"] = s
                if pend_w["step"] == s:
                    issue_w(pend_w["idx"])
                for off_, fn in stages:
                    j = s - off_
                    if 0 <= j < len(items):
                        fn(items[j])
                    if fn is D1:
                        flush_old_evacs()
                credit += rate
                credit -= run_bg(credit)
                credit = min(credit, 3.0 * rate)
            run_bg(0, force=True)
            for bk in BMM:
                flush_bank(bk)
            issue_w(pend_w["idx"])
            for f in outproj_units(H - 1):
                f()
            for bk in BMM:
                flush_bank(bk)


        def df_layer(l):
            rmsnorm_to_uT(l)
            pb = l * PL
            db = l * 4
            BZ = [0, 1]
            SETS = [dict(O=[2, 3], SS=4), dict(O=[5, 6], SS=7)]
            H = cfg.H_DF
            kb.dma("sp", scr_f[:, 0:2 * S], rope_d[:, :], "d_misc", writes=SCR_ALL)
            rope3 = scr_f[:, 0:2 * S].rearrange("p (a s) -> p a s", a=2)
            issue_w(unit_pos[(l, 0)] + 1)
            gq_s = dpar[:, db:db + 1]
            gk = par[:, pb + NC + 1:pb + NC + 2]
            nlam = dpar[:, db + 3:db + 4]
            zc = {"i": 0}
            grp = {"i": 0}

            op_pending = []
            op_evac = {0: None, 1: None}
            op_rr = {"i": 0}

            def op_flush(bk):
                if op_evac[bk] is not None:
                    f = op_evac[bk]
                    op_evac[bk] = None
                    f()

            def make_op_units(hd):
                us = []
                for b_ in range(NB):
                    for m_ in range(NC):
                        def fo(b_=b_, m_=m_):
                            op_rr["i"] += 1
                            bk = op_rr["i"] % 2
                            op_flush(bk)
                            out_proj_unit(l, [2 * hd, 2 * hd + 1], [0, 1], m_, b_, bk,
                                          defer_fn=lambda bank, ev: op_evac.__setitem__(bank, ev))
                        us.append(fo)
                return us

            def run_ops(k, flush=True):
                if flush:
                    for bk in (0, 1):
                        op_flush(bk)
                for _ in range(min(k, 2)):
                    if op_pending:
                        op_pending.pop(0)()

            epi_carry = []
            for hd in range(H):
                vp = hd % 2
                uA = unit_pos[(l, 2 * hd)]
                uB = uA + 1
                sA, sB = uA % NSLOT, uB % NSLOT
                def make_v_units():
                    us = []
                    for t0_ in range(0, NT, 2):
                        def fv(t0_=t0_):
                            op_rr["i"] += 1
                            bk = op_rr["i"] % 2
                            op_flush(bk)
                            proj_v(sB, 2, 256, vp, t0_, 2, bk,
                                   defer_fn=lambda bank, ev: op_evac.__setitem__(bank, ev), evac_eng="act")
                        us.append(fv)
                    return us

                vus = make_v_units()
                merged = []
                while op_pending or vus:
                    for _ in range(4):
                        if op_pending:
                            merged.append(op_pending.pop(0))
                    if vus:
                        merged.append(vus.pop(0))
                op_pending.extend(merged)
                qitems = [(qi, b) for qi in range(4) for b in range(NB)]
                MMR = [4, 5, 6, 7]
                BSQ = 2
                BRB = 3

                def q_proj(j, qi, b):
                    proj_fm(sA, qi, b, MMR[j % 4])

                def q_sq(j, qi, b):
                    kb.op("act", lambda e: e.activation(out=attn[:, j % 2, :], in_=banks[MMR[j % 4]][:, :], func=AF.Square),
                          reads=[bankB[MMR[j % 4]]], writes=[B_attn[j % 2]])

                def q_ss(j, qi, b):
                    kb.op("pe", lambda e: e.matmul(banks[BSQ][:, :], lhsT=ones, rhs=attn[:, j % 2, :], start=True, stop=True),
                          reads=[B_attn[j % 2], B_c], writes=[bankB[BSQ]])

                def q_ln(j, qi, b):
                    kb.op("act", lambda e: e.activation(out=etmp[:, j % 2, :], in_=banks[BSQ][:, :], func=AF.Ln,
                                                        scale=1.0 / HEAD_DIM, bias=EPS_AP),
                          reads=[bankB[BSQ], B_dpar], writes=[B_etmp[j % 2]])
                    kb.op("act", lambda e: e.activation(out=etmp[:, j % 2, :], in_=etmp[:, j % 2, :], func=AF.Exp, scale=-0.5),
                          reads=[B_etmp[j % 2]], writes=[B_etmp[j % 2]])

                def q_qg(j, qi, b):
                    gcol = gq_s if qi < 2 else gk
                    kb.op("dve", lambda e: e.scalar_tensor_tensor(out=spr[:, j % NSP, :], in0=banks[MMR[j % 4]][:, :], scalar=gcol,
                                                                  in1=etmp[:, j % 2, :], op0=ALU.mult, op1=ALU.mult),
                          reads=[bankB[MMR[j % 4]], B_etmp[j % 2], B_dpar, B_par], writes=[B_sp[j % NSP]])

                def q_rot(j, qi, b):
                    kb.op("pe", lambda e: e.matmul(banks[BRB][:, :], lhsT=pswap, rhs=spr[:, j % NSP, :], start=True, stop=True),
                          reads=[B_sp[j % NSP], B_c], writes=[bankB[BRB]])

                def q_t(j, qi, b):
                    cols = slice(b * 512, (b + 1) * 512)
                    kb.op("pool", lambda e: e.tensor_tensor(out=wtmp[:, j % 2, :], in0=spr[:, j % NSP, :], in1=rope3[:, 0, cols],
                                                            op=ALU.mult),
                          reads=[B_sp[j % NSP]] + SCR_ALL, writes=[B_wt[j % 2]])
                    kb.op("dve", lambda e: e.tensor_tensor(out=o_sb[:, j % 2, :], in0=banks[BRB][:, :], in1=rope3[:, 1, cols],
                                                           op=ALU.mult),
                          reads=[bankB[BRB]] + SCR_ALL, writes=[B_o[j % 2]])

                def q_out(j, qi, b):
                    cols = slice(b * 512, (b + 1) * 512)
                    kb.op("pool", lambda e: e.tensor_tensor(out=qk[qi][:, cols], in0=o_sb[:, j % 2, :], in1=wtmp[:, j % 2, :],
                                                            op=ALU.add),
                          reads=[B_o[j % 2], B_wt[j % 2]], writes=[B_qk[qi][b]])

                qst = [(5, q_out), (4, q_rot), (4, q_t), (3, q_qg), (2, q_ss), (2, q_ln), (1, q_sq), (0, q_proj)]
                nq = len(qitems) + 5
                for s_ in range(nq):
                    if s_ >= 1 and epi_carry:
                        epi_carry.pop(0)()
                    for bk in (0, 1):
                        op_flush(bk)
                    for off_, fn in qst:
                        j = s_ - off_
                        if 0 <= j < len(qitems):
                            fn(j, *qitems[j])
                    run_ops(2 if len(op_pending) > (nq - s_) else 1, flush=False)
                issue_w(uA + 2)
                while op_pending:
                    run_ops(2)
                for bk in (0, 1):
                    op_flush(bk)
                issue_wo(wo_pos[(l, 2 * hd + 1)])
                pre_gates = [(gc, b) for b in range(NB) for gc in range(2) if 4 * b + 4 < 8]
                gate_units = [(gc, b) for b in range(NB) for gc in range(2) if 4 * b + 4 >= 8]
                k = 0
                for gc_, gb_ in pre_gates:
                    bank = 4 + k % 4
                    k += 1
                    proj_fm(sB, gc_, gb_, bank)
                    gate_from_bank(bank, gc_, gb_, use_act=True)
                if not gate_units:
                    issue_w(uB + 2)
                epi = []

                def make_epilogue(b, ssb):
                    cols = slice(b * 512, (b + 1) * 512)

                    def e1():
                        for c in range(2):
                            kb.op("pool", lambda e: e.tensor_tensor(out=spr[:, c, :], in0=o_sb[:, c, :], in1=o_sb[:, c, :],
                                                                    op=ALU.mult),
                                  reads=[B_o[c]], writes=[B_sp[c]])

                    def e1b():
                        for c in range(2):
                            kb.op("pe", lambda e: e.matmul(banks[ssb][:, :], lhsT=ones, rhs=spr[:, c, :], start=(c == 0), stop=(c == 1)),
                                  reads=[B_sp[c], B_c], writes=[bankB[ssb]], pe_acc=(c > 0))

                    def e2():
                        kb.op("act", lambda e: e.activation(out=etmp[:, 0, :], in_=banks[ssb][:, :], func=AF.Ln,
                                                            scale=1.0 / (2 * HEAD_DIM), bias=EPS_AP),
                              reads=[bankB[ssb], B_dpar], writes=[B_etmp[0]])
                        kb.op("act", lambda e: e.activation(out=etmp[:, 0, :], in_=etmp[:, 0, :], func=AF.Exp, scale=-0.5),
                              reads=[B_etmp[0]], writes=[B_etmp[0]])

                    def e3():
                        for c in range(2):
                            kb.op("dve", lambda e: e.scalar_tensor_tensor(out=wtmp[:, c, :], in0=o_sb[:, c, :],
                                                                          scalar=dpar[:, db + 1 + c:db + 2 + c], in1=etmp[:, 0, :],
                                                                          op0=ALU.mult, op1=ALU.mult),
                                  reads=[B_o[c], B_dpar, B_etmp[0]], writes=[B_wt[c]])
                            kb.op("pool", lambda e: e.tensor_tensor(out=gate[:, c, cols], in0=wtmp[:, c, :], in1=gate[:, c, cols],
                                                                    op=ALU.mult),
                                  reads=[B_wt[c], B_gate[c][b]], writes=[B_gate[c][b]])
                    return [e1, e1b, e2, e3]

                tiles = []
                for b in range(NB):
                    for m in range(2):
                        n = 4 * b + 4
                        for i in range(n):
                            tiles.append((b, m, i, n))
                BZ3 = [0, 1, 2]
                OSETS = [[3, 4], [5, 6]]
                BSS1 = 7

                def gidx(b, m):
                    return 2 * b + m

                def qk_mm(t):
                    b, m, i, n = tiles[t]
                    off, diag = tile_geom(b, i)
                    zb = BZ3[t % 3]
                    qcols = b * 512
                    kb.op("pe", lambda e: e.matmul(banks[zb][:, off:512], lhsT=qk[2 + m][:, i * 128:(i + 1) * 128],
                                                   rhs=qk[m][:, qcols + off:qcols + 512], start=True, stop=(not diag)),
                          reads=[B_qk[2 + m][i // 4], B_qk[m][b]], writes=[bankB[zb]], inc=(not diag))
                    if diag:
                        kb.op("pe", lambda e: e.matmul(banks[zb][:, off:off + 128], lhsT=ident, rhs=mneg_i,
                                                       start=False, stop=True),
                              reads=[B_c], writes=[bankB[zb]], pe_acc=True)

                def act_p(t):
                    b, m, i, n = tiles[t]
                    off, diag = tile_geom(b, i)
                    zb = BZ3[t % 3]
                    kb.op("act", lambda e: e.activation(out=attn[:, t % NATT, off:512], in_=banks[zb][:, off:512], func=AF.Exp),
                          reads=[bankB[zb]], writes=[B_attn[t % NATT]])

                def av_mm(t):
                    b, m, i, n = tiles[t]
                    off, diag = tile_geom(b, i)
                    BOa = OSETS[gidx(b, m) % 2]
                    for c in range(2):
                        kb.op("pe", lambda e: e.matmul(banks[BOa[c]][:, off:512],
                                                       lhsT=vbuf[vp][:, i, c * 128:(c + 1) * 128],
                                                       rhs=attn[:, t % NATT, off:512], start=(i == 0), stop=(i == n - 1)),
                              reads=[B_v[vp][i], B_attn[t % NATT]], writes=[bankB[BOa[c]]], pe_acc=(i > 0), inc=(c == 1))
                    p_ = gidx(b, m) % 2
                    for hf, eng_ in ((0, "dve"), (1, "pool")):
                        lo, hi = (max(off, 0), 384) if hf == 0 else (max(off, 384), 512)
                        if lo >= hi:
                            continue
                        if i == 0:
                            kb.op(eng_, lambda e: e.tensor_copy(out=sacc[:, p_, lo:hi], in_=attn[:, t % NATT, lo:hi]),
                                  reads=[B_attn[t % NATT]], writes=[B_sacc[p_][hf]])
                        else:
                            kb.op(eng_, lambda e: e.tensor_tensor(out=sacc[:, p_, lo:hi], in0=sacc[:, p_, lo:hi],
                                                                  in1=attn[:, t % NATT, lo:hi], op=ALU.add),
                                  reads=[B_attn[t % NATT], B_sacc[p_][hf]], writes=[B_sacc[p_][hf]])

                def make_evac(b, m):
                    BOa = OSETS[gidx(b, m) % 2]

                    def ev():
                        p_ = gidx(b, m) % 2
                        kb.op("dve", lambda e: e.tensor_copy(out=spr[:, 2, :], in_=sacc[:, p_, :]),
                              reads=B_sacc[p_], writes=[B_sp[2]])
                        kb.op("pe", lambda e: e.matmul(banks[BSS1][:, :], lhsT=ones, rhs=spr[:, 2, :], start=True, stop=True),
                              reads=[B_sp[2], B_c], writes=[bankB[BSS1]])
                        kb.op("act", lambda e: e.activation(out=etmp[:, 1, :], in_=banks[BSS1][:, :], func=AF.Ln),
                              reads=[bankB[BSS1]], writes=[B_etmp[1]])
                        kb.op("act", lambda e: e.activation(out=etmp[:, 1, :], in_=etmp[:, 1, :], func=AF.Exp, scale=-1.0),
                              reads=[B_etmp[1]], writes=[B_etmp[1]])
                        for c in range(2):
                            if m == 0:
                                kb.op("dve", lambda e: e.tensor_tensor(out=o_sb[:, c, :], in0=banks[BOa[c]][:, :],
                                                                       in1=etmp[:, 1, :], op=ALU.mult),
                                      reads=[bankB[BOa[c]], B_etmp[1]], writes=[B_o[c]])
                            else:
                                kb.op("dve", lambda e: e.tensor_tensor(out=wtmp[:, c, :], in0=banks[BOa[c]][:, :],
                                                                       in1=etmp[:, 1, :], op=ALU.mult),
                                      reads=[bankB[BOa[c]], B_etmp[1]], writes=[B_wt[c]])
                                kb.op("dve", lambda e: e.scalar_tensor_tensor(out=o_sb[:, c, :], in0=wtmp[:, c, :], scalar=nlam,
                                                                              in1=o_sb[:, c, :], op0=ALU.mult, op1=ALU.add),
                                      reads=[B_wt[c], B_o[c], B_dpar], writes=[B_o[c]])
                        if m == 1:
                            epi.extend(make_epilogue(b, BOa[0] if b < NB - 1 else 3))
                    return ev

                pend_evac = []
                gchain = []
                gtmp = cs_sb[:, :, :].rearrange("p a n -> p (a n)").bitcast(F32)

                def gate_chain(bank, gc, b_):
                    cols_ = slice(b_ * 512, (b_ + 1) * 512)

                    def g1():
                        kb.op("act", lambda e: e.activation(out=gtmp, in_=banks[bank][:, :], func=AF.Exp, scale=-1.0),
                              reads=[bankB[bank]], writes=B_cs)

                    def g2():
                        kb.op("act", lambda e: e.activation(out=gtmp, in_=gtmp, func=AF.Ln, bias=ONE_AP),
                              reads=B_cs + [B_dpar], writes=B_cs)

                    def g3():
                        kb.op("act", lambda e: e.activation(out=gtmp, in_=gtmp, func=AF.Exp, scale=-1.0),
                              reads=B_cs, writes=B_cs)
                        kb.op("dve", lambda e: e.tensor_tensor(out=gate[:, gc, cols_], in0=banks[bank][:, :], in1=gtmp,
                                                               op=ALU.mult),
                              reads=[bankB[bank]] + B_cs, writes=[B_gate[gc][b_]])
                    return [g1, g2, g3]

                epi_age = {"n": 0}
                NTL = len(tiles)
                qk_mm(0)
                if NTL > 1:
                    qk_mm(1)
                for t in range(NTL):
                    b, m, i, n = tiles[t]
                    if t + 2 < NTL:
                        qk_mm(t + 2)
                    act_p(t)
                    if pend_evac and (i >= 2 or i == n - 1):
                        while pend_evac:
                            pend_evac.pop(0)()
                    av_mm(t)
                    if epi:
                        epi_age["n"] += 1
                        if epi_age["n"] >= 3:
                            epi.pop(0)()
                    else:
                        epi_age["n"] = 0
                    if gchain:
                        gchain.pop(0)()
                        if i == n - 1:
                            while gchain:
                                gchain.pop(0)()
                    elif gate_units and n >= 8 and (i == 6 or (n >= 16 and i == 10)):
                        gc_, gb_ = gate_units.pop(0)
                        gbank = OSETS[(gidx(b, m) + 1) % 2][1]
                        proj_fm(sB, gc_, gb_, gbank)
                        gchain.extend(gate_chain(gbank, gc_, gb_))
                        if not gate_units:
                            issue_w(uB + 2)
                    if i == n - 1:
                        pend_evac.append(make_evac(b, m))
                while pend_evac:
                    pend_evac.pop(0)()
                while gchain:
                    gchain.pop(0)()
                assert not gate_units
                assert len(epi) == 4 and not epi_carry
                epi.pop(0)()
                if hd + 1 < H:
                    e1b_, e2_, e3_ = epi
                    epi_carry.extend([lambda e1b_=e1b_, e2_=e2_: (e1b_(), e2_()), e3_])
                else:
                    while epi:
                        epi.pop(0)()
                del epi[:]
                op_pending.extend(make_op_units(hd))
            while op_pending:
                run_ops(2)
            for bk in (0, 1):
                op_flush(bk)

        for l in range(depth):
            if l % 2 == 0:
                sb_layer(l)
            else:
                df_layer(l)

        mmb = 0
        for si, t0 in enumerate(range(0, NT, tiles_per_stage)):
            nt = min(tiles_per_stage, NT - t0)
            sbuf_ap, scap, sbufs = stage_bufs[si % len(stage_bufs)]
            xv = sbuf_ap[:, 0:nt * D].rearrange("p (a d) -> p a d", a=nt)
            for a_ in range(nt):
                tt = t0 + a_
                b_ = tt // 4
                for c0 in range(0, NC, 4):
                    cn = min(4, NC - c0)
                    bk = 4 + mmb % 4
                    mmb += 1
                    for k_ in range(cn):
                        c = c0 + k_
                        kb.op("pe", lambda e: e.transpose(out=banks[bk][:, k_ * 128:(k_ + 1) * 128],
                                                          in_=hT[:, c, tt * 128:(tt + 1) * 128], identity=ident_f),
                              reads=[B_hT[c][b_], B_c], writes=[bankB[bk]], pe_acc=(k_ > 0))
                    kb.op("dve", lambda e: e.tensor_copy(out=xv[:, a_, c0 * 128:(c0 + cn) * 128], in_=banks[bk][:, 0:cn * 128]),
                          reads=[bankB[bk]], writes=sbufs)
            for a_ in range(nt):
                kb.dma("sp", out_d[(t0 + a_) * 128:(t0 + a_ + 1) * 128, :], xv[:, a_, :],
                       f"d_out{si % len(stage_bufs)}", reads=sbufs)
        kb.wait_all("sp", [f"d_out{i}" for i in range(len(stage_bufs))])
        print(f"[build] instructions={kb.n_ins} waits={kb.n_wait}")
    return nc


def prep_weights(cfg, inputs):
    D, E, NC = cfg.D, cfg.E, cfg.NC
    out = {}
    P = np.zeros((128, cfg.depth * cfg.PL), np.float32)
    for l in range(cfg.depth):
        j = l // 2
        pb = l * cfg.PL
        if l % 2 == 0:
            W = np.asarray(inputs["sb_w_in"][j], np.float32)
            H = cfg.H_SB
            Wh = W.reshape(NC, 128, 4, H, 128)
            sel = Wh[:, :, [0, 1, 3, 2]]
            win = np.ascontiguousarray(sel.transpose(3, 1, 0, 2, 4)).reshape(H, 128, NC * 4 * 128)
            out[f"win{l}"] = win
            out[f"wout{l}"] = np.ascontiguousarray(np.asarray(inputs["sb_w_out"][j], np.float32))
            g = np.asarray(inputs["sb_norm"][j], np.float32)
            P[:, pb:pb + NC] = g.reshape(NC, 128).T
        else:
            W = np.asarray(inputs["df_w_in"][j], np.float32)
            H = cfg.H_DF
            Wr = W.reshape(NC, 128, 4, H, 2, 128)
            chunks = np.stack([Wr[:, :, 0, :, 0], Wr[:, :, 0, :, 1], Wr[:, :, 1, :, 0], Wr[:, :, 1, :, 1],
                               Wr[:, :, 3, :, 0], Wr[:, :, 3, :, 1], Wr[:, :, 2, :, 0], Wr[:, :, 2, :, 1]], axis=0)
            chunks = chunks.reshape(2, 4, NC, 128, H, 128)
            win = np.ascontiguousarray(chunks.transpose(4, 0, 3, 2, 1, 5)).reshape(H * 2, 128, NC * 4 * 128)
            out[f"win{l}"] = win
            out[f"wout{l}"] = np.ascontiguousarray(np.asarray(inputs["df_w_out"][j], np.float32))
            g = np.asarray(inputs["df_norm"][j], np.float32)
            P[:, pb:pb + NC] = g.reshape(NC, 128).T
            P[:, pb + NC] = np.asarray(inputs["df_q_norm"][j], np.float32)
            P[:, pb + NC + 1] = np.asarray(inputs["df_k_norm"][j], np.float32)
            P[:, pb + NC + 2] = np.asarray(inputs["df_lam_q1"][j], np.float32)
            P[:, pb + NC + 3] = np.asarray(inputs["df_lam_k1"][j], np.float32)
            P[:, pb + NC + 4] = np.asarray(inputs["df_lam_q2"][j], np.float32)
            P[:, pb + NC + 5] = np.asarray(inputs["df_lam_k2"][j], np.float32)
            gs = np.asarray(inputs["df_sub_norm"][j], np.float32)
            P[:, pb + NC + 6] = gs[:128]
            P[:, pb + NC + 7] = gs[128:]
    out["params"] = P
    out["consts"] = make_consts()
    out["rope"] = make_rope(cfg.S)
    return out


_PROG_CACHE = {}


def run(cfg, inputs, n_cores, trace=False):
    key = (cfg.S, cfg.D, cfg.depth)
    if key not in _PROG_CACHE:
        _PROG_CACHE[key] = build_program(cfg)
    nc = _PROG_CACHE[key]
    shared = prep_weights(cfg, inputs)
    x = np.asarray(inputs["x"], np.float32)
    in_maps = []
    for b in range(n_cores):
        m = dict(shared)
        m["x"] = np.ascontiguousarray(x[b])
        in_maps.append(m)
    res = run_bass_kernel_spmd(nc, in_maps, core_ids=list(range(n_cores)), trace=trace)
    out = np.stack([np.asarray(r["out"], np.float32) for r in res.results], axis=0)
    return out, res


def kernel(x, sb_norm, sb_w_in, sb_w_out, df_norm, df_w_in, df_w_out,
           df_q_norm, df_k_norm, df_lam_q1, df_lam_k1, df_lam_q2, df_lam_k2, df_sub_norm):
    inputs = dict(x=x, sb_norm=sb_norm, sb_w_in=sb_w_in, sb_w_out=sb_w_out, df_norm=df_norm,
                  df_w_in=df_w_in, df_w_out=df_w_out, df_q_norm=df_q_norm, df_k_norm=df_k_norm,
                  df_lam_q1=df_lam_q1, df_lam_k1=df_lam_k1, df_lam_q2=df_lam_q2, df_lam_k2=df_lam_k2,
                  df_sub_norm=df_sub_norm)
    cfg = Cfg(S=2048, D=1024, depth=4)
    out, _ = run(cfg, inputs, 8)
    return out.astype(np.float32)
```
